# Optimizing a Trainium2 kernel written in Bass

```python
import jax, jax.numpy as jnp
from jax import lax
import numpy as np

D_MODEL = 1024
BATCH = 32
SEQ = 256
DEPTH = 2
DEC_BATCH = 4
DEC_SEQ = 4096
PAST_LEN = 512

GRID_W = 64
N_DIRS = 2
EPS = 1e-6
HEAD_DIM = 64
ATTN_WIDTH = D_MODEL // 2
ATTN_HEADS = ATTN_WIDTH // HEAD_DIM
ATTN_KV_HEADS = ATTN_HEADS // 4
ATTN_GROUPS = ATTN_HEADS // ATTN_KV_HEADS
KV_WIDTH = ATTN_KV_HEADS * HEAD_DIM
WINDOW = 128
ATTN_BLOCK = 128
ROPE_BASE = 10000.0
NEG_INF = -1e30
GLA_WIDTH = D_MODEL // 4
GLA_DK = 64
GLA_DV = 64
GLA_HEADS = GLA_WIDTH // GLA_DV
GLA_QK_WIDTH = GLA_HEADS * GLA_DK
GLA_GATE_RANK = 16
GLA_GATE_NORM = 16.0
GLA_CHUNK = 64
LRU_WIDTH = D_MODEL - ATTN_WIDTH - GLA_WIDTH
LRU_BLOCK_W = 64
LRU_BLOCKS = LRU_WIDTH // LRU_BLOCK_W
LRU_CONV = 4
LRU_C = 8.0
MIX_WIDTH = ATTN_WIDTH + GLA_WIDTH + LRU_WIDTH
D_FF = 4 * D_MODEL
IN_SIZES = (ATTN_WIDTH, KV_WIDTH, KV_WIDTH,
            GLA_QK_WIDTH, GLA_QK_WIDTH, GLA_WIDTH, N_DIRS * GLA_GATE_RANK, GLA_WIDTH,
            LRU_WIDTH, LRU_WIDTH)
IN_COLS = (ATTN_WIDTH + 2 * KV_WIDTH + 2 * GLA_QK_WIDTH + 2 * GLA_WIDTH
           + N_DIRS * GLA_GATE_RANK + 2 * LRU_WIDTH)

kernel_name = 'hybrid_prefix_flow_step'


def rmsnorm(x, g):
    xf = x.astype(jnp.float32)
    y = xf * lax.rsqrt(jnp.mean(xf * xf, axis=-1, keepdims=True) + EPS)
    return (y * g.astype(jnp.float32)).astype(x.dtype)


def split_cols(a, sizes):
    out, start = [], 0
    for s in sizes:
        out.append(a[..., start:start + s])
        start += s
    return out


def ada_mod(cond, w_mod, b_mod):
    mod = jax.nn.silu(cond) @ w_mod + b_mod
    mod = mod.reshape(-1, 1, 6 * D_MODEL)
    return jnp.split(mod, 6, axis=-1)


def axial_rope(x):
    T = x.shape[1]
    rows = T // GRID_W
    row = jnp.repeat(jnp.arange(rows), GRID_W).astype(jnp.float32)
    col = jnp.tile(jnp.arange(GRID_W), rows).astype(jnp.float32)
    nf = HEAD_DIM // 4
    inv = ROPE_BASE ** (-jnp.arange(nf, dtype=jnp.float32) / nf)
    ang = jnp.stack([row[:, None] * inv, col[:, None] * inv], axis=1)
    cos = jnp.cos(ang)[None, :, None]
    sin = jnp.sin(ang)[None, :, None]
    xr = x.astype(jnp.float32).reshape(x.shape[:3] + (2, 2, nf))
    x1, x2 = xr[..., 0, :], xr[..., 1, :]
    out = jnp.stack([x1 * cos - x2 * sin, x2 * cos + x1 * sin], axis=-2)
    return out.reshape(x.shape).astype(x.dtype)


def softmax_with_sink(scores, sink):
    m = sink
    for s in scores:
        m = jnp.maximum(m, jnp.max(s, axis=-1, keepdims=True))
    es = [jnp.exp(s - m) for s in scores]
    denom = jnp.exp(sink - m)
    for e in es:
        denom = denom + jnp.sum(e, axis=-1, keepdims=True)
    inv = 1.0 / denom
    return [e * inv for e in es]


def attn_context(q, k, v, sink):
    B, L = q.shape[0], q.shape[1]
    qg = q.reshape(B, L, ATTN_KV_HEADS, ATTN_GROUPS, HEAD_DIM)
    s = jnp.einsum('blkgd,bmkd->bkglm', qg, k, preferred_element_type=jnp.float32) * (HEAD_DIM ** -0.5)
    s_sink = jnp.broadcast_to(sink.astype(jnp.float32).reshape(1, ATTN_KV_HEADS, ATTN_GROUPS, 1, 1), s.shape[:-1] + (1,))
    (p,) = softmax_with_sink([s], s_sink)
    o = jnp.einsum('bkglm,bmkd->blkgd', p.astype(v.dtype), v)
    return o.reshape(B, L, ATTN_WIDTH)


def attn_latent(q, k, v, ck, cv, sink):
    B, T = q.shape[0], q.shape[1]
    nb = T // ATTN_BLOCK
    qb = q.reshape(B, nb, ATTN_BLOCK, ATTN_KV_HEADS, ATTN_GROUPS, HEAD_DIM)

    def band(a):
        ap = jnp.pad(a, ((0, 0), (ATTN_BLOCK, ATTN_BLOCK), (0, 0), (0, 0)))
        ap = ap.reshape(B, nb + 2, ATTN_BLOCK, ATTN_KV_HEADS, HEAD_DIM)
        return jnp.concatenate([ap[:, :-2], ap[:, 1:-1], ap[:, 2:]], axis=2)

    kw, vw = band(k), band(v)
    blk = jnp.arange(nb)[:, None, None] * ATTN_BLOCK
    qpos = blk + jnp.arange(ATTN_BLOCK)[None, :, None]
    kpos = blk - ATTN_BLOCK + jnp.arange(3 * ATTN_BLOCK)[None, None, :]
    valid = (kpos >= 0) & (kpos < T) & (jnp.abs(qpos - kpos) <= WINDOW)
    scale = HEAD_DIM ** -0.5
    s_loc = jnp.einsum('bnikgd,bnjkd->bnkgij', qb, kw, preferred_element_type=jnp.float32) * scale
    s_loc = jnp.where(valid[None, :, None, None], s_loc, NEG_INF)
    s_ctx = jnp.einsum('bnikgd,bmkd->bnkgim', qb, ck, preferred_element_type=jnp.float32) * scale
    s_sink = jnp.broadcast_to(sink.astype(jnp.float32).reshape(1, 1, ATTN_KV_HEADS, ATTN_GROUPS, 1, 1), s_loc.shape[:-1] + (1,))
    p_loc, p_ctx = softmax_with_sink([s_loc, s_ctx], s_sink)
    o = (jnp.einsum('bnkgij,bnjkd->bnikgd', p_loc.astype(v.dtype), vw)
         + jnp.einsum('bnkgim,bmkd->bnikgd', p_ctx.astype(cv.dtype), cv))
    return o.reshape(B, T, ATTN_WIDTH)


def gla_chunked(q, k, v, g, s0):
    B, T, H, _ = q.shape
    n = T // GLA_CHUNK
    rs = lambda a: a.reshape(B, n, GLA_CHUNK, H, a.shape[-1])
    q, k, v, g = rs(q), rs(k), rs(v), rs(g)
    b = jnp.cumsum(g, axis=2)
    b_last = b[:, :, -1:]
    q_in = q * jnp.exp(b)
    k_in = k * jnp.exp(-b)
    k_end = k * jnp.exp(b_last - b)
    causal = jnp.tril(jnp.ones((GLA_CHUNK, GLA_CHUNK), dtype=bool))
    a_intra = jnp.where(causal, jnp.einsum('bnchd,bnshd->bnhcs', q_in, k_in), 0.0)
    o_intra = jnp.einsum('bnhcs,bnshv->bnchv', a_intra, v)
    delta = jnp.einsum('bnchd,bnchv->bnhdv', k_end, v)
    decay = jnp.exp(b_last[:, :, 0])

    def step(s, inp):
        dec, dlt = inp
        return dec[..., None] * s + dlt, s

    s_final, s_before = lax.scan(step, s0, (jnp.moveaxis(decay, 1, 0), jnp.moveaxis(delta, 1, 0)))
    s_before = jnp.moveaxis(s_before, 0, 1)
    o_inter = jnp.einsum('bnchd,bnhdv->bnchv', q_in, s_before)
    return (o_intra + o_inter).reshape(B, T, H, v.shape[-1]), s_final


def gla_mixer(q, k, v, z, r, gate_w, gate_b, norm_g, s0):
    B, T = q.shape[0], q.shape[1]
    f = jnp.float32
    q = q.astype(f).reshape(B, T, GLA_HEADS, GLA_DK) * (GLA_DK ** -0.5)
    k = k.astype(f).reshape(B, T, GLA_HEADS, GLA_DK)
    v = v.astype(f).reshape(B, T, GLA_HEADS, GLA_DV)
    z = z.astype(f).reshape(B, T, N_DIRS, GLA_GATE_RANK)
    g = jax.nn.log_sigmoid(jnp.einsum('btdr,drw->btdw', z, gate_w.astype(f)) + gate_b.astype(f)) / GLA_GATE_NORM
    g = g.reshape(B, T, N_DIRS, GLA_HEADS, GLA_DK)
    s0 = s0.astype(f)
    flip = lambda a: jnp.flip(a, axis=1)
    o_f, s_f = gla_chunked(q, k, v, g[:, :, 0], s0[:, 0])
    o_b, s_b = gla_chunked(flip(q), flip(k), flip(v), flip(g[:, :, 1]), s0[:, 1])
    o = o_f + flip(o_b)
    o = o * lax.rsqrt(jnp.mean(o * o, axis=-1, keepdims=True) + EPS)
    o = o.reshape(B, T, GLA_WIDTH) * norm_g.astype(f) * jax.nn.silu(r.astype(f))
    return o, jnp.stack([s_f, s_b], axis=1)


def centred_dwconv(x, w, b):
    T = x.shape[1]
    left = LRU_CONV // 2
    xp = jnp.pad(x, ((0, 0), (left, LRU_CONV - 1 - left), (0, 0)))
    y = b
    for tap in range(LRU_CONV):
        y = y + w[tap] * xp[:, tap:tap + T]
    return y


def linear_scan(a, b, h0):
    def combine(e1, e2):
        a1, b1 = e1
        a2, b2 = e2
        return a1 * a2, a2 * b1 + b2
    a_cum, b_cum = lax.associative_scan(combine, (a, b), axis=1)
    return a_cum * h0[:, None, :] + b_cum


def rglru_direction(x, wa, ba, wx, bx, lam, h0):
    B, T, W = x.shape
    xb = x.reshape(B, T, LRU_BLOCKS, LRU_BLOCK_W)
    r = jax.nn.sigmoid(jnp.einsum('btnc,ncd->btnd', xb, wa).reshape(B, T, W) + ba)
    i = jax.nn.sigmoid(jnp.einsum('btnc,ncd->btnd', xb, wx).reshape(B, T, W) + bx)
    log_a = -LRU_C * r * jax.nn.softplus(-lam)
    h = linear_scan(jnp.exp(log_a), jnp.sqrt(-jnp.expm1(2.0 * log_a)) * (i * x), h0)
    return h, h[:, -1]


def rglru_mixer(xb, gb, conv_w, conv_b, wa, ba, wx, bx, lam, h0):
    f = jnp.float32
    xc = centred_dwconv(xb.astype(f), conv_w.astype(f), conv_b.astype(f))
    wa, ba, wx, bx, lam, h0 = (a.astype(f) for a in (wa, ba, wx, bx, lam, h0))
    flip = lambda a: jnp.flip(a, axis=1)
    h_f, s_f = rglru_direction(xc, wa[0], ba[0], wx[0], bx[0], lam[0], h0[:, 0])
    h_b, s_b = rglru_direction(flip(xc), wa[1], ba[1], wx[1], bx[1], lam[1], h0[:, 1])
    y = (h_f + flip(h_b)) * jax.nn.gelu(gb.astype(f))
    return y, jnp.stack([s_f, s_b], axis=1)


def token_mixer(h, l, P, ctx):
    B, T, _ = h.shape
    q, k, v, gq, gk, gv, gz, gr, lx, lg = split_cols(h @ P['w_in'][l], IN_SIZES)
    q = q.reshape(B, T, ATTN_HEADS, HEAD_DIM)
    k = k.reshape(B, T, ATTN_KV_HEADS, HEAD_DIM)
    v = v.reshape(B, T, ATTN_KV_HEADS, HEAD_DIM)
    sink = P['attn_sink'][l]
    if ctx is None:
        att = attn_context(q, k, v, sink)
        s_gla0 = jnp.zeros((B, N_DIRS, GLA_HEADS, GLA_DK, GLA_DV), jnp.float32)
        s_lru0 = jnp.zeros((B, N_DIRS, LRU_WIDTH), jnp.float32)
    else:
        ck, cv, s_gla0, s_lru0 = ctx
        att = attn_latent(axial_rope(q), axial_rope(k), v, ck, cv, sink)
    gla, s_gla = gla_mixer(gq, gk, gv, gz, gr, P['gla_gate_w'][l], P['gla_gate_b'][l], P['gla_norm'][l], s_gla0)
    lru, s_lru = rglru_mixer(lx, lg, P['lru_conv_w'][l], P['lru_conv_b'][l], P['lru_wa'][l], P['lru_ba'][l],
                             P['lru_wx'][l], P['lru_bx'][l], P['lru_lambda'][l], s_lru0)
    mixed = jnp.concatenate([att, gla.astype(h.dtype), lru.astype(h.dtype)], axis=-1)
    out = mixed @ P['w_out'][l]
    if ctx is None:
        return out, (k, v, s_gla.astype(h.dtype), s_lru.astype(h.dtype))
    return out, None


def layer(x, cond, l, P, ctx):
    sh1, sc1, g1, sh2, sc2, g2 = ada_mod(cond, P['w_mod'][l], P['b_mod'][l])
    h = rmsnorm(x, P['norm1'][l]) * (1.0 + sc1) + sh1
    mix, new_ctx = token_mixer(h, l, P, ctx)
    x = x + g1 * mix
    h = rmsnorm(x, P['norm2'][l]) * (1.0 + sc2) + sh2
    u = jax.nn.relu(h @ P['w_mlp1'][l])
    x = x + g2 * ((u * u) @ P['w_mlp2'][l])
    return x, new_ctx


def setup_inputs(seed: int = 0) -> dict:
    key = jax.random.key(seed)
    ks = jax.random.split(key, 28)
    nrm = lambda k, shape, s: jax.random.normal(k, shape, jnp.float32) * s
    L, D = DEPTH, D_MODEL
    u = jax.random.uniform(ks[23], (L, N_DIRS, LRU_WIDTH), jnp.float32, 0.9, 0.999)
    a0 = u ** (1.0 / LRU_C)
    return {
        'x_prompt': nrm(ks[0], (BATCH, SEQ, D), 1.0),
        'x_sample': nrm(ks[1], (DEC_BATCH, DEC_SEQ, D), 1.0),
        'c': nrm(ks[2], (DEC_BATCH, D), 1.0),
        'cache_k': nrm(ks[3], (DEC_BATCH, L, PAST_LEN, ATTN_KV_HEADS, HEAD_DIM), 1.0),
        'cache_v': nrm(ks[4], (DEC_BATCH, L, PAST_LEN, ATTN_KV_HEADS, HEAD_DIM), 1.0),
        'state_gla': nrm(ks[5], (DEC_BATCH, L, N_DIRS, GLA_HEADS, GLA_DK, GLA_DV), 1.0),
        'state_lru': nrm(ks[6], (DEC_BATCH, L, N_DIRS, LRU_WIDTH), 0.5),
        'c_ctx': nrm(ks[7], (D,), 1.0),
        'w_mod': nrm(ks[8], (L, D, 6 * D), D ** -0.5),
        'b_mod': nrm(ks[9], (L, 6 * D), 0.02),
        'norm1': 1.0 + nrm(ks[10], (L, D), 0.02),
        'norm2': 1.0 + nrm(ks[11], (L, D), 0.02),
        'w_in': nrm(ks[12], (L, D, IN_COLS), D ** -0.5),
        'attn_sink': nrm(ks[13], (L, ATTN_HEADS), 0.5),
        'gla_gate_w': nrm(ks[14], (L, N_DIRS, GLA_GATE_RANK, GLA_QK_WIDTH), GLA_GATE_RANK ** -0.5),
        'gla_gate_b': nrm(ks[15], (L, N_DIRS, GLA_QK_WIDTH), 0.1),
        'gla_norm': 1.0 + nrm(ks[16], (L, GLA_WIDTH), 0.02),
        'lru_conv_w': nrm(ks[17], (L, LRU_CONV, LRU_WIDTH), LRU_CONV ** -0.5),
        'lru_conv_b': nrm(ks[18], (L, LRU_WIDTH), 0.02),
        'lru_wa': nrm(ks[19], (L, N_DIRS, LRU_BLOCKS, LRU_BLOCK_W, LRU_BLOCK_W), LRU_BLOCK_W ** -0.5),
        'lru_ba': nrm(ks[20], (L, N_DIRS, LRU_WIDTH), 0.1),
        'lru_wx': nrm(ks[21], (L, N_DIRS, LRU_BLOCKS, LRU_BLOCK_W, LRU_BLOCK_W), LRU_BLOCK_W ** -0.5),
        'lru_bx': nrm(ks[22], (L, N_DIRS, LRU_WIDTH), 0.1),
        'lru_lambda': jnp.log(a0) - jnp.log1p(-a0),
        'w_out': nrm(ks[24], (L, MIX_WIDTH, D), MIX_WIDTH ** -0.5),
        'w_mlp1': nrm(ks[25], (L, D, D_FF), D ** -0.5),
        'w_mlp2': nrm(ks[26], (L, D_FF, D), D_FF ** -0.5),
        'final_norm': 1.0 + nrm(ks[27], (D,), 0.02),
    }


def reference(x_prompt, x_sample, c, cache_k, cache_v, state_gla, state_lru, c_ctx,
              w_mod, b_mod, norm1, norm2, w_in, attn_sink, gla_gate_w, gla_gate_b, gla_norm,
              lru_conv_w, lru_conv_b, lru_wa, lru_ba, lru_wx, lru_bx, lru_lambda,
              w_out, w_mlp1, w_mlp2, final_norm):
    P = {'w_mod': w_mod, 'b_mod': b_mod, 'norm1': norm1, 'norm2': norm2, 'w_in': w_in,
         'attn_sink': attn_sink, 'gla_gate_w': gla_gate_w, 'gla_gate_b': gla_gate_b,
         'gla_norm': gla_norm, 'lru_conv_w': lru_conv_w, 'lru_conv_b': lru_conv_b,
         'lru_wa': lru_wa, 'lru_ba': lru_ba, 'lru_wx': lru_wx, 'lru_bx': lru_bx,
         'lru_lambda': lru_lambda, 'w_out': w_out, 'w_mlp1': w_mlp1, 'w_mlp2': w_mlp2}

    xp = x_prompt
    ks_, vs_, sg_, sl_ = [], [], [], []
    for l in range(DEPTH):
        xp, (k_l, v_l, sg_l, sl_l) = layer(xp, c_ctx, l, P, None)
        ks_.append(k_l)
        vs_.append(v_l)
        sg_.append(sg_l)
        sl_.append(sl_l)
    y_prompt = rmsnorm(xp, final_norm)
    new_cache_k = jnp.stack(ks_, axis=1)
    new_cache_v = jnp.stack(vs_, axis=1)
    new_state_gla = jnp.stack(sg_, axis=1)
    new_state_lru = jnp.stack(sl_, axis=1)

    xs = x_sample
    for l in range(DEPTH):
        xs, _ = layer(xs, c, l, P, (cache_k[:, l], cache_v[:, l], state_gla[:, l], state_lru[:, l]))
    y_sample = rmsnorm(xs, final_norm)

    return (y_prompt, y_sample, new_cache_k, new_cache_v, new_state_gla, new_state_lru)
```

```python
import bisect
import contextlib
import numpy as np
import concourse.bass as bass
import concourse.mybir as mybir
from concourse.bass_utils import run_bass_kernel_spmd

F32 = mybir.dt.float32
BF16 = mybir.dt.bfloat16
AF = mybir.ActivationFunctionType
ALU = mybir.AluOpType
AX = mybir.AxisListType

L = 2
D = 1024
T = 4096
NT = 32
P = 128
NSEQ = 16
WIN = 2464
QC, KF, LXO, LGO, GZO, KVO, GQKO, GVRO = 0, 512, 640, 896, 1152, 1184, 1440, 1952
EPS = 1e-6
NVEC = 22

COMPUTE = ('pe', 'act', 'dve', 'pool')
EPOCH = 8000


class Sched:
    def __init__(self):
        self.ins = []
        self.last_w = {}
        self.readers = {}
        self.pending = {}
        self.since_barrier_dma = []
        self.last_on = {}

    def barrier(self):
        ids = set(self.last_on.values()) | set(self.since_barrier_dma)
        self.since_barrier_dma = []
        for e in COMPUTE + ('sp',):
            self.pending[e] = set(ids) | self.pending.get(e, set())

    dead = False

    stopped = False

    def ckpt(self, n):
        kd = os.environ.get('KDEAD', '')
        if kd:
            a, b = [int(v) for v in kd.split(':')]
            if n == a and not self.stopped:
                self.dead = True
            if n == b and not self.stopped:
                self.dead = False
        if KSTOP[0] == n:
            self.dead = True
            self.stopped = True

    def add(self, eng, fn, reads=(), writes=(), dma_key=None):
        if self.dead:
            return -1
        pr = [r for r in reads if isinstance(r, str) and r.startswith('ps') and r[2:].isdigit()]
        if pr:
            writes = list(writes) + [r for r in pr if r not in writes]
        i = len(self.ins)
        deps = set()
        for r in reads:
            w = self.last_w.get(r)
            if w is not None:
                deps.add(w)
        for w_ in writes:
            w = self.last_w.get(w_)
            if w is not None:
                deps.add(w)
            for rd in self.readers.get(w_, {}).values():
                deps.add(rd)
        for r in reads:
            d = self.readers.setdefault(r, {})
            d[('dma', i) if dma_key is not None else eng] = i
        for w_ in writes:
            self.last_w[w_] = i
            self.readers[w_] = {}
        pend = self.pending.pop(eng, None)
        if pend:
            deps |= pend
        deps.discard(i)
        self.ins.append(dict(eng=eng, fn=fn, deps=deps, dma_key=dma_key, id=i))
        if dma_key is not None:
            self.since_barrier_dma.append(i)
        else:
            self.last_on[eng] = i
        return i

    def finalize(self):
        ins = self.ins
        for it in ins:
            best = {}
            keep = set()
            for d in it['deps']:
                p = ins[d]
                if p['dma_key'] is not None:
                    keep.add(d)
                else:
                    e = p['eng']
                    if e == 'pe' and it['eng'] == 'pe' and it['dma_key'] is None:
                        continue
                    if e not in best or best[e] < d:
                        best[e] = d
            keep.update(best.values())
            it['deps'] = sorted(keep)
        sig = set()
        for it in ins:
            sig.update(it['deps'])
        self.eng_count = {e: 0 for e in COMPUTE}
        self.dma_count = {}
        for i, it in enumerate(ins):
            if it['dma_key'] is not None:
                k = it['dma_key']
                self.dma_count[k] = self.dma_count.get(k, 0) + 1
                it['sig'] = ('dma', k, self.dma_count[k] * 16)
            elif i in sig:
                e = it['eng']
                c = self.eng_count[e]
                self.eng_count[e] = c + 1
                it['sig'] = ('eng', e, c // EPOCH, c % EPOCH + 1)
            else:
                it['sig'] = None
        self.n_eng_sems = {e: self.eng_count[e] // EPOCH + 1 for e in COMPUTE}
        self.dma_keys = sorted(self.dma_count.keys(), key=str)
        self.dma_ids = {}
        for i, it in enumerate(ins):
            if it['dma_key'] is not None:
                self.dma_ids.setdefault(it['dma_key'], []).append(i)

    def emit_engine(self, eng, handle, eng_sems, dma_sems, final_wait_all_dma=False):
        seen = {}
        ins = self.ins
        for it in ins:
            if it['eng'] != eng:
                continue
            need = {}
            for d in it['deps']:
                s = ins[d]['sig']
                if s[0] == 'dma':
                    nbefore = bisect.bisect_left(self.dma_ids[s[1]], it['id'])
                    sem, val, key = dma_sems[s[1]], max(s[2], nbefore * 16), ('dma', s[1])
                else:
                    sem, val, key = eng_sems[s[1]][s[2]], s[3], ('eng', s[1], s[2])
                if key not in need or need[key][1] < val:
                    need[key] = (sem, val)
            for key, (sem, val) in need.items():
                if seen.get(key, 0) < val:
                    handle.wait_ge(sem, val)
                    seen[key] = val
            r = it['fn'](handle)
            s = it['sig']
            if s is not None:
                if s[0] == 'dma':
                    r.then_inc(dma_sems[s[1]], 16)
                else:
                    r.then_inc(eng_sems[s[1]][s[2]], 1)
        if final_wait_all_dma:
            for k, n in self.dma_count.items():
                handle.wait_ge(dma_sems[k], n * 16)


import os
KSTOP = [int(os.environ.get('KSTOP', '0'))]


class Builder:
    def __init__(self, nc, es):
        self.nc = nc
        self.es = es
        self.S = Sched()
        self.nb = 0
        self.nmod = 8

    def sb(self, name, shape, dt):
        return self.es.enter_context(self.nc.sbuf_tensor('sb_' + name, list(shape), dt))

    def din(self, name, shape, dt=F32):
        return self.nc.dram_tensor(name, list(shape), dt, kind="ExternalInput").ap()

    def dout(self, name, shape, dt=F32):
        return self.nc.dram_tensor(name, list(shape), dt, kind="ExternalOutput").ap()

    def dscr(self, name, shape, dt=F32):
        return self.nc.dram_tensor(name, list(shape), dt, kind="Internal").ap()

    def bank(self):
        b = self.nb % self.nmod
        self.nb += 1
        return b

    def dma(self, q, out, in_, reads, writes, key):
        self.S.add(q, lambda e, o=out, i=in_: e.dma_start(out=o, in_=i), reads, writes, dma_key=key)

    def act(self, out, in_, func, reads, writes, bias=None, scale=None, accum=None):
        kw = {}
        if bias is not None:
            kw['bias'] = bias
        if scale is not None:
            kw['scale'] = scale
        if accum is not None:
            kw['accum_out'] = accum
        self.S.add('act', lambda e, o=out, i=in_, f=func, kw=kw: e.activation(out=o, in_=i, func=f, **kw),
                   reads, writes)

    def tt(self, eng, out, a, b, op, reads, writes):
        self.S.add(eng, lambda e, o=out, a=a, b=b, op=op: e.tensor_tensor(out=o, in0=a, in1=b, op=op),
                   reads, writes)

    def ts(self, eng, out, a, s1, s2, op0, op1, reads, writes):
        if s2 is None:
            self.S.add(eng, lambda e, o=out, a=a, s1=s1, op0=op0: e.tensor_single_scalar(out=o, in_=a, scalar=s1, op=op0),
                       reads, writes)
        else:
            self.S.add(eng, lambda e, o=out, a=a, s1=s1, s2=s2, op0=op0, op1=op1:
                       e.tensor_scalar(out=o, in0=a, scalar1=s1, scalar2=s2, op0=op0, op1=op1), reads, writes)

    def stt(self, eng, out, a, s, b, op0, op1, reads, writes):
        self.S.add(eng, lambda e, o=out, a=a, s=s, b=b, op0=op0, op1=op1:
                   e.scalar_tensor_tensor(out=o, in0=a, scalar=s, in1=b, op0=op0, op1=op1), reads, writes)

    def copy(self, eng, out, in_, reads, writes):
        if eng == 'act':
            self.S.add('act', lambda e, o=out, i=in_: e.copy(out=o, in_=i), reads, writes)
        else:
            self.S.add(eng, lambda e, o=out, i=in_: e.tensor_copy(out=o, in_=i), reads, writes)

    def memset(self, eng, ap, val, writes):
        self.S.add(eng, lambda e, a=ap, v=val: e.memset(a, v), (), writes)

    def recip(self, out, in_, reads, writes):
        self.S.add('dve', lambda e, o=out, i=in_: e.reciprocal(out=o, in_=i), reads, writes)

    def mm(self, specs, reads, writes):
        def fn(e, specs=specs):
            r = None
            for sp_ in specs:
                o, l, rh, st, sp = sp_[:5]
                if len(sp_) > 5 and sp_[5]:
                    r = e.matmul(o, lhsT=l, rhs=rh, start=st, stop=sp, skip_group_check=True)
                else:
                    r = e.matmul(o, lhsT=l, rhs=rh, start=st, stop=sp)
            return r
        self.S.add('pe', fn, reads, writes)

    def tr(self, specs, ident, reads, writes):
        def fn(e, specs=specs, ident=ident):
            r = None
            for (o, i) in specs:
                r = e.transpose(out=o, in_=i, identity=ident)
            return r
        self.S.add('pe', fn, reads, writes)

    def scan(self, out, d0, d1, init, reads, writes):
        self.S.add('dve', lambda e, o=out, a=d0, b=d1, i=init:
                   e.tensor_tensor_scan(out=o, data0=a, data1=b, initial=i, op0=ALU.mult, op1=ALU.add),
                   reads, writes)

    def reduce_sum(self, out, in_, reads, writes):
        self.S.add('dve', lambda e, o=out, i=in_: e.reduce_sum(out=o, in_=i, axis=AX.X), reads, writes)


def build_program():
    nc = bass.Bass("TRN2", target_bir_lowering=False)
    with contextlib.ExitStack() as es:
        B = Builder(nc, es)
        S = B.S
        x_d = B.din("x", [T, D])
        cond_d = B.din("cond8", [P, 8])
        ck_d = B.din("ck", [L, 512, 128])
        cv_d = B.din("cv", [L, 512, 128])
        sgla_d = B.din("sgla", [L, 2, P, 2, 64])
        lruh0_d = B.din("lruh0", [P, L * 4])
        flags_d = B.din("flags", [P, 4])
        ident_d = B.din("ident", [P, P])
        tri_d = B.din("tri", [4, P, P])
        glam_d = B.din("glam", [2, P, P])
        attm_d = B.din("attm", [4, P, P])
        perm_d = B.din("perm", [P, P])
        rope_d = B.din("rope", [NT, P, 2, P])
        wmod_d = B.din("w_mod", [L, D, 6 * D])
        bmod_d = B.din("b_mod", [L, 1, 6 * D])
        n1_d = B.din("norm1", [L, 1, D])
        n2_d = B.din("norm2", [L, 1, D])
        fn_d = B.din("final_norm", [1, D])
        win_d = B.din("w_in_r", [L, D, WIN])
        wout_d = B.din("w_out", [L, D, D])
        w1_d = B.din("w_mlp1", [L, D, 4 * D])
        w2_d = B.din("w_mlp2", [L, 4 * D, D])
        gaug_d = B.din("gaug", [L, 33, 512])
        gnorm_d = B.din("gla_norm", [L, 1, 256])
        sink_d = B.din("attn_sink", [L, 1, 8])
        lvec_d = B.din("lruvec", [L, P, NVEC])
        wa_d = B.din("lru_wa", [L, 2, 4, 64, 64])
        wx_d = B.din("lru_wx", [L, 2, 4, 64, 64])

        y_d = B.dout("y", [T, D])
        kout_d = B.dout("kout", [L, T, 128])
        vout_d = B.dout("vout", [L, T, 128])
        gst_d = B.dout("gst", [NSEQ, L, 2, 4, 64, 64])
        lst_d = B.dout("lst", [NSEQ, L, 2, 256])

        xs_s = B.dscr("xs_s", [T, D])
        mod_s = B.dscr("mod_s", [1, 6 * D])
        lxg_s = B.dscr("lxg_s", [4, P, T])
        qkT_s = B.dscr("qkT_s", [NT, P, 1024], BF16)
        ke_s = B.dscr("ke_s", [NT, P, 512], BF16)
        v16_s = B.dscr("v16_s", [NT, P, 256], BF16)
        dec_s = B.dscr("dec_s", [NT, P, 4])
        sgr_s = B.dscr("sgr_s", [NT, P, 256])
        o_s = B.dscr("o_s", [2, NT, P, 256])
        mixT_s = B.dscr("mixT_s", [NT, P, 1024], BF16)

        SW = 2048
        ident16 = B.sb("ident16", [P, P], BF16)
        identf = B.sb("identf", [P, P], F32)
        tri = B.sb("tri", [P, 4, P], F32)
        glam16 = B.sb("glam16", [P, 2, P], BF16)
        attm16 = B.sb("attm16", [P, 4, P], BF16)
        perm = B.sb("perm", [P, P], F32)
        flags = B.sb("flags", [P, 4], F32)
        ones_col = B.sb("ones_col", [P, 1], F32)
        scond = B.sb("scond", [P, 8], F32)
        scond16 = B.sb("scond16", [P, 8], BF16)
        lruh0 = B.sb("lruh0", [P, L * 4], F32)
        M = [B.sb(f"mod{i}", [P, D], F32) for i in range(4)]
        wout16 = B.sb("wout16", [P, 8, D], BF16)
        wbig = B.sb("wbig", [P, 65536], BF16)
        stage = [B.sb("stage0", [P, SW], F32), B.sb("stage1", [P, 1024], F32)]
        esink = B.sb("esink", [P, 8], F32)
        lvec = B.sb("lvec", [P, NVEC], F32)
        lder = B.sb("lder", [P, 16], F32)
        xt = [B.sb(f"xt{i}", [P, D], F32) for i in range(2)]
        tmp32 = B.sb("tmp32", [P, D], F32)
        h16 = B.sb("h16", [P, D], BF16)
        hTs = [B.sb(f"hT{i}", [P, 8, P], BF16) for i in range(2)]
        hT = hTs[0]
        st4 = B.sb("st4", [P, 8], F32)
        u16 = [B.sb(f"u16_{i}", [P, 512], BF16) for i in range(2)]
        wbd32 = stage[1][:, 0:1024].rearrange("p (a b) -> p a b", a=8)

        off = [0]
        un = B.sb("un", [P, 4096], BF16)
        base = [wbig, 65536]

        def carve(nelem_bf16, dt, shape):
            a = base[0][:, off[0]:off[0] + nelem_bf16]
            off[0] += nelem_bf16
            assert off[0] <= base[1], off[0]
            if dt == F32:
                a = a.bitcast(F32)
            if len(shape) == 3:
                a = a.rearrange("p (a b) -> p a b", a=shape[1])
            elif len(shape) == 4:
                a = a.rearrange("p (a b c) -> p a b c", a=shape[1], b=shape[2])
            assert list(a.shape) == list(shape), (a.shape, shape)
            return a

        base[:] = [un, 4096]
        gaug_full = carve(1024, F32, [P, 512])
        gaug = gaug_full[0:33, :]
        gnb = carve(512, F32, [P, 256])
        wbd16 = carve(1024, BF16, [P, 8, P])
        ckT = carve(512, BF16, [P, 512])
        cva = carve(520, BF16, [P, 4, 2, 65])
        off[0] = 0
        uT = carve(1024, BF16, [P, 2, 4, P])
        rl32 = carve(1024, F32, [P, 512])
        mixTt = [carve(1024, BF16, [P, 8, P]) for _ in range(2)]
        off[0] = 0
        base[:] = [wbig, 65536]
        win16 = wbig[:, 0:8 * WIN].rearrange("p (k n) -> p k n", k=8)
        w1_16 = wbig[:, 0:32768].rearrange("p (k n) -> p k n", k=8)
        w2_16 = wbig[:, 32768:65536].rearrange("p (k n) -> p k n", k=32)
        wmod16 = wbig[:, 0:8 * 6144].rearrange("p (k n) -> p k n", k=8)
        off[0] = 8 * WIN
        KT = carve(T, BF16, [P, T])
        VAflat = wbig[:, off[0]:off[0] + NT * 130]
        off[0] += NT * 130
        VA = VAflat.rearrange("p (t g c) -> p t g c", t=NT, g=2)
        QT = carve(3 * 4 * P, BF16, [P, 3, 4, P])
        qk32 = carve(2 * 5 * P, F32, [P, 5, P])
        rt1 = carve(2 * 5 * P, F32, [P, 5, P])
        rt2 = carve(2 * 5 * P, F32, [P, 5, P])
        ropet = [carve(2 * 2 * P, F32, [P, 2, P]) for _ in range(2)]
        kv32 = [carve(2 * 256, F32, [P, 256]) for _ in range(2)]
        lxg = [carve(2 * 4 * P, F32, [P, 4, P]) for _ in range(2)]
        zT33 = carve(2 * P, F32, [P, P])
        sp32 = carve(2 * 512, F32, [P, 512])
        EqEe = [carve(2 * 512, F32, [P, 512]) for _ in range(2)]
        Ek = [carve(2 * 256, F32, [P, 256]) for _ in range(2)]
        qin16 = carve(512, BF16, [P, 512])
        kin16 = carve(512, BF16, [P, 512])
        ke16 = [carve(512, BF16, [P, 512]) for _ in range(2)]
        v16 = [carve(256, BF16, [P, 256]) for _ in range(2)]
        sgr32 = [carve(2 * 256, F32, [P, 256]) for _ in range(2)]
        qkT16 = [carve(1024, BF16, [P, 8, P]) for _ in range(2)]
        dec32 = [carve(2 * 4, F32, [P, 4]) for _ in range(2)]
        PT = [carve(512, BF16, [P, 512]) for _ in range(3)]
        den = carve(2 * 8, F32, [P, 8])
        rec = carve(2 * 8, F32, [P, 8])
        o16 = carve(512, BF16, [P, 512])
        oT16 = [carve(512, BF16, [P, 4, P]) for _ in range(2)]
        gqk32 = carve(2 * 512, F32, [P, 512])
        offA = off[0]
        off[0] = 0
        gq_ld = [[carve(512, BF16, [P, 4, P]) for _ in range(2)] for _ in range(2)]
        gke_ld = [[carve(256, BF16, [P, 256]) for _ in range(2)] for _ in range(2)]
        gv_ld = [[carve(256, BF16, [P, 256]) for _ in range(2)] for _ in range(2)]
        gdec_ld = [[carve(2 * 2, F32, [P, 2]) for _ in range(2)] for _ in range(2)]
        S32 = [carve(2 * 128, F32, [P, 2, 64]) for _ in range(2)]
        S16 = [carve(128, BF16, [P, 2, 64]) for _ in range(2)]
        AT16 = [carve(512, BF16, [P, 512]) for _ in range(2)]
        o32 = [[carve(2 * 256, F32, [P, 256]) for _ in range(2)] for _ in range(2)]
        FS = [dict(fo=[carve(2 * 256, F32, [P, 256]) for _ in range(3)], fsq=carve(2 * 256, F32, [P, 256]),
                   fy=carve(2 * 256, F32, [P, 256]), fy16=carve(256, BF16, [P, 256]), fyT=carve(256, BF16, [P, 2, P]),
                   st=carve(2 * 4, F32, [P, 4])) for _ in range(2)]
        CH = 512
        NCH = T // CH
        W2PRE = 24
        LT = []
        for _ct in range(2):
            LT.append(dict(
                hf=carve(2 * T, F32, [P, T]),
                lxc=[carve(2 * (CH + 4), F32, [P, CH + 4]) for _ in range(2)],
                xc=carve(2 * CH, F32, [P, CH]), xc16=carve(CH, BF16, [P, CH]),
                rg=carve(2 * CH, F32, [P, CH]), ig=carve(2 * CH, F32, [P, CH]),
                av=carve(2 * CH, F32, [P, CH]), bt=carve(2 * CH, F32, [P, CH]),
                hb=[carve(2 * CH, F32, [P, CH]) for _ in range(2)],
                lgc=[carve(2 * CH, F32, [P, CH]) for _ in range(2)],
                y16=[carve(CH, BF16, [P, CH]) for _ in range(2)],
                stc=carve(2 * 32, F32, [P, 2, 16]),
                sto=[carve(2 * P, F32, [P, P]) for _ in range(2)]))
        assert off[0] <= 32768 + W2PRE * 1024, off[0]

        ps = [es.enter_context(nc.psum_tensor(f"ps{i}", [P, 512], F32)) for i in range(8)]
        ps16 = [p_[:].bitcast(BF16) for p_ in ps]

        def PSK(b):
            return f"ps{b}"

        def h4(ap, n=4):
            return ap.rearrange("p (h c) -> p h c", h=n)

        CK = 'ld_c'
        B.dma('sp', identf[:], ident_d[:, :], [], ['consts'], CK)
        B.dma('sp', tri[:], tri_d.rearrange("k p n -> p k n"), [], ['consts'], CK)
        B.dma('sp', perm[:], perm_d[:, :], [], ['consts'], CK)
        B.dma('sp', flags[:], flags_d[:, :], [], ['consts'], CK)
        B.dma('sp', lruh0[:], lruh0_d[:, :], [], ['consts'], CK)
        B.dma('sp', scond[:], cond_d[:, :], [], ['consts'], CK)
        B.dma('sp', stage[0][:, 0:256].rearrange("p (k n) -> p k n", k=2), glam_d.rearrange("k p n -> p k n"), [], ['consts'], CK)
        B.dma('sp', stage[0][:, 256:768].rearrange("p (k n) -> p k n", k=4), attm_d.rearrange("k p n -> p k n"), [], ['consts'], CK)
        B.copy('dve', ident16[:], identf[:], ['consts'], ['ident16'])
        B.copy('dve', glam16[:].rearrange("p a b -> p (a b)"), stage[0][:, 0:256], ['consts'], ['glam16', 'stage0'])
        B.copy('dve', attm16[:].rearrange("p a b -> p (a b)"), stage[0][:, 256:768], ['consts'], ['attm16', 'stage0'])
        B.act(scond[:], scond[:], AF.Silu, ['consts'], ['scond'])
        B.copy('dve', scond16[:], scond[:], ['scond'], ['scond16'])
        B.memset('dve', ones_col[:], 1.0, ['ones_col'])
        ctxb = flags[:, 0:1]
        sf = flags[:, 2:3]
        npf = flags[:, 3:4]
        FLG = 'consts'

        def load_cast_weight(dst_fn, src_fn, nk, width, rkey):
            n = [0]
            for kc in range(nk):
                for c0 in range(0, width, SW):
                    c1 = min(width, c0 + SW)
                    s = n[0] % 2
                    n[0] += 1
                    B.dma('sp', stage[s][:, 0:c1 - c0], src_fn(kc)[:, c0:c1], [], [f'stage{s}'], f'ld_st{s}')
                    B.copy('pool' if s == 0 else 'dve', dst_fn(kc)[:, c0:c1], stage[s][:, 0:c1 - c0], [f'stage{s}'], [rkey])

        def load_mod(dst, chunk, key, dkey):
            B.dma('sp', dst[:], mod_s[0:1, chunk * D:(chunk + 1) * D].partition_broadcast(P), ['mod_s'], [key], dkey)

        for l in range(L):
            x_src = x_d if l == 0 else xs_s
            B.nmod = 8
            S.barrier()
            for j in range(4):
                B.dma('pool', wmod16[:, 2 * j:2 * j + 2, :],
                      wmod_d[l, 2 * j * P:(2 * j + 2) * P, :].rearrange("(k p) n -> p k n", p=P), [], [f'wmod{j}'], f'ld_wm{j}')
            B.dma('pool', wout16[:], wout_d[l].rearrange("(k p) n -> p k n", p=P), [], ['wout16'], 'ld_wout')
            for part in range(3):
                row = stage[0][0:1, 0:2048]
                rk_ = 'stage0'
                B.dma('sp', row, bmod_d[l][0:1, part * 2048:(part + 1) * 2048], [], [rk_], 'ld_st0')
                banks = [B.bank() for _ in range(4)]
                for kc in range(8):
                    B.mm([(ps[banks[j]][0:1, :], scond16[:, kc:kc + 1], wmod16[:, kc, (part * 4 + j) * 512:(part * 4 + j + 1) * 512],
                           kc == 0, kc == 7) for j in range(4)], ['scond16', f'wmod{kc // 2}'], [PSK(b) for b in banks])
                for j in range(4):
                    B.tt('dve', row[:, j * 512:(j + 1) * 512], ps[banks[j]][0:1, :], row[:, j * 512:(j + 1) * 512], ALU.add,
                         [PSK(banks[j]), rk_], [rk_])
                B.dma('pool', mod_s[0:1, part * 2048:(part + 1) * 2048], row, [rk_], ['mod_s'], 'st_mod')
            S.ckpt(1)
            S.barrier()
            for j in range(2):
                B.dma('pool', win16[:, 4 * j:4 * j + 4, :],
                      win_d[l, 4 * j * P:(4 * j + 4) * P, :].rearrange("(k p) n -> p k n", p=P), [], ['win16'], f'ld_win{j}')
            load_mod(M[0], 1, 'M0', 'ld_m0')
            B.dma('sp', tmp32[:], n1_d[l].partition_broadcast(P), [], ['tmp32'], 'ld_tmp')
            B.stt('dve', M[0][:], M[0][:], 1.0, tmp32[:], ALU.add, ALU.mult, ['M0', 'tmp32'], ['M0'])
            load_mod(M[1], 0, 'M1', 'ld_m1')
            B.dma('sp', gaug, gaug_d[l], [], ['gaug'], 'ld_s0')
            B.dma('sp', gnb[:], gnorm_d[l].partition_broadcast(P), [], ['gnb'], 'ld_s1')
            B.dma('sp', esink[:], sink_d[l].partition_broadcast(P), [], ['esink'], 'ld_s2')
            B.act(esink[:], esink[:], AF.Exp, ['esink'], ['esink'])
            B.dma('sp', lvec[:], lvec_d[l], [], ['lvec'], 'ld_s3')
            for ct in range(2):
                for dr in range(2):
                    lamc = lvec[:, ct * 11 + 5 + dr * 3 + 2: ct * 11 + 5 + dr * 3 + 3]
                    dc = lder[:, dr * 2 + ct: dr * 2 + ct + 1]
                    B.act(dc, lamc, AF.Exp, ['lvec'], ['lder'], scale=-1.0)
                    B.act(dc, dc, AF.Ln, ['lder'], ['lder'], bias=1.0)
                    B.ts('dve', dc, dc, -8.0, None, ALU.mult, None, ['lder'], ['lder'])
                B.ts('dve', lder[:, 4 + ct * 4: 8 + ct * 4], lvec[:, ct * 11: ct * 11 + 4], npf, None, ALU.mult, None,
                     ['lvec', FLG], ['lder'])
            B.memset('dve', stage[1][:, 0:1024], 0.0, ['stage1'])
            for g_, wsrc in enumerate((wa_d, wx_d)):
                for dr in range(2):
                    for n in range(4):
                        ct, hb_ = n // 2, n % 2
                        idx = g_ * 4 + dr * 2 + ct
                        B.dma('sp', wbd32[hb_ * 64:(hb_ + 1) * 64, idx, hb_ * 64:(hb_ + 1) * 64], wsrc[l, dr, n],
                              ['stage1'], ['wbd32'], 'ld_wbd')
            B.copy('dve', wbd16[:].rearrange("p a b -> p (a b)"), stage[1][:, 0:1024], ['wbd32', 'stage1'], ['wbd16', 'stage1'])
            B.memset('dve', cva[:].rearrange("p a b c -> p (a b c)"), 1.0, ['cva'])
            for c in range(4):
                B.dma('sp', stage[0][:, 0:128], ck_d[l, c * P:(c + 1) * P, :], [], ['stage0'], 'ld_st0')
                B.copy('dve', h16[:, 0:128], stage[0][:, 0:128], ['stage0'], ['h16'])
                b = B.bank()
                B.tr([(ps16[b][:, 0:128], h16[:, 0:128])], ident16[:], ['h16', 'ident16'], [PSK(b)])
                B.copy('act', ckT[:, c * P:(c + 1) * P], ps16[b][:, 0:128], [PSK(b)], ['ckT'])
                B.dma('sp', stage[1][:, 0:128], cv_d[l, c * P:(c + 1) * P, :], [], ['stage1'], 'ld_st1')
                B.copy('dve', cva[:, c, :, 0:64], stage[1][:, 0:128].rearrange("p (g d) -> p g d", g=2), ['stage1'], ['cva'])
            B.memset('pool', VAflat, 1.0, ['VAall'])
            B.memset('pool', zT33[32:33, :], 1.0, ['zT33'])

            S.ckpt(2)
            def norm_mod(xs_, xk, m_a, m_b, ka, kb, b, hT_=None, hk='hT'):
                hT_ = hT if hT_ is None else hT_
                B.act(h16[:], xs_[:], AF.Square, [xk], ['h16', 'st0'], accum=st4[:, 0:1])
                B.act(st4[:, 1:2], st4[:, 0:1], AF.Ln, ['st0'], ['st1'], bias=EPS, scale=1.0 / D)
                B.act(st4[:, 2:3], st4[:, 1:2], AF.Exp, ['st1'], ['st2'], scale=-0.5)
                B.stt('dve', tmp32[:], xs_[:], st4[:, 2:3], m_a[:], ALU.mult, ALU.mult, [xk, 'st2', ka], ['tmp32'])
                B.tt('dve', h16[:], tmp32[:], m_b[:], ALU.add, ['tmp32', kb], ['h16'])
                B.tr([(ps16[b][:, kc * P:(kc + 1) * P], h16[:, kc * P:(kc + 1) * P]) for kc in range(8)], ident16[:],
                     ['h16', 'ident16'], [PSK(b)])
                B.copy('act', hT_[:].rearrange("p a b -> p (a b)"), ps16[b][:, :], [PSK(b)], [hk])

            def norm_mod_g(xs_, xk, m_a, m_b, ka, kb, b, hT_, hk):
                B.act(h16[:], xs_[:], AF.Square, [xk], ['h16', 'st0'], accum=st4[:, 0:1])
                yield
                B.act(st4[:, 1:2], st4[:, 0:1], AF.Ln, ['st0'], ['st1'], bias=EPS, scale=1.0 / D)
                yield
                B.act(st4[:, 2:3], st4[:, 1:2], AF.Exp, ['st1'], ['st2'], scale=-0.5)
                yield
                B.stt('dve', tmp32[:], xs_[:], st4[:, 2:3], m_a[:], ALU.mult, ALU.mult, [xk, 'st2', ka], ['tmp32'])
                yield
                B.tt('dve', h16[:], tmp32[:], m_b[:], ALU.add, ['tmp32', kb], ['h16'])
                yield
                B.tr([(ps16[b][:, kc * P:(kc + 1) * P], h16[:, kc * P:(kc + 1) * P]) for kc in range(8)], ident16[:],
                     ['h16', 'ident16'], [PSK(b)])
                yield
                B.copy('act', hT_[:].rearrange("p a b -> p (a b)"), ps16[b][:, :], [PSK(b)], [hk])
                yield

            def attention(i):
                slot = i % 3
                kbs = [('c', c) for c in range(4)]
                if i > 0:
                    kbs.append(('p', i - 1))
                kbs.append(('o', i))
                if i < NT - 1:
                    kbs.append(('n', i + 1))
                ob = [6, 7]
                seq = [(g, n_, kind, j) for g in range(2) for n_, (kind, j) in enumerate(kbs)]

                def s_mm(idx):
                    g, n_, kind, j = seq[idx]
                    b = 4 + idx % 2
                    if kind == 'c':
                        ksrc, rk = ckT[64 * g:64 * g + 64, j * P:(j + 1) * P], 'ckT'
                    else:
                        ksrc, rk = KT[64 * g:64 * g + 64, j * P:(j + 1) * P], f'KT{j}'
                    B.mm([(ps[b][:, :], ksrc, QT[64 * g:64 * g + 64, slot, :, :].rearrange("p a b -> p (a b)"), True, True)],
                         [rk, f'QT{slot}'], [PSK(b)])

                s_mm(0)
                for idx, (g, n_, kind, j) in enumerate(seq):
                    if idx + 1 < len(seq):
                        s_mm(idx + 1)
                    b = 4 + idx % 2
                    pt = PT[idx % 3]
                    ptk = f'PT{idx % 3}'
                    if kind == 'c':
                        vsrc, rv = cva[:, j, g, :], 'cva'
                        B.act(pt[:], ps[b][:, :], AF.Exp, [PSK(b), FLG], [ptk], bias=ctxb, scale=0.125)
                    else:
                        vsrc, rv = VA[:, j, g, :], f'VA{j}'
                        B.act(pt[:], ps[b][:, :], AF.Exp, [PSK(b)], [ptk], scale=0.125)
                    if kind in ('p', 'n'):
                        mi = (0 if kind == 'p' else 2) + (i % 2)
                        B.tt('dve', h4(pt[:]), h4(pt[:]), attm16[:, mi, :].unsqueeze(1).to_broadcast([P, 4, P]), ALU.mult,
                             [ptk, 'attm16'], [ptk])
                    B.mm([(ps[ob[g]][:, a * 65:(a + 1) * 65], pt[:, a * P:(a + 1) * P], vsrc,
                           n_ == 0 and a == 0, n_ == len(kbs) - 1 and a == 3, True) for a in range(4)],
                         [ptk, rv], [PSK(ob[g])])
                    yield
                for g in range(2):
                    o3 = h4(ps[ob[g]][:, 0:260])
                    B.tt('dve', den[:, 4 * g:4 * g + 4], o3[:, :, 64], esink[:, 4 * g:4 * g + 4], ALU.add,
                         [PSK(ob[g]), 'esink'], ['den'])
                B.recip(rec[:], den[:], ['den'], ['rec'])
                for g in range(2):
                    o3 = h4(ps[ob[g]][:, 0:260])
                    B.tt('dve', h4(o16[:, 256 * g:256 * (g + 1)]), o3[:, :, 0:64],
                         rec[:, 4 * g:4 * g + 4].unsqueeze(2).to_broadcast([P, 4, 64]), ALU.mult,
                         [PSK(ob[g]), 'rec'], ['o16'])
                yield
                b = 4
                B.tr([(ps16[b][:, c * P:(c + 1) * P], o16[:, c * P:(c + 1) * P]) for c in range(4)], ident16[:],
                     ['o16', 'ident16'], [PSK(b)])
                so = i % 2
                B.copy('act', oT16[so][:].rearrange("p a b -> p (a b)"), ps16[b][:, 0:512], [PSK(b)], [f'oT16_{so}'])
                B.dma('pool', mixT_s[i, :, 0:512], oT16[so][:].rearrange("p a b -> p (a b)"),
                      [f'oT16_{so}'], [f'mixA{i}'], f'st_oT{so}')
                yield

            def load_x_a(t):
                s2 = t % 2
                B.dma('sp', xt[s2][:], x_src[t * P:(t + 1) * P, :], [f'xs{t}'] if l > 0 else [], [f'xt{s2}'], f'ld_xt{s2}')

            def load_rope(t):
                s2 = t % 2
                B.dma('sp', ropet[s2][:], rope_d[t], [], [f'rope{s2}'], f'ld_rope{s2}')

            def proj_q(t):
                s2 = t % 2
                hT_, hk = hTs[s2], f'hT{s2}'
                bq = 0
                if t + 2 < NT:
                    load_x_a(t + 2)
                if t + 1 < NT:
                    load_rope(t + 1)
                for j in range(4):
                    B.mm([(ps[bq][:, j * P:(j + 1) * P], win16[:, kc, QC + j * P:QC + (j + 1) * P], hT_[:, kc, :], kc == 0, kc == 7)
                          for kc in range(8)], ['win16', hk], [PSK(bq)])
                    if j == 1:
                        yield
                yield
                B.copy('act', qk32[:, 0:4, :].rearrange("p a b -> p (a b)"), ps[bq][:, :], [PSK(bq)], ['q32'])
                yield
                B.mm([(ps[bq][:, j * P:(j + 1) * P], perm[:], qk32[:, j, :], True, True) for j in range(4)],
                     ['consts', 'q32'], [PSK(bq)])
                yield
                B.tt('dve', rt1[:, 0:4, :], qk32[:, 0:4, :], ropet[s2][:, 0, :].unsqueeze(1).to_broadcast([P, 4, P]), ALU.mult,
                     ['q32', f'rope{s2}'], ['rt1q'])
                yield
                B.tt('dve', rt2[:, 0:4, :], h4(ps[bq][:, :]),
                     ropet[s2][:, 1, :].unsqueeze(1).to_broadcast([P, 4, P]), ALU.mult, [PSK(bq), f'rope{s2}'], ['rt2q'])
                yield
                B.tt('dve', QT[:, t % 3, :, :], rt1[:, 0:4, :], rt2[:, 0:4, :], ALU.add, ['rt1q', 'rt2q'], [f'QT{t % 3}'])
                yield

            def proj_x(t):
                s2 = t % 2
                hT_, hk = hTs[s2], f'hT{s2}'
                bk = 1
                B.mm([(ps[bk][:, 0:P], win16[:, kc, KF:KF + P], hT_[:, kc, :], kc == 0, kc == 7) for kc in range(8)],
                     ['win16', hk], [PSK(bk)])
                yield
                B.copy('act', qk32[:, 4, :], ps[bk][:, 0:P], [PSK(bk)], ['k32'])
                yield
                B.mm([(ps[bk][:, P:2 * P], perm[:], qk32[:, 4, :], True, True)]
                     + [(ps[bk][:, 2 * P:4 * P], hT_[:, kc, :], win16[:, kc, KVO:KVO + 256], kc == 0, kc == 7) for kc in range(8)],
                     ['win16', hk, 'consts', 'k32'], [PSK(bk)])
                yield
                B.tt('dve', rt1[:, 4, :], qk32[:, 4, :], ropet[s2][:, 0, :], ALU.mult, ['k32', f'rope{s2}'], ['rt1k'])
                yield
                B.tt('dve', rt2[:, 4, :], ps[bk][:, P:2 * P], ropet[s2][:, 1, :], ALU.mult, [PSK(bk), f'rope{s2}'], ['rt2k'])
                yield
                B.copy('act', kv32[s2][:], ps[bk][:, 2 * P:4 * P], [PSK(bk)], [f'kv32_{s2}'])
                B.copy('dve', VA[:, t, :, 0:64], ps[bk][:, 3 * P:4 * P].rearrange("p (g d) -> p g d", g=2),
                       [PSK(bk), 'VAall'], [f'VA{t}'])
                yield
                B.tt('dve', KT[:, t * P:(t + 1) * P], rt1[:, 4, :], rt2[:, 4, :], ALU.add, ['rt1k', 'rt2k'], [f'KT{t}'])
                B.dma('pool', kout_d[l, t * P:(t + 1) * P, :], kv32[s2][:, 0:128], [f'kv32_{s2}'], [], f'st_kv{s2}')
                B.dma('pool', vout_d[l, t * P:(t + 1) * P, :], kv32[s2][:, 128:256], [f'kv32_{s2}'], [], f'st_kv{s2}')
                yield
                if t + 1 < NT:
                    s3 = (t + 1) % 2
                    yield from norm_mod_g(xt[s3], f'xt{s3}', M[0], M[1], 'M0', 'M1', 1, hTs[s3], f'hT{s3}')

            def proj_y(t):
                s2 = t % 2
                hT_, hk = hTs[s2], f'hT{s2}'
                bqk, bz = 2, 3
                B.mm([(ps[bz][0:32, 0:P], win16[:, kc, GZO:GZO + 32], hT_[:, kc, :], kc == 0, kc == 7) for kc in range(8)],
                     ['win16', hk], [PSK(bz)])
                yield
                B.copy('act', zT33[0:32, :], ps[bz][0:32, 0:P], [PSK(bz)], ['zT33'])
                yield
                B.mm([(ps[bz][:, :], zT33[0:33, :], gaug, True, True)], ['zT33', 'gaug'], [PSK(bz)])
                assert S.last_w.get(f'lxg{s2}') is not None
                B.mm([(ps[bqk][:, :], hT_[:, kc, :], win16[:, kc, GQKO:GQKO + 512], kc == 0, kc == 7) for kc in range(8)],
                     ['win16', hk], [PSK(bqk)])
                yield
                B.act(sp32[:], ps[bz][:, :], AF.Exp, [PSK(bz)], ['sp32'], scale=-1.0)
                B.copy('dve', gqk32[:], ps[bqk][:, :], [PSK(bqk)], ['gqk32'])
                yield
                B.act(sp32[:], sp32[:], AF.Ln, ['sp32'], ['sp32'], bias=1.0)
                yield
                yield from proj_dir(t, 0, bz)

            def proj_dir(t, dr, b):
                s2 = t % 2
                hT_, hk = hTs[s2], f'hT{s2}'
                B.mm([(ps[b][:, 0:256], tri[:, 2 * dr, :], sp32[:, 256 * dr:256 * (dr + 1)], True, True),
                      (ps[b][:, 256:512], tri[:, 2 * dr + 1, :], sp32[:, 256 * dr:256 * (dr + 1)], True, True)],
                     ['consts', 'sp32'], [PSK(b)])
                yield
                B.act(EqEe[dr][:], ps[b][:, :], AF.Exp, [PSK(b)], [f'EqEe{dr}'], scale=-1.0 / 16)
                yield
                B.act(Ek[dr][:], ps[b][:, 0:256], AF.Exp, [PSK(b)], [f'Ek{dr}'], scale=1.0 / 16)
                yield
                qd, kd = qin16[:, 256 * dr:256 * (dr + 1)], kin16[:, 256 * dr:256 * (dr + 1)]
                B.stt('dve', qd, gqk32[:, 0:256], 0.125, EqEe[dr][:, 0:256], ALU.mult, ALU.mult, ['gqk32', f'EqEe{dr}'], [f'qin16_{dr}'])
                yield
                B.tt('dve', kd, gqk32[:, 256:512], Ek[dr][:], ALU.mult, ['gqk32', f'Ek{dr}'], [f'kin16_{dr}'])
                yield
                B.tt('dve', ke16[s2][:, 256 * dr:256 * (dr + 1)], gqk32[:, 256:512], EqEe[dr][:, 256:512], ALU.mult,
                     ['gqk32', f'EqEe{dr}'], [f'ke16_{s2}_{dr}'])
                B.dma('pool', ke_s[t, :, 256 * dr:256 * (dr + 1)], ke16[s2][:, 256 * dr:256 * (dr + 1)],
                      [f'ke16_{s2}_{dr}'], [f'ke_s{t}_{dr}'], f'st_ke{s2}{dr}')
                specs = []
                for hp in range(2):
                    specs.append((ps16[b][:, hp * P:(hp + 1) * P], qin16[:, 256 * dr + hp * P:256 * dr + (hp + 1) * P]))
                    specs.append((ps16[b][:, (2 + hp) * P:(3 + hp) * P], kin16[:, 256 * dr + hp * P:256 * dr + (hp + 1) * P]))
                B.tr(specs, ident16[:], [f'qin16_{dr}', f'kin16_{dr}', 'ident16'], [PSK(b)])
                yield
                B.copy('act', qkT16[s2][:, 4 * dr:4 * dr + 4, :].rearrange("p a b -> p (a b)"), ps16[b][:, 0:512], [PSK(b)], [f'qkT16_{s2}_{dr}'])
                B.dma('pool', qkT_s[t, :, 512 * dr:512 * (dr + 1)], qkT16[s2][:, 4 * dr:4 * dr + 4, :].rearrange("p a b -> p (a b)"),
                      [f'qkT16_{s2}_{dr}'], [f'qkT_s{t}_{dr}'], f'st_qkT{s2}{dr}')
                yield
                if dr == 0:
                    B.mm([(ps[b][:, d2 * 2 + hp:d2 * 2 + hp + 1], sp32[:, 256 * d2 + hp * P:256 * d2 + (hp + 1) * P], ones_col[:, :], True, True)
                          for d2 in range(2) for hp in range(2)], ['sp32', 'ones_col'], [PSK(b)])
                    yield
                    B.act(dec32[s2][:], ps[b][:, 0:4], AF.Exp, [PSK(b)], [f'dec{s2}'], scale=-1.0 / 16)
                    B.dma('pool', dec_s[t], dec32[s2][:], [f'dec{s2}'], [f'dec_s{t}'], f'st_dec{s2}')
                    yield
                else:
                    B.mm([(ps[b][:, :], hT_[:, kc, :], win16[:, kc, GVRO:GVRO + 512], kc == 0, kc == 7) for kc in range(8)],
                         ['win16', hk], [PSK(b)])
                    yield
                    B.copy('act', v16[s2][:], ps[b][:, 0:256], [PSK(b)], [f'v16_{s2}'])
                    B.dma('pool', v16_s[t], v16[s2][:], [f'v16_{s2}'], [f'v16_s{t}'], f'st_v16{s2}')
                    yield
                    B.act(sgr32[s2][:], ps[b][:, 256:512], AF.Exp, [PSK(b)], [f'sgr{s2}'], scale=-1.0)
                    yield
                    B.ts('dve', sgr32[s2][:], sgr32[s2][:], 1.0, None, ALU.add, None, [f'sgr{s2}'], [f'sgr{s2}'])
                    yield
                    B.recip(sgr32[s2][:], sgr32[s2][:], [f'sgr{s2}'], [f'sgr{s2}'])
                    yield
                    B.tt('dve', sgr32[s2][:], ps[b][:, 256:512], sgr32[s2][:], ALU.mult, [PSK(b), f'sgr{s2}'], [f'sgr{s2}'])
                    yield
                    B.tt('pool', sgr32[s2][:], sgr32[s2][:], gnb[:], ALU.mult, [f'sgr{s2}', 'gnb'], [f'sgr{s2}'])
                    B.dma('pool', sgr_s[t], sgr32[s2][:], [f'sgr{s2}'], [f'sgr_s{t}'], f'st_sgr{s2}')
                    yield

            def proj_y1(t):
                s2 = t % 2
                hT_, hk = hTs[s2], f'hT{s2}'
                for j, co in enumerate((LXO, LXO + P, LGO, LGO + P)):
                    B.mm([(ps[2][:, j * P:(j + 1) * P], win16[:, kc, co:co + P], hT_[:, kc, :], kc == 0, kc == 7)
                          for kc in range(8)], ['win16', hk], [PSK(2)])
                yield
                B.copy('dve', lxg[s2][:].rearrange("p a b -> p (a b)"), ps[2][:, :], [PSK(2)], [f'lxg{s2}'])
                B.dma('pool', lxg_s[:, :, t * P:(t + 1) * P].rearrange("c p n -> p c n"), lxg[s2][:],
                      [f'lxg{s2}'], ['lxg_s'], f'st_lxg{s2}')
                yield
                for _ in range(3):
                    yield
                assert S.last_w.get('gqk32') is not None
                yield from proj_dir(t, 1, 2)

            def interleave(gens, weights=None):
                gens = list(gens)
                weights = weights or [1] * len(gens)
                alive = [True] * len(gens)
                while any(alive):
                    for gi, g_ in enumerate(gens):
                        for _ in range(weights[gi]):
                            if alive[gi]:
                                try:
                                    next(g_)
                                except StopIteration:
                                    alive[gi] = False

            load_x_a(0)
            load_x_a(1)
            load_rope(0)
            norm_mod(xt[0], 'xt0', M[0], M[1], 'M0', 'M1', 0, hTs[0], 'hT0')
            for t in range(NT + 2):
                gs = []
                if t < NT:
                    gs.append(proj_y(t))
                    gs.append(proj_y1(t))
                    gs.append(proj_x(t))
                    gs.append(proj_q(t))
                if t >= 2:
                    gs.append(attention(t - 2))
                interleave(gs)
                if t == 0:
                    S.ckpt(3)
            S.ckpt(5)

            S.barrier()
            for j in range(2):
                mid_ = W2PRE + (32 - W2PRE) // 2
                k0, k1 = (W2PRE, mid_) if j == 0 else (mid_, 32)
                B.dma('pool', w2_16[:, k0:k1, :], w2_d[l, k0 * P:k1 * P, :].rearrange("(k p) n -> p k n", p=P), [], ['w2'], f'ld_w2b{j}')
            for dr in range(2):
                B.dma('sp', S32[dr][:], sgla_d[l, dr], [], [f'S32_{dr}'], f'ld_S{dr}')
                B.copy('dve', S16[dr][:].rearrange("p a b -> p (a b)"), S32[dr][:].rearrange("p a b -> p (a b)"),
                       [f'S32_{dr}'], [f'S16_{dr}'])

            def gla_load(dr, t, slot):
                k_ = f'ld_g{dr}{slot}'
                B.dma('sp', gq_ld[dr][slot][:].rearrange("p a b -> p (a b)"), qkT_s[t, :, dr * 512:(dr + 1) * 512],
                      [f'qkT_s{t}_{dr}'], [f'gq{dr}{slot}'], k_)
                B.dma('sp', gke_ld[dr][slot][:], ke_s[t, :, dr * 256:(dr + 1) * 256], [f'ke_s{t}_{dr}'], [f'gke{dr}{slot}'], k_)
                B.dma('sp', gv_ld[dr][slot][:], v16_s[t], [f'v16_s{t}'], [f'gv{dr}{slot}'], k_)
                B.dma('sp', gdec_ld[dr][slot][:], dec_s[t, :, dr * 2:(dr + 1) * 2], [f'dec_s{t}'], [f'gdec{dr}{slot}'], k_)

            def gla_final(t, b, si):
                assert S.last_w.get(f'o_s0_{t}', -1) >= b_start and S.last_w.get(f'o_s1_{t}', -1) >= b_start, t
                F_ = FS[si]
                fo, fsq, fy, fy16, fyT, st_ = F_['fo'], F_['fsq'], F_['fy'], F_['fy16'], F_['fyT'], F_['st']
                n_ = lambda k: f'{k}_f{si}'
                k_ = f'ld_fo{si}'
                B.dma('sp', fo[0][:], o_s[0, t], [f'o_s0_{t}'], [n_('fo0')], k_)
                B.dma('sp', fo[1][:], o_s[1, t], [f'o_s1_{t}'], [n_('fo1')], k_)
                B.dma('sp', fo[2][:], sgr_s[t], [f'sgr_s{t}'], [n_('fo2')], k_)
                fr = [n_('fo0'), n_('fo1'), n_('fo2')]
                yield
                B.tt('dve', fy[:], fo[0][:], fo[1][:], ALU.add, fr, [n_('fy')])
                yield
                B.act(fsq[:], fy[:], AF.Square, [n_('fy')], [n_('fsq')])
                yield
                B.reduce_sum(st_[:, 0:4], h4(fsq[:]), [n_('fsq')], [n_('st')])
                yield
                B.act(st_[:, 0:4], st_[:, 0:4], AF.Ln, [n_('st')], [n_('st')], bias=EPS, scale=1.0 / 64)
                yield
                B.act(st_[:, 0:4], st_[:, 0:4], AF.Exp, [n_('st')], [n_('st')], scale=-0.5)
                yield
                B.tt('dve', h4(fy[:]), h4(fy[:]), st_[:, 0:4].unsqueeze(2).to_broadcast([P, 4, 64]), ALU.mult, [n_('fy'), n_('st')], [n_('fy')])
                yield
                B.tt('dve', fy16[:], fy[:], fo[2][:], ALU.mult, [n_('fy')] + fr, [n_('fy16')])
                yield
                B.tr([(ps16[b][:, c * P:(c + 1) * P], fy16[:, c * P:(c + 1) * P]) for c in range(2)], ident16[:],
                     [n_('fy16'), 'ident16'], [PSK(b)])
                yield
                B.copy('act', fyT[:].rearrange("p a b -> p (a b)"), ps16[b][:, 0:256], [PSK(b)], [n_('fyT')])
                B.dma('pool', mixT_s[t, :, 512:768], fyT[:].rearrange("p a b -> p (a b)"), [n_('fyT')], [f'mixG{t}'], f'st_fyT{si}')
                yield

            fin_done = set()
            b_start = len(S.ins)

            def fin_pop():
                for t_ in range(NT):
                    if (t_ not in fin_done and S.last_w.get(f'o_s0_{t_}', -1) >= b_start
                            and S.last_w.get(f'o_s1_{t_}', -1) >= b_start):
                        fin_done.add(t_)
                        return t_
                return None

            def gla(dr):
                bA, bB = 2 * dr, 2 * dr + 1
                order_ = order[dr]
                gla_load(dr, order_[0], 0)
                for step in range(NT):
                    t = order_[step]
                    slot = step % 2
                    if step + 1 < NT:
                        gla_load(dr, order_[step + 1], (step + 1) % 2)
                    qk = gq_ld[dr][slot]
                    ke_ = gke_ld[dr][slot]
                    v_ = gv_ld[dr][slot]
                    dc_ = gdec_ld[dr][slot]
                    rk = [f'gq{dr}{slot}', f'gke{dr}{slot}', f'gv{dr}{slot}', f'gdec{dr}{slot}']
                    seq_start = (t % 2 == 0) if dr == 0 else (t % 2 == 1)
                    s32f = S32[dr][:].rearrange("p a b -> p (a b)")
                    s16f = S16[dr][:].rearrange("p a b -> p (a b)")
                    if seq_start and step > 0:
                        B.ts('dve', s32f, s32f, sf, None, ALU.mult, None, [f'S32_{dr}', FLG], [f'S32_{dr}'])
                        B.copy('dve', s16f, s32f, [f'S32_{dr}'], [f'S16_{dr}'])
                        yield
                    ba = [bA, bB]
                    B.mm([(ps[ba[h % 2]][:, (h // 2) * P:(h // 2 + 1) * P], qk[64 * (h % 2):64 * (h % 2) + 64, 2 + h // 2, :],
                           qk[64 * (h % 2):64 * (h % 2) + 64, h // 2, :], True, True) for h in range(4)],
                         rk, [PSK(bA), PSK(bB)])
                    yield
                    at4 = h4(AT16[dr][:])
                    for hh in range(2):
                        B.tt('dve', at4[:, hh:4:2, :], h4(ps[ba[hh]][:, 0:256], 2),
                             glam16[:, dr, :].unsqueeze(1).to_broadcast([P, 2, P]), ALU.mult,
                             [PSK(ba[hh]), 'glam16'], [f'AT16_{dr}'])
                        yield
                    specs = []
                    for h in range(4):
                        specs.append((ps[bA][:, h * 64:(h + 1) * 64], AT16[dr][:, h * P:(h + 1) * P], v_[:, h * 64:(h + 1) * 64], True, False))
                        specs.append((ps[bA][:, h * 64:(h + 1) * 64], qk[64 * (h % 2):64 * (h % 2) + 64, h // 2, :],
                                      S16[dr][64 * (h % 2):64 * (h % 2) + 64, h // 2, :], False, True))
                    B.mm(specs, [f'AT16_{dr}', f'S16_{dr}'] + rk, [PSK(bA)])
                    B.mm([(ps[bB][:, hp * P:(hp + 1) * P], ke_[:, hp * P:(hp + 1) * P], v_[:, hp * P:(hp + 1) * P], True, True)
                          for hp in range(2)], rk, [PSK(bB)])
                    yield
                    os_ = o32[dr][slot]
                    B.copy('act', os_[:], ps[bA][:, 0:256], [PSK(bA)], [f'o32_{dr}{slot}'])
                    B.dma('pool', o_s[dr, t], os_[:], [f'o32_{dr}{slot}'], [f'o_s{dr}_{t}'], f'st_o{dr}{slot}')
                    yield
                    for hp in range(2):
                        for hh in range(2):
                            B.stt('dve', S32[dr][64 * hh:64 * hh + 64, hp, :], S32[dr][64 * hh:64 * hh + 64, hp, :],
                                  dc_[64 * hh:64 * hh + 64, hp:hp + 1], ps[bB][64 * hh:64 * hh + 64, hp * P + hh * 64: hp * P + hh * 64 + 64],
                                  ALU.mult, ALU.add, [f'S32_{dr}', PSK(bB)] + rk, [f'S32_{dr}'])
                        yield
                    B.copy('dve', s16f, s32f, [f'S32_{dr}'], [f'S16_{dr}'])
                    seq_end = (t % 2 == 1) if dr == 0 else (t % 2 == 0)
                    if seq_end:
                        sq = t // 2
                        B.dma('pool', gst_d[sq, l, dr].rearrange("(hp hh) d v -> (hh d) hp v", hp=2), S32[dr][:],
                              [f'S32_{dr}'], [], f'st_gst{dr}')
                    yield
                    tt_ = fin_pop()
                    if tt_ is not None:
                        yield from gla_final(tt_, bA, dr)
                while len(fin_done) < NT:
                    tt_ = fin_pop()
                    if tt_ is None:
                        yield
                    else:
                        yield from gla_final(tt_, bA, dr)

            def lru(ct):
                L_ = LT[ct]
                hf, lxc, xc, xc16, rg, ig, av, bt = (L_[k] for k in ('hf', 'lxc', 'xc', 'xc16', 'rg', 'ig', 'av', 'bt'))
                hb, lgc, y16, stc, sto = (L_[k] for k in ('hb', 'lgc', 'y16', 'stc', 'sto'))
                br_, bi_ = 4 + 2 * ct, 5 + 2 * ct
                n_ = lambda k: f'{k}_{ct}'
                wv = lambda k: lvec[:, ct * 11 + k: ct * 11 + k + 1]
                nw = lambda k: lder[:, 4 + ct * 4 + k: 4 + ct * 4 + k + 1]

                def conv(c, slot):
                    c0 = c * CH
                    lo = max(0, c0 - 2)
                    hi = min(T, c0 + CH + 1)
                    lt = lxc[slot]
                    lk = n_(f'lxc{slot}')
                    if c == 0:
                        B.memset('pool', lt[:, 0:2], 0.0, [lk])
                    if c == NCH - 1:
                        B.memset('pool', lt[:, CH + 2:CH + 4], 0.0, [lk])
                    B.dma('sp', lt[:, 2 - (c0 - lo):2 - (c0 - lo) + (hi - lo)], lxg_s[ct, :, lo:hi], ['lxg_s'], [lk], f'ld_lxc{ct}{slot}')
                    yield
                    B.ts('pool', xc[:], lt[:, 2:2 + CH], wv(2), wv(4), ALU.mult, ALU.add, [lk, 'lvec'], [n_('xc')])
                    yield
                    for k in (0, 1, 3):
                        B.stt('dve', xc[:], lt[:, k:k + CH], wv(k), xc[:], ALU.mult, ALU.add, [lk, 'lvec', n_('xc')], [n_('xc')])
                        yield
                    for (tt_, k) in ((0, 0), (0, 1), (1, 0), (255, 3)):
                        src = lt[:, tt_ + k: tt_ + k + 257: 256]
                        dst = xc[:, tt_: tt_ + 257: 256]
                        B.stt('dve', dst, src, nw(k), dst, ALU.mult, ALU.add, [lk, 'lder', n_('xc')], [n_('xc')])
                    yield
                    B.copy('act', xc16[:], xc[:], [n_('xc')], [n_('xc16')])
                    yield

                def gates(dr, resets):
                    B.mm([(ps[br_][:, :], wbd16[:, 0 * 4 + dr * 2 + ct, :], xc16[:], True, True)], ['wbd16', n_('xc16')], [PSK(br_)])
                    B.mm([(ps[bi_][:, :], wbd16[:, 1 * 4 + dr * 2 + ct, :], xc16[:], True, True)], ['wbd16', n_('xc16')], [PSK(bi_)])
                    yield
                    vb = ct * 11 + 5 + dr * 3
                    B.act(rg[:], ps[br_][:, :], AF.Sigmoid, [PSK(br_), 'lvec'], [n_('rg')], bias=lvec[:, vb:vb + 1])
                    yield
                    B.act(ig[:], ps[bi_][:, :], AF.Sigmoid, [PSK(bi_), 'lvec'], [n_('ig')], bias=lvec[:, vb + 1:vb + 2])
                    yield
                    B.act(av[:], rg[:], AF.Exp, [n_('rg'), 'lder'], [n_('av')], scale=lder[:, dr * 2 + ct: dr * 2 + ct + 1])
                    yield
                    B.act(rg[:], av[:], AF.Square, [n_('av')], [n_('rg')])
                    yield
                    B.act(rg[:], rg[:], AF.Sqrt, [n_('rg')], [n_('rg')], bias=1.0, scale=-1.0)
                    yield
                    B.tt('dve', ig[:], ig[:], xc[:], ALU.mult, [n_('ig'), n_('xc')], [n_('ig')])
                    yield
                    B.tt('dve', bt[:], rg[:], ig[:], ALU.mult, [n_('rg'), n_('ig')], [n_('bt')])
                    for r0 in resets:
                        d_ = av[:, r0:r0 + 257:256]
                        B.ts('dve', d_, d_, sf, None, ALU.mult, None, [n_('av'), FLG], [n_('av')])
                    yield

                for c in range(NCH):
                    yield from conv(c, c % 2)
                    yield from gates(0, [0])
                    init = lruh0[:, (l * 2 + 0) * 2 + ct:(l * 2 + 0) * 2 + ct + 1] if c == 0 else hf[:, c * CH - 1:c * CH]
                    B.scan(hf[:, c * CH:(c + 1) * CH], av[:], bt[:], init, [n_('av'), n_('bt'), 'consts', n_('hf')], [n_('hf')])
                    yield
                B.copy('dve', stc[:, 0, :], hf[:, 255:T:256], [n_('hf')], [n_('stc')])
                for ci, c in enumerate(range(NCH - 1, -1, -1)):
                    s2 = ci % 2
                    yield from conv(c, c % 2)
                    yield from gates(1, [255])
                    B.dma('sp', lgc[s2][:], lxg_s[2 + ct, :, c * CH:(c + 1) * CH], ['lxg_s'], [n_(f'lgc{s2}')], f'ld_lgc{ct}{s2}')
                    init = lruh0[:, (l * 2 + 1) * 2 + ct:(l * 2 + 1) * 2 + ct + 1] if ci == 0 else hb[1 - s2][:, 0:1]
                    B.scan(hb[s2][:, ::-1], av[:, ::-1], bt[:, ::-1], init,
                           [n_('av'), n_('bt'), 'consts', n_(f'hb{1 - s2}')], [n_(f'hb{s2}')])
                    yield
                    B.copy('dve', stc[:, 1, 2 * c:2 * c + 2], hb[s2][:, 0:257:256], [n_(f'hb{s2}')], [n_('stc')])
                    g_ = lgc[s2]
                    gk = n_(f'lgc{s2}')
                    B.tt('dve', av[:], hf[:, c * CH:(c + 1) * CH], hb[s2][:], ALU.add, [n_('hf'), n_(f'hb{s2}'), n_('av')], [n_('av')])
                    yield
                    B.tt('pool', bt[:], g_[:], g_[:], ALU.mult, [gk, n_('bt')], [n_('bt')])
                    yield
                    B.ts('pool', bt[:], bt[:], 0.044715, 1.0, ALU.mult, ALU.add, [n_('bt')], [n_('bt')])
                    yield
                    B.tt('pool', bt[:], bt[:], g_[:], ALU.mult, [n_('bt'), gk], [n_('bt')])
                    yield
                    B.act(bt[:], bt[:], AF.Sigmoid, [n_('bt')], [n_('bt')], scale=1.5957691216057308)
                    B.tt('dve', av[:], av[:], g_[:], ALU.mult, [n_('av'), gk], [n_('av')])
                    yield
                    B.tt('dve', y16[s2][:], av[:], bt[:], ALU.mult, [n_('av'), n_('bt')], [n_(f'y16_{s2}')])
                    B.dma('pool', mixT_s[4 * c:4 * c + 4, :, (6 + ct) * P:(7 + ct) * P].rearrange("t p n -> p t n"),
                          y16[s2][:].rearrange("p (t n) -> p t n", t=4), [n_(f'y16_{s2}')], [f'mixL{ct}'], f'st_y16{ct}{s2}')
                    yield
                for dr in range(2):
                    b = br_ if dr == 0 else bi_
                    B.mm([(ps[b][0:16, 0:P], stc[:, dr, :], identf[:], True, True)], [n_('stc'), 'consts'], [PSK(b)])
                    B.copy('act', sto[dr][0:16, :], ps[b][0:16, 0:P], [PSK(b)], [n_(f'sto{dr}')])
                    B.dma('pool', lst_d[:, l, dr, ct * P:(ct + 1) * P], sto[dr][0:16, :], [n_(f'sto{dr}')], [], f'st_sto{ct}{dr}')
                    yield

            order = [list(range(NT)), list(range(NT - 1, -1, -1))]
            interleave([gla(0), gla(1), lru(0), lru(1)])

            S.ckpt(8)
            S.barrier()
            B.nmod = 6
            for j in range(4):
                B.dma('pool', w1_16[:, 2 * j:2 * j + 2, :],
                      w1_d[l, 2 * j * P:(2 * j + 2) * P, :].rearrange("(k p) n -> p k n", p=P), [], ['w1'], f'ld_w1{j}')
            B.dma('pool', w2_16[:, 0:W2PRE, :], w2_d[l, 0:W2PRE * P, :].rearrange("(k p) n -> p k n", p=P), [], ['w2'], 'ld_w2a')
            last = (l == L - 1)
            load_mod(M[0], 2, 'M0', 'ld_m0')
            load_mod(M[1], 4, 'M1', 'ld_m1')
            B.dma('sp', tmp32[:], n2_d[l].partition_broadcast(P), [], ['tmp32'], 'ld_tmp')
            B.stt('dve', M[1][:], M[1][:], 1.0, tmp32[:], ALU.add, ALU.mult, ['M1', 'tmp32'], ['M1'])
            load_mod(M[2], 3, 'M2', 'ld_m2')
            load_mod(M[3], 5, 'M3', 'ld_m3')
            if last:
                fnb = stage[0][:, 0:D]
                B.dma('sp', fnb, fn_d.partition_broadcast(P), [], ['fnb', 'stage0'], 'ld_st0')
            S.ckpt(9)

            def front(t):
                s2 = t % 2
                xs_ = xt[s2]
                xk = f'xt{s2}'
                yield
                yield
                for hh in range(2):
                    b = hh
                    B.mm([(ps[b][:, :], mixTt[s2][:, kc, :], wout16[:, kc, hh * 512:(hh + 1) * 512], kc == 0, kc == 7)
                          for kc in range(8)], [f'mixTt{s2}', 'wout16'], [PSK(b)])
                    yield
                for hh in range(2):
                    b = hh
                    B.tt('dve', tmp32[:, hh * 512:(hh + 1) * 512], ps[b][:, :], M[0][:, hh * 512:(hh + 1) * 512], ALU.mult,
                         [PSK(b), 'M0'], ['tmp32'])
                    yield
                B.tt('dve', xs_[:], xs_[:], tmp32[:], ALU.add, [xk, 'tmp32'], [xk])
                yield
                yield from norm_mod_g(xs_, xk, M[1], M[2], 'M1', 'M2', 2, hTs[s2], f'hT{s2}')

            def mlp(t):
                s2 = t % 2
                xs_ = xt[s2]
                xk = f'xt{s2}'
                hT_ = hTs[s2]
                hk = f'hT{s2}'

                def mlp1(c8):
                    b = 3 + c8 % 2
                    B.mm([(ps[b][:, :], hT_[:, kc, :], w1_16[:, kc, c8 * 512:(c8 + 1) * 512], kc == 0, kc == 7)
                          for kc in range(8)], ['w1', hk], [PSK(b)])
                    B.act(rl32[:], ps[b][:, :], AF.Relu, [PSK(b)], ['rl32'])
                    B.tt('dve', u16[c8 % 2][:], rl32[:], rl32[:], ALU.mult, ['rl32'], [f'u16_{c8 % 2}'])

                def utr(c8):
                    b = 5
                    B.tr([(ps16[b][:, fi * P:(fi + 1) * P], u16[c8 % 2][:, fi * P:(fi + 1) * P]) for fi in range(4)], ident16[:],
                         [f'u16_{c8 % 2}', 'ident16'], [PSK(b)])
                    B.copy('act', uT[:, c8 % 2, :, :].rearrange("p a b -> p (a b)"), ps16[b][:, 0:512], [PSK(b)], [f'uT{c8 % 2}'])

                def mlp2(f4):
                    for hh in range(2):
                        B.mm([(ps[6 + hh][:, :], uT[:, f4 % 2, fi, :], w2_16[:, f4 * 4 + fi, hh * 512:(hh + 1) * 512],
                               f4 == 0 and fi == 0, f4 == 7 and fi == 3) for fi in range(4)],
                             [f'uT{f4 % 2}', 'w2'], [PSK(6 + hh)])

                mlp1(0)
                for c8 in range(8):
                    if c8 + 1 < 8:
                        mlp1(c8 + 1)
                        yield
                    utr(c8)
                    if c8 >= 1:
                        mlp2(c8 - 1)
                    yield
                mlp2(7)
                for hh in range(2):
                    B.tt('dve', rl32[:], ps[6 + hh][:, :], M[3][:, hh * 512:(hh + 1) * 512], ALU.mult,
                         [PSK(6 + hh), 'M3'], ['rl32'])
                    B.tt('dve', xs_[:, hh * 512:(hh + 1) * 512], xs_[:, hh * 512:(hh + 1) * 512], rl32[:], ALU.add, [xk, 'rl32'], [xk])
                yield
                if not last:
                    B.dma('pool', xs_s[t * P:(t + 1) * P, :], xs_[:], [xk], [f'xs{t}'], f'st_x{s2}')
                else:
                    B.act(h16[:], xs_[:], AF.Square, [xk], ['h16', 'st0'], accum=st4[:, 0:1])
                    B.act(st4[:, 1:2], st4[:, 0:1], AF.Ln, ['st0'], ['st1'], bias=EPS, scale=1.0 / D)
                    B.act(st4[:, 2:3], st4[:, 1:2], AF.Exp, ['st1'], ['st2'], scale=-0.5)
                    B.stt('dve', xs_[:], xs_[:], st4[:, 2:3], fnb, ALU.mult, ALU.mult, [xk, 'st2', 'fnb'], [xk])
                    B.dma('pool', y_d[t * P:(t + 1) * P, :], xs_[:], [xk], [], f'st_x{s2}')
                yield

            def loads_c(t):
                s2 = t % 2
                B.dma('sp', xt[s2][:], x_src[t * P:(t + 1) * P, :], [f'xs{t}'] if l > 0 else [], [f'xt{s2}'], f'ld_xt{s2}')
                B.dma('sp', mixTt[s2][:].rearrange("p a b -> p (a b)"), mixT_s[t],
                      [f'mixA{t}', f'mixG{t}', 'mixL0', 'mixL1'], [f'mixTt{s2}'], f'ld_mix{s2}')

            loads_c(0)
            for t in range(NT + 1):
                gs = []
                if t >= 1:
                    gs.append(mlp(t - 1))
                if t < NT:
                    gs.append(front(t))
                interleave(gs)
                if t + 1 < NT:
                    loads_c(t + 1)
                if t == 1:
                    S.ckpt(10)
            S.ckpt(11)
        S.finalize()
        print("program: %d sched-instrs, eng counts %s, dma keys %d, phaseA carve end %d" %
              (len(S.ins), S.eng_count, len(S.dma_keys), offA), flush=True)
        eng_sems = {e: [es.enter_context(nc.semaphore(f"s_{e}{k}")) for k in range(S.n_eng_sems[e])] for e in COMPUTE}
        dma_sems = {k: es.enter_context(nc.semaphore(f"d_{k}")) for k in S.dma_keys}
        with nc.Block() as block:
            @block.sync
            def _(h):
                S.emit_engine('sp', h, eng_sems, dma_sems, final_wait_all_dma=True)

            @block.tensor
            def _(h):
                S.emit_engine('pe', h, eng_sems, dma_sems)

            @block.scalar
            def _(h):
                S.emit_engine('act', h, eng_sems, dma_sems)

            @block.vector
            def _(h):
                S.emit_engine('dve', h, eng_sems, dma_sems)

            @block.gpsimd
            def _(h):
                S.emit_engine('pool', h, eng_sems, dma_sems)
    return nc


def _consts():
    s = np.arange(P)[:, None]
    c = np.arange(P)[None, :]
    tri = np.stack([(s <= c), (s > c), (s >= c), (s < c)]).astype(np.float32)
    glam = np.stack([(s <= c), (s >= c)]).astype(np.float32)
    perm = np.zeros((P, P), np.float32)
    for dp in range(P):
        hd = dp % 64
        src = dp + 16 if (hd % 32) < 16 else dp - 16
        perm[src, dp] = 1.0
    return tri, glam, perm


def _rope_tables(enabled):
    tab = np.zeros((NT, P, 2, P), np.float32)
    if not enabled:
        tab[:, :, 0, :] = 1.0
        return tab
    t = np.arange(T)
    row = (t // 64).astype(np.float32)
    col = (t % 64).astype(np.float32)
    nf = 16
    inv = (np.float32(10000.0) ** (-np.arange(nf, dtype=np.float32) / nf)).astype(np.float32)
    cosT = np.zeros((64, T), np.float32)
    sinT = np.zeros((64, T), np.float32)
    for d in range(64):
        half, j = d // 32, d % 32
        pos = row if half == 0 else col
        ang = (pos * inv[j % 16]).astype(np.float32)
        cosT[d] = np.cos(ang)
        sinT[d] = -np.sin(ang) if j < 16 else np.sin(ang)
    cos2 = np.concatenate([cosT, cosT], 0).reshape(P, NT, P).transpose(1, 0, 2)
    sin2 = np.concatenate([sinT, sinT], 0).reshape(P, NT, P).transpose(1, 0, 2)
    tab[:, :, 0, :] = cos2
    tab[:, :, 1, :] = sin2
    return tab


_NC_CACHE = {}


def prepare(x_prompt, x_sample, c, cache_k, cache_v, state_gla, state_lru, c_ctx,
           w_mod, b_mod, norm1, norm2, w_in, attn_sink, gla_gate_w, gla_gate_b, gla_norm,
           lru_conv_w, lru_conv_b, lru_wa, lru_ba, lru_wx, lru_bx, lru_lambda,
           w_out, w_mlp1, w_mlp2, final_norm):
    f = np.float32
    A = lambda a: np.ascontiguousarray(np.asarray(a, dtype=f))
    x_prompt, x_sample, c, cache_k, cache_v = A(x_prompt), A(x_sample), A(c), A(cache_k), A(cache_v)
    state_gla, state_lru, c_ctx = A(state_gla), A(state_lru), A(c_ctx)
    w_in = A(w_in)
    qcols = []
    for cc in range(4):
        qcols += list(range(cc * 64, cc * 64 + 64)) + list(range((4 + cc) * 64, (4 + cc) * 64 + 64))
    cols = (qcols + list(range(512, 640)) + list(range(1824, 2080)) + list(range(2080, 2336)) + list(range(1536, 1568))
            + list(range(512, 640)) + list(range(640, 768)) + list(range(768, 1024)) + list(range(1024, 1280))
            + list(range(1280, 1536)) + list(range(1568, 1824)))
    assert len(cols) == WIN
    w_in_r = np.ascontiguousarray(w_in[:, :, cols])
    gaug = np.zeros((L, 33, 512), f)
    gw, gb = A(gla_gate_w), A(gla_gate_b)
    gaug[:, 0:16, 0:256] = gw[:, 0]
    gaug[:, 16:32, 256:512] = gw[:, 1]
    gaug[:, 32, 0:256] = gb[:, 0]
    gaug[:, 32, 256:512] = gb[:, 1]
    lvec = np.zeros((L, P, NVEC), f)
    cw, cb, ba, bx, lam = A(lru_conv_w), A(lru_conv_b), A(lru_ba), A(lru_bx), A(lru_lambda)
    for ct in range(2):
        sl = slice(ct * P, (ct + 1) * P)
        for k in range(4):
            lvec[:, :, ct * 11 + k] = cw[:, k, sl]
        lvec[:, :, ct * 11 + 4] = cb[:, sl]
        for dr in range(2):
            lvec[:, :, ct * 11 + 5 + dr * 3 + 0] = ba[:, dr, sl]
            lvec[:, :, ct * 11 + 5 + dr * 3 + 1] = bx[:, dr, sl]
            lvec[:, :, ct * 11 + 5 + dr * 3 + 2] = lam[:, dr, sl]
    tri, glam, perm = _consts()
    s_ = np.arange(P)[:, None]
    c_ = np.arange(P)[None, :]
    one = np.ones((P, P), f)
    zero = np.zeros((P, P), f)
    attm_sample = np.stack([s_ >= c_, s_ >= c_, s_ <= c_, s_ <= c_]).astype(f)
    attm_prompt = np.stack([zero, one, one, zero]).astype(f)
    shared = dict(
        ident=np.eye(P, dtype=f), tri=tri, glam=glam, perm=perm,
        w_mod=A(w_mod), b_mod=A(b_mod).reshape(L, 1, 6 * D), norm1=A(norm1).reshape(L, 1, D),
        norm2=A(norm2).reshape(L, 1, D), final_norm=A(final_norm).reshape(1, D), w_in_r=w_in_r,
        w_out=A(w_out), w_mlp1=A(w_mlp1), w_mlp2=A(w_mlp2), gaug=gaug,
        gla_norm=A(gla_norm).reshape(L, 1, 256), attn_sink=A(attn_sink).reshape(L, 1, 8), lruvec=lvec,
        lru_wa=A(lru_wa), lru_wx=A(lru_wx))
    rope_on = _rope_tables(True)
    rope_off = _rope_tables(False)
    in_maps = []
    for core in range(8):
        m = dict(shared)
        if core < 4:
            b = core
            m['x'] = x_sample[b]
            m['cond8'] = np.ascontiguousarray(c[b].reshape(8, P).T)
            m['ck'] = np.ascontiguousarray(cache_k[b].reshape(L, 512, 128))
            m['cv'] = np.ascontiguousarray(cache_v[b].reshape(L, 512, 128))
            sg = state_gla[b].reshape(L, 2, 2, 2, 64, 64)
            m['sgla'] = np.ascontiguousarray(sg.transpose(0, 1, 3, 4, 2, 5).reshape(L, 2, P, 2, 64))
            sl_ = state_lru[b].reshape(L, 2, 2, P)
            m['lruh0'] = np.ascontiguousarray(sl_.transpose(3, 0, 1, 2).reshape(P, L * 4))
            fl = np.zeros((P, 4), f)
            fl[:, 2] = 1.0
            m['flags'] = fl
            m['attm'] = attm_sample
            m['rope'] = rope_on
        else:
            pc = (core - 4) % 2
            m['x'] = np.ascontiguousarray(x_prompt[pc * NSEQ:(pc + 1) * NSEQ].reshape(T, D))
            m['cond8'] = np.ascontiguousarray(c_ctx.reshape(8, P).T)
            m['ck'] = np.zeros((L, 512, 128), f)
            m['cv'] = np.zeros((L, 512, 128), f)
            m['sgla'] = np.zeros((L, 2, P, 2, 64), f)
            m['lruh0'] = np.zeros((P, L * 4), f)
            fl = np.zeros((P, 4), f)
            fl[:, 0] = -30000.0
            fl[:, 1] = 1.0
            fl[:, 3] = -1.0
            m['flags'] = fl
            m['attm'] = attm_prompt
            m['rope'] = rope_off
        in_maps.append(m)
    return in_maps


def kernel(**inputs):
    f = np.float32
    in_maps = prepare(**inputs)
    if 'nc' not in _NC_CACHE:
        _NC_CACHE['nc'] = build_program()
    nc = _NC_CACHE['nc']
    res = run_bass_kernel_spmd(nc, in_maps, core_ids=list(range(8)))
    R = res.results
    y_sample = np.stack([R[b]['y'] for b in range(4)]).astype(f)
    y_prompt = np.concatenate([R[4]['y'], R[5]['y']], 0).reshape(32, 256, D).astype(f)

    def kvfix(name):
        parts = []
        for core in (4, 5):
            a = R[core][name].reshape(L, NSEQ, 256, 2, 64).transpose(1, 0, 2, 3, 4)
            parts.append(a)
        return np.ascontiguousarray(np.concatenate(parts, 0)).astype(f)

    new_k = kvfix('kout')
    new_v = kvfix('vout')
    gst = np.concatenate([R[4]['gst'], R[5]['gst']], 0).astype(f)
    lst = np.concatenate([R[4]['lst'], R[5]['lst']], 0).astype(f)
    return (y_prompt, y_sample, new_k, new_v, gst, lst)
```

```python
import bisect
import contextlib
import numpy as np
import concourse.bass as bass
import concourse.mybir as mybir
from concourse.bass_utils import run_bass_kernel_spmd

F32 = mybir.dt.float32
BF16 = mybir.dt.bfloat16
AF = mybir.ActivationFunctionType
ALU = mybir.AluOpType
AX = mybir.AxisListType

L = 2
D = 1024
T = 4096
NT = 32
P = 128
NSEQ = 16
WIN = 2464
QC, KF, LXO, LGO, GZO, KVO, GQKO, GVRO = 0, 512, 640, 896, 1152, 1184, 1440, 1952
EPS = 1e-6
NVEC = 22

COMPUTE = ('pe', 'act', 'dve', 'pool')
EPOCH = 8000


class Sched:
    def __init__(self):
        self.ins = []
        self.last_w = {}
        self.readers = {}
        self.pending = {}
        self.since_barrier_dma = []
        self.last_on = {}

    def barrier(self):
        ids = set(self.last_on.values()) | set(self.since_barrier_dma)
        self.since_barrier_dma = []
        for e in COMPUTE + ('sp',):
            self.pending[e] = set(ids) | self.pending.get(e, set())

    dead = False

    stopped = False

    def ckpt(self, n):
        kd = os.environ.get('KDEAD', '')
        if kd:
            a, b = [int(v) for v in kd.split(':')]
            if n == a and not self.stopped:
                self.dead = True
            if n == b and not self.stopped:
                self.dead = False
        if KSTOP[0] == n:
            self.dead = True
            self.stopped = True

    def add(self, eng, fn, reads=(), writes=(), dma_key=None):
        if self.dead:
            return -1
        pr = [r for r in reads if isinstance(r, str) and r.startswith('ps') and r[2:].isdigit()]
        if pr:
            writes = list(writes) + [r for r in pr if r not in writes]
        i = len(self.ins)
        deps = set()
        for r in reads:
            w = self.last_w.get(r)
            if w is not None:
                deps.add(w)
        for w_ in writes:
            w = self.last_w.get(w_)
            if w is not None:
                deps.add(w)
            for rd in self.readers.get(w_, {}).values():
                deps.add(rd)
        for r in reads:
            d = self.readers.setdefault(r, {})
            d[('dma', i) if dma_key is not None else eng] = i
        for w_ in writes:
            self.last_w[w_] = i
            self.readers[w_] = {}
        pend = self.pending.pop(eng, None)
        if pend:
            deps |= pend
        deps.discard(i)
        self.ins.append(dict(eng=eng, fn=fn, deps=deps, dma_key=dma_key, id=i))
        if dma_key is not None:
            self.since_barrier_dma.append(i)
        else:
            self.last_on[eng] = i
        return i

    def finalize(self):
        ins = self.ins
        for it in ins:
            best = {}
            keep = set()
            for d in it['deps']:
                p = ins[d]
                if p['dma_key'] is not None:
                    keep.add(d)
                else:
                    e = p['eng']
                    if e == 'pe' and it['eng'] == 'pe' and it['dma_key'] is None:
                        continue
                    if e not in best or best[e] < d:
                        best[e] = d
            keep.update(best.values())
            it['deps'] = sorted(keep)
        sig = set()
        for it in ins:
            sig.update(it['deps'])
        self.eng_count = {e: 0 for e in COMPUTE}
        self.dma_count = {}
        for i, it in enumerate(ins):
            if it['dma_key'] is not None:
                k = it['dma_key']
                self.dma_count[k] = self.dma_count.get(k, 0) + 1
                it['sig'] = ('dma', k, self.dma_count[k] * 16)
            elif i in sig:
                e = it['eng']
                c = self.eng_count[e]
                self.eng_count[e] = c + 1
                it['sig'] = ('eng', e, c // EPOCH, c % EPOCH + 1)
            else:
                it['sig'] = None
        self.n_eng_sems = {e: self.eng_count[e] // EPOCH + 1 for e in COMPUTE}
        self.dma_keys = sorted(self.dma_count.keys(), key=str)
        self.dma_ids = {}
        for i, it in enumerate(ins):
            if it['dma_key'] is not None:
                self.dma_ids.setdefault(it['dma_key'], []).append(i)

    def emit_engine(self, eng, handle, eng_sems, dma_sems, final_wait_all_dma=False):
        seen = {}
        ins = self.ins
        for it in ins:
            if it['eng'] != eng:
                continue
            need = {}
            for d in it['deps']:
                s = ins[d]['sig']
                if s[0] == 'dma':
                    nbefore = bisect.bisect_left(self.dma_ids[s[1]], it['id'])
                    sem, val, key = dma_sems[s[1]], max(s[2], nbefore * 16), ('dma', s[1])
                else:
                    sem, val, key = eng_sems[s[1]][s[2]], s[3], ('eng', s[1], s[2])
                if key not in need or need[key][1] < val:
                    need[key] = (sem, val)
            for key, (sem, val) in need.items():
                if seen.get(key, 0) < val:
                    handle.wait_ge(sem, val)
                    seen[key] = val
            r = it['fn'](handle)
            s = it['sig']
            if s is not None:
                if s[0] == 'dma':
                    r.then_inc(dma_sems[s[1]], 16)
                else:
                    r.then_inc(eng_sems[s[1]][s[2]], 1)
        if final_wait_all_dma:
            for k, n in self.dma_count.items():
                handle.wait_ge(dma_sems[k], n * 16)


import os
KSTOP = [int(os.environ.get('KSTOP', '0'))]


class Builder:
    def __init__(self, nc, es):
        self.nc = nc
        self.es = es
        self.S = Sched()
        self.nb = 0
        self.nmod = 8

    def sb(self, name, shape, dt):
        return self.es.enter_context(self.nc.sbuf_tensor('sb_' + name, list(shape), dt))

    def din(self, name, shape, dt=F32):
        return self.nc.dram_tensor(name, list(shape), dt, kind="ExternalInput").ap()

    def dout(self, name, shape, dt=F32):
        return self.nc.dram_tensor(name, list(shape), dt, kind="ExternalOutput").ap()

    def dscr(self, name, shape, dt=F32):
        return self.nc.dram_tensor(name, list(shape), dt, kind="Internal").ap()

    def bank(self):
        b = self.nb % self.nmod
        self.nb += 1
        return b

    def dma(self, q, out, in_, reads, writes, key):
        self.S.add(q, lambda e, o=out, i=in_: e.dma_start(out=o, in_=i), reads, writes, dma_key=key)

    def act(self, out, in_, func, reads, writes, bias=None, scale=None, accum=None):
        kw = {}
        if bias is not None:
            kw['bias'] = bias
        if scale is not None:
            kw['scale'] = scale
        if accum is not None:
            kw['accum_out'] = accum
        self.S.add('act', lambda e, o=out, i=in_, f=func, kw=kw: e.activation(out=o, in_=i, func=f, **kw),
                   reads, writes)

    def tt(self, eng, out, a, b, op, reads, writes):
        self.S.add(eng, lambda e, o=out, a=a, b=b, op=op: e.tensor_tensor(out=o, in0=a, in1=b, op=op),
                   reads, writes)

    def ts(self, eng, out, a, s1, s2, op0, op1, reads, writes):
        if s2 is None:
            self.S.add(eng, lambda e, o=out, a=a, s1=s1, op0=op0: e.tensor_single_scalar(out=o, in_=a, scalar=s1, op=op0),
                       reads, writes)
        else:
            self.S.add(eng, lambda e, o=out, a=a, s1=s1, s2=s2, op0=op0, op1=op1:
                       e.tensor_scalar(out=o, in0=a, scalar1=s1, scalar2=s2, op0=op0, op1=op1), reads, writes)

    def stt(self, eng, out, a, s, b, op0, op1, reads, writes):
        self.S.add(eng, lambda e, o=out, a=a, s=s, b=b, op0=op0, op1=op1:
                   e.scalar_tensor_tensor(out=o, in0=a, scalar=s, in1=b, op0=op0, op1=op1), reads, writes)

    def copy(self, eng, out, in_, reads, writes):
        if eng == 'act':
            self.S.add('act', lambda e, o=out, i=in_: e.copy(out=o, in_=i), reads, writes)
        else:
            self.S.add(eng, lambda e, o=out, i=in_: e.tensor_copy(out=o, in_=i), reads, writes)

    def memset(self, eng, ap, val, writes):
        self.S.add(eng, lambda e, a=ap, v=val: e.memset(a, v), (), writes)

    def recip(self, out, in_, reads, writes):
        self.S.add('dve', lambda e, o=out, i=in_: e.reciprocal(out=o, in_=i), reads, writes)

    def mm(self, specs, reads, writes):
        def fn(e, specs=specs):
            r = None
            for sp_ in specs:
                o, l, rh, st, sp = sp_[:5]
                if len(sp_) > 5 and sp_[5]:
                    r = e.matmul(o, lhsT=l, rhs=rh, start=st, stop=sp, skip_group_check=True)
                else:
                    r = e.matmul(o, lhsT=l, rhs=rh, start=st, stop=sp)
            return r
        self.S.add('pe', fn, reads, writes)

    def tr(self, specs, ident, reads, writes):
        def fn(e, specs=specs, ident=ident):
            r = None
            for (o, i) in specs:
                r = e.transpose(out=o, in_=i, identity=ident)
            return r
        self.S.add('pe', fn, reads, writes)

    def scan(self, out, d0, d1, init, reads, writes):
        self.S.add('dve', lambda e, o=out, a=d0, b=d1, i=init:
                   e.tensor_tensor_scan(out=o, data0=a, data1=b, initial=i, op0=ALU.mult, op1=ALU.add),
                   reads, writes)

    def reduce_sum(self, out, in_, reads, writes):
        self.S.add('dve', lambda e, o=out, i=in_: e.reduce_sum(out=o, in_=i, axis=AX.X), reads, writes)


def build_program():
    nc = bass.Bass("TRN2", target_bir_lowering=False)
    with contextlib.ExitStack() as es:
        B = Builder(nc, es)
        S = B.S
        x_d = B.din("x", [T, D])
        cond_d = B.din("cond8", [P, 8])
        ck_d = B.din("ck", [L, 512, 128])
        cv_d = B.din("cv", [L, 512, 128])
        sgla_d = B.din("sgla", [L, 2, P, 2, 64])
        lruh0_d = B.din("lruh0", [P, L * 4])
        flags_d = B.din("flags", [P, 4])
        ident_d = B.din("ident", [P, P])
        tri_d = B.din("tri", [4, P, P])
        glam_d = B.din("glam", [2, P, P])
        attm_d = B.din("attm", [4, P, P])
        perm_d = B.din("perm", [P, P])
        rope_d = B.din("rope", [NT, P, 2, P])
        wmod_d = B.din("w_mod", [L, D, 6 * D])
        bmod_d = B.din("b_mod", [L, 1, 6 * D])
        n1_d = B.din("norm1", [L, 1, D])
        n2_d = B.din("norm2", [L, 1, D])
        fn_d = B.din("final_norm", [1, D])
        win_d = B.din("w_in_r", [L, D, WIN])
        wout_d = B.din("w_out", [L, D, D])
        w1_d = B.din("w_mlp1", [L, D, 4 * D])
        w2_d = B.din("w_mlp2", [L, 4 * D, D])
        gaug_d = B.din("gaug", [L, 33, 512])
        gnorm_d = B.din("gla_norm", [L, 1, 256])
        sink_d = B.din("attn_sink", [L, 1, 8])
        lvec_d = B.din("lruvec", [L, P, NVEC])
        wa_d = B.din("lru_wa", [L, 2, 4, 64, 64])
        wx_d = B.din("lru_wx", [L, 2, 4, 64, 64])

        y_d = B.dout("y", [T, D])
        kout_d = B.dout("kout", [L, T, 128])
        vout_d = B.dout("vout", [L, T, 128])
        gst_d = B.dout("gst", [NSEQ, L, 2, 4, 64, 64])
        lst_d = B.dout("lst", [NSEQ, L, 2, 256])

        xs_s = B.dscr("xs_s", [T, D])
        mod_s = B.dscr("mod_s", [1, 6 * D])
        lxg_s = B.dscr("lxg_s", [4, P, T])
        qkT_s = B.dscr("qkT_s", [NT, P, 1024], BF16)
        ke_s = B.dscr("ke_s", [NT, P, 512], BF16)
        v16_s = B.dscr("v16_s", [NT, P, 256], BF16)
        dec_s = B.dscr("dec_s", [NT, P, 4])
        sgr_s = B.dscr("sgr_s", [NT, P, 256])
        o_s = B.dscr("o_s", [2, NT, P, 256])
        mixT_s = B.dscr("mixT_s", [NT, P, 1024], BF16)

        SW = 2048
        ident16 = B.sb("ident16", [P, P], BF16)
        identf = B.sb("identf", [P, P], F32)
        tri = B.sb("tri", [P, 4, P], F32)
        glam16 = B.sb("glam16", [P, 2, P], BF16)
        attm16 = B.sb("attm16", [P, 4, P], BF16)
        perm = B.sb("perm", [P, P], F32)
        flags = B.sb("flags", [P, 4], F32)
        ones_col = B.sb("ones_col", [P, 1], F32)
        scond = B.sb("scond", [P, 8], F32)
        scond16 = B.sb("scond16", [P, 8], BF16)
        lruh0 = B.sb("lruh0", [P, L * 4], F32)
        M = [B.sb(f"mod{i}", [P, D], F32) for i in range(4)]
        wout16 = B.sb("wout16", [P, 8, D], BF16)
        wbig = B.sb("wbig", [P, 65536], BF16)
        stage = [B.sb("stage0", [P, SW], F32), B.sb("stage1", [P, 1024], F32)]
        esink = B.sb("esink", [P, 8], F32)
        lvec = B.sb("lvec", [P, NVEC], F32)
        lder = B.sb("lder", [P, 16], F32)
        xt = [B.sb(f"xt{i}", [P, D], F32) for i in range(2)]
        tmp32 = B.sb("tmp32", [P, D], F32)
        h16 = B.sb("h16", [P, D], BF16)
        hTs = [B.sb(f"hT{i}", [P, 8, P], BF16) for i in range(2)]
        hT = hTs[0]
        st4 = B.sb("st4", [P, 8], F32)
        u16 = [B.sb(f"u16_{i}", [P, 512], BF16) for i in range(2)]
        wbd32 = stage[1][:, 0:1024].rearrange("p (a b) -> p a b", a=8)

        off = [0]
        un = B.sb("un", [P, 4096], BF16)
        base = [wbig, 65536]

        def carve(nelem_bf16, dt, shape):
            a = base[0][:, off[0]:off[0] + nelem_bf16]
            off[0] += nelem_bf16
            assert off[0] <= base[1], off[0]
            if dt == F32:
                a = a.bitcast(F32)
            if len(shape) == 3:
                a = a.rearrange("p (a b) -> p a b", a=shape[1])
            elif len(shape) == 4:
                a = a.rearrange("p (a b c) -> p a b c", a=shape[1], b=shape[2])
            assert list(a.shape) == list(shape), (a.shape, shape)
            return a

        base[:] = [un, 4096]
        gaug_full = carve(1024, F32, [P, 512])
        gaug = gaug_full[0:33, :]
        gnb = carve(512, F32, [P, 256])
        wbd16 = carve(1024, BF16, [P, 8, P])
        ckT = carve(512, BF16, [P, 512])
        cva = carve(520, BF16, [P, 4, 2, 65])
        off[0] = 0
        uT = carve(1024, BF16, [P, 2, 4, P])
        rl32 = carve(1024, F32, [P, 512])
        mixTt = [carve(1024, BF16, [P, 8, P]) for _ in range(2)]
        off[0] = 0
        base[:] = [wbig, 65536]
        win16 = wbig[:, 0:8 * WIN].rearrange("p (k n) -> p k n", k=8)
        w1_16 = wbig[:, 0:32768].rearrange("p (k n) -> p k n", k=8)
        w2_16 = wbig[:, 32768:65536].rearrange("p (k n) -> p k n", k=32)
        wmod16 = wbig[:, 0:8 * 6144].rearrange("p (k n) -> p k n", k=8)
        off[0] = 8 * WIN
        KT = carve(T, BF16, [P, T])
        VAflat = wbig[:, off[0]:off[0] + NT * 130]
        off[0] += NT * 130
        VA = VAflat.rearrange("p (t g c) -> p t g c", t=NT, g=2)
        QT = carve(3 * 4 * P, BF16, [P, 3, 4, P])
        qk32 = carve(2 * 5 * P, F32, [P, 5, P])
        rt1 = carve(2 * 5 * P, F32, [P, 5, P])
        rt2 = carve(2 * 5 * P, F32, [P, 5, P])
        ropet = [carve(2 * 2 * P, F32, [P, 2, P]) for _ in range(2)]
        kv32 = [carve(2 * 256, F32, [P, 256]) for _ in range(2)]
        lxg = [carve(2 * 4 * P, F32, [P, 4, P]) for _ in range(2)]
        zT33 = carve(2 * P, F32, [P, P])
        sp32 = carve(2 * 512, F32, [P, 512])
        EqEe = [carve(2 * 512, F32, [P, 512]) for _ in range(2)]
        Ek = [carve(2 * 256, F32, [P, 256]) for _ in range(2)]
        qin16 = carve(512, BF16, [P, 512])
        kin16 = carve(512, BF16, [P, 512])
        ke16 = [carve(512, BF16, [P, 512]) for _ in range(2)]
        v16 = [carve(256, BF16, [P, 256]) for _ in range(2)]
        sgr32 = [carve(2 * 256, F32, [P, 256]) for _ in range(2)]
        qkT16 = [carve(1024, BF16, [P, 8, P]) for _ in range(2)]
        dec32 = [carve(2 * 4, F32, [P, 4]) for _ in range(2)]
        PT = [carve(512, BF16, [P, 512]) for _ in range(3)]
        den = carve(2 * 8, F32, [P, 8])
        rec = carve(2 * 8, F32, [P, 8])
        o16 = carve(512, BF16, [P, 512])
        oT16 = [carve(512, BF16, [P, 4, P]) for _ in range(2)]
        gqk32 = carve(2 * 512, F32, [P, 512])
        offA = off[0]
        off[0] = 0
        gq_ld = [[carve(512, BF16, [P, 4, P]) for _ in range(2)] for _ in range(2)]
        gke_ld = [[carve(256, BF16, [P, 256]) for _ in range(2)] for _ in range(2)]
        gv_ld = [[carve(256, BF16, [P, 256]) for _ in range(2)] for _ in range(2)]
        gdec_ld = [[carve(2 * 2, F32, [P, 2]) for _ in range(2)] for _ in range(2)]
        S32 = [carve(2 * 128, F32, [P, 2, 64]) for _ in range(2)]
        S16 = [carve(128, BF16, [P, 2, 64]) for _ in range(2)]
        AT16 = [carve(512, BF16, [P, 512]) for _ in range(2)]
        o32 = [[carve(2 * 256, F32, [P, 256]) for _ in range(2)] for _ in range(2)]
        FS = [dict(fo=[carve(2 * 256, F32, [P, 256]) for _ in range(3)], fsq=carve(2 * 256, F32, [P, 256]),
                   fy=carve(2 * 256, F32, [P, 256]), fy16=carve(256, BF16, [P, 256]), fyT=carve(256, BF16, [P, 2, P]),
                   st=carve(2 * 4, F32, [P, 4])) for _ in range(2)]
        CH = 512
        NCH = T // CH
        W2PRE = 24
        LT = []
        for _ct in range(2):
            LT.append(dict(
                hf=carve(2 * T, F32, [P, T]),
                lxc=[carve(2 * (CH + 4), F32, [P, CH + 4]) for _ in range(2)],
                xc=carve(2 * CH, F32, [P, CH]), xc16=carve(CH, BF16, [P, CH]),
                rg=carve(2 * CH, F32, [P, CH]), ig=carve(2 * CH, F32, [P, CH]),
                av=carve(2 * CH, F32, [P, CH]), bt=carve(2 * CH, F32, [P, CH]),
                hb=[carve(2 * CH, F32, [P, CH]) for _ in range(2)],
                lgc=[carve(2 * CH, F32, [P, CH]) for _ in range(2)],
                y16=[carve(CH, BF16, [P, CH]) for _ in range(2)],
                stc=carve(2 * 32, F32, [P, 2, 16]),
                sto=[carve(2 * P, F32, [P, P]) for _ in range(2)]))
        assert off[0] <= 32768 + W2PRE * 1024, off[0]

        ps = [es.enter_context(nc.psum_tensor(f"ps{i}", [P, 512], F32)) for i in range(8)]
        ps16 = [p_[:].bitcast(BF16) for p_ in ps]

        def PSK(b):
            return f"ps{b}"

        def h4(ap, n=4):
            return ap.rearrange("p (h c) -> p h c", h=n)

        CK = 'ld_c'
        B.dma('sp', identf[:], ident_d[:, :], [], ['consts'], CK)
        B.dma('sp', tri[:], tri_d.rearrange("k p n -> p k n"), [], ['consts'], CK)
        B.dma('sp', perm[:], perm_d[:, :], [], ['consts'], CK)
        B.dma('sp', flags[:], flags_d[:, :], [], ['consts'], CK)
        B.dma('sp', lruh0[:], lruh0_d[:, :], [], ['consts'], CK)
        B.dma('sp', scond[:], cond_d[:, :], [], ['consts'], CK)
        B.dma('sp', stage[0][:, 0:256].rearrange("p (k n) -> p k n", k=2), glam_d.rearrange("k p n -> p k n"), [], ['consts'], CK)
        B.dma('sp', stage[0][:, 256:768].rearrange("p (k n) -> p k n", k=4), attm_d.rearrange("k p n -> p k n"), [], ['consts'], CK)
        B.copy('dve', ident16[:], identf[:], ['consts'], ['ident16'])
        B.copy('dve', glam16[:].rearrange("p a b -> p (a b)"), stage[0][:, 0:256], ['consts'], ['glam16', 'stage0'])
        B.copy('dve', attm16[:].rearrange("p a b -> p (a b)"), stage[0][:, 256:768], ['consts'], ['attm16', 'stage0'])
        B.act(scond[:], scond[:], AF.Silu, ['consts'], ['scond'])
        B.copy('dve', scond16[:], scond[:], ['scond'], ['scond16'])
        B.memset('dve', ones_col[:], 1.0, ['ones_col'])
        ctxb = flags[:, 0:1]
        sf = flags[:, 2:3]
        npf = flags[:, 3:4]
        FLG = 'consts'

        def load_cast_weight(dst_fn, src_fn, nk, width, rkey):
            n = [0]
            for kc in range(nk):
                for c0 in range(0, width, SW):
                    c1 = min(width, c0 + SW)
                    s = n[0] % 2
                    n[0] += 1
                    B.dma('sp', stage[s][:, 0:c1 - c0], src_fn(kc)[:, c0:c1], [], [f'stage{s}'], f'ld_st{s}')
                    B.copy('pool' if s == 0 else 'dve', dst_fn(kc)[:, c0:c1], stage[s][:, 0:c1 - c0], [f'stage{s}'], [rkey])

        def load_mod(dst, chunk, key, dkey):
            B.dma('sp', dst[:], mod_s[0:1, chunk * D:(chunk + 1) * D].partition_broadcast(P), ['mod_s'], [key], dkey)

        for l in range(L):
            x_src = x_d if l == 0 else xs_s
            B.nmod = 8
            S.barrier()
            for j in range(4):
                B.dma('pool', wmod16[:, 2 * j:2 * j + 2, :],
                      wmod_d[l, 2 * j * P:(2 * j + 2) * P, :].rearrange("(k p) n -> p k n", p=P), [], [f'wmod{j}'], f'ld_wm{j}')
            B.dma('pool', wout16[:], wout_d[l].rearrange("(k p) n -> p k n", p=P), [], ['wout16'], 'ld_wout')
            for part in range(3):
                row = stage[0][0:1, 0:2048]
                rk_ = 'stage0'
                B.dma('sp', row, bmod_d[l][0:1, part * 2048:(part + 1) * 2048], [], [rk_], 'ld_st0')
                banks = [B.bank() for _ in range(4)]
                for kc in range(8):
                    B.mm([(ps[banks[j]][0:1, :], scond16[:, kc:kc + 1], wmod16[:, kc, (part * 4 + j) * 512:(part * 4 + j + 1) * 512],
                           kc == 0, kc == 7) for j in range(4)], ['scond16', f'wmod{kc // 2}'], [PSK(b) for b in banks])
                for j in range(4):
                    B.tt('dve', row[:, j * 512:(j + 1) * 512], ps[banks[j]][0:1, :], row[:, j * 512:(j + 1) * 512], ALU.add,
                         [PSK(banks[j]), rk_], [rk_])
                B.dma('pool', mod_s[0:1, part * 2048:(part + 1) * 2048], row, [rk_], ['mod_s'], 'st_mod')
            S.ckpt(1)
            S.barrier()
            for j in range(2):
                B.dma('pool', win16[:, 4 * j:4 * j + 4, :],
                      win_d[l, 4 * j * P:(4 * j + 4) * P, :].rearrange("(k p) n -> p k n", p=P), [], ['win16'], f'ld_win{j}')
            load_mod(M[0], 1, 'M0', 'ld_m0')
            B.dma('sp', tmp32[:], n1_d[l].partition_broadcast(P), [], ['tmp32'], 'ld_tmp')
            B.stt('dve', M[0][:], M[0][:], 1.0, tmp32[:], ALU.add, ALU.mult, ['M0', 'tmp32'], ['M0'])
            load_mod(M[1], 0, 'M1', 'ld_m1')
            B.dma('sp', gaug, gaug_d[l], [], ['gaug'], 'ld_s0')
            B.dma('sp', gnb[:], gnorm_d[l].partition_broadcast(P), [], ['gnb'], 'ld_s1')
            B.dma('sp', esink[:], sink_d[l].partition_broadcast(P), [], ['esink'], 'ld_s2')
            B.act(esink[:], esink[:], AF.Exp, ['esink'], ['esink'])
            B.dma('sp', lvec[:], lvec_d[l], [], ['lvec'], 'ld_s3')
            for ct in range(2):
                for dr in range(2):
                    lamc = lvec[:, ct * 11 + 5 + dr * 3 + 2: ct * 11 + 5 + dr * 3 + 3]
                    dc = lder[:, dr * 2 + ct: dr * 2 + ct + 1]
                    B.act(dc, lamc, AF.Exp, ['lvec'], ['lder'], scale=-1.0)
                    B.act(dc, dc, AF.Ln, ['lder'], ['lder'], bias=1.0)
                    B.ts('dve', dc, dc, -8.0, None, ALU.mult, None, ['lder'], ['lder'])
                B.ts('dve', lder[:, 4 + ct * 4: 8 + ct * 4], lvec[:, ct * 11: ct * 11 + 4], npf, None, ALU.mult, None,
                     ['lvec', FLG], ['lder'])
            B.memset('dve', stage[1][:, 0:1024], 0.0, ['stage1'])
            for g_, wsrc in enumerate((wa_d, wx_d)):
                for dr in range(2):
                    for n in range(4):
                        ct, hb_ = n // 2, n % 2
                        idx = g_ * 4 + dr * 2 + ct
                        B.dma('sp', wbd32[hb_ * 64:(hb_ + 1) * 64, idx, hb_ * 64:(hb_ + 1) * 64], wsrc[l, dr, n],
                              ['stage1'], ['wbd32'], 'ld_wbd')
            B.copy('dve', wbd16[:].rearrange("p a b -> p (a b)"), stage[1][:, 0:1024], ['wbd32', 'stage1'], ['wbd16', 'stage1'])
            B.memset('dve', cva[:].rearrange("p a b c -> p (a b c)"), 1.0, ['cva'])
            for c in range(4):
                B.dma('sp', stage[0][:, 0:128], ck_d[l, c * P:(c + 1) * P, :], [], ['stage0'], 'ld_st0')
                B.copy('dve', h16[:, 0:128], stage[0][:, 0:128], ['stage0'], ['h16'])
                b = B.bank()
                B.tr([(ps16[b][:, 0:128], h16[:, 0:128])], ident16[:], ['h16', 'ident16'], [PSK(b)])
                B.copy('act', ckT[:, c * P:(c + 1) * P], ps16[b][:, 0:128], [PSK(b)], ['ckT'])
                B.dma('sp', stage[1][:, 0:128], cv_d[l, c * P:(c + 1) * P, :], [], ['stage1'], 'ld_st1')
                B.copy('dve', cva[:, c, :, 0:64], stage[1][:, 0:128].rearrange("p (g d) -> p g d", g=2), ['stage1'], ['cva'])
            B.memset('pool', VAflat, 1.0, ['VAall'])
            B.memset('pool', zT33[32:33, :], 1.0, ['zT33'])

            S.ckpt(2)
            def norm_mod(xs_, xk, m_a, m_b, ka, kb, b, hT_=None, hk='hT'):
                hT_ = hT if hT_ is None else hT_
                B.act(h16[:], xs_[:], AF.Square, [xk], ['h16', 'st0'], accum=st4[:, 0:1])
                B.act(st4[:, 1:2], st4[:, 0:1], AF.Ln, ['st0'], ['st1'], bias=EPS, scale=1.0 / D)
                B.act(st4[:, 2:3], st4[:, 1:2], AF.Exp, ['st1'], ['st2'], scale=-0.5)
                B.stt('dve', tmp32[:], xs_[:], st4[:, 2:3], m_a[:], ALU.mult, ALU.mult, [xk, 'st2', ka], ['tmp32'])
                B.tt('dve', h16[:], tmp32[:], m_b[:], ALU.add, ['tmp32', kb], ['h16'])
                B.tr([(ps16[b][:, kc * P:(kc + 1) * P], h16[:, kc * P:(kc + 1) * P]) for kc in range(8)], ident16[:],
                     ['h16', 'ident16'], [PSK(b)])
                B.copy('act', hT_[:].rearrange("p a b -> p (a b)"), ps16[b][:, :], [PSK(b)], [hk])

            def norm_mod_g(xs_, xk, m_a, m_b, ka, kb, b, hT_, hk):
                B.act(h16[:], xs_[:], AF.Square, [xk], ['h16', 'st0'], accum=st4[:, 0:1])
                yield
                B.act(st4[:, 1:2], st4[:, 0:1], AF.Ln, ['st0'], ['st1'], bias=EPS, scale=1.0 / D)
                yield
                B.act(st4[:, 2:3], st4[:, 1:2], AF.Exp, ['st1'], ['st2'], scale=-0.5)
                yield
                B.stt('dve', tmp32[:], xs_[:], st4[:, 2:3], m_a[:], ALU.mult, ALU.mult, [xk, 'st2', ka], ['tmp32'])
                yield
                B.tt('dve', h16[:], tmp32[:], m_b[:], ALU.add, ['tmp32', kb], ['h16'])
                yield
                B.tr([(ps16[b][:, kc * P:(kc + 1) * P], h16[:, kc * P:(kc + 1) * P]) for kc in range(8)], ident16[:],
                     ['h16', 'ident16'], [PSK(b)])
                yield
                B.copy('act', hT_[:].rearrange("p a b -> p (a b)"), ps16[b][:, :], [PSK(b)], [hk])
                yield

            def attention(i):
                slot = i % 3
                kbs = [('c', c) for c in range(4)]
                if i > 0:
                    kbs.append(('p', i - 1))
                kbs.append(('o', i))
                if i < NT - 1:
                    kbs.append(('n', i + 1))
                ob = [6, 7]
                seq = [(g, n_, kind, j) for g in range(2) for n_, (kind, j) in enumerate(kbs)]

                def s_mm(idx):
                    g, n_, kind, j = seq[idx]
                    b = 4 + idx % 2
                    if kind == 'c':
                        ksrc, rk = ckT[64 * g:64 * g + 64, j * P:(j + 1) * P], 'ckT'
                    else:
                        ksrc, rk = KT[64 * g:64 * g + 64, j * P:(j + 1) * P], f'KT{j}'
                    B.mm([(ps[b][:, :], ksrc, QT[64 * g:64 * g + 64, slot, :, :].rearrange("p a b -> p (a b)"), True, True)],
                         [rk, f'QT{slot}'], [PSK(b)])

                s_mm(0)
                for idx, (g, n_, kind, j) in enumerate(seq):
                    if idx + 1 < len(seq):
                        s_mm(idx + 1)
                    b = 4 + idx % 2
                    pt = PT[idx % 3]
                    ptk = f'PT{idx % 3}'
                    if kind == 'c':
                        vsrc, rv = cva[:, j, g, :], 'cva'
                        B.act(pt[:], ps[b][:, :], AF.Exp, [PSK(b), FLG], [ptk], bias=ctxb, scale=0.125)
                    else:
                        vsrc, rv = VA[:, j, g, :], f'VA{j}'
                        B.act(pt[:], ps[b][:, :], AF.Exp, [PSK(b)], [ptk], scale=0.125)
                    if kind in ('p', 'n'):
                        mi = (0 if kind == 'p' else 2) + (i % 2)
                        B.tt('dve', h4(pt[:]), h4(pt[:]), attm16[:, mi, :].unsqueeze(1).to_broadcast([P, 4, P]), ALU.mult,
                             [ptk, 'attm16'], [ptk])
                    B.mm([(ps[ob[g]][:, a * 65:(a + 1) * 65], pt[:, a * P:(a + 1) * P], vsrc,
                           n_ == 0 and a == 0, n_ == len(kbs) - 1 and a == 3, True) for a in range(4)],
                         [ptk, rv], [PSK(ob[g])])
                    yield
                for g in range(2):
                    o3 = h4(ps[ob[g]][:, 0:260])
                    B.tt('dve', den[:, 4 * g:4 * g + 4], o3[:, :, 64], esink[:, 4 * g:4 * g + 4], ALU.add,
                         [PSK(ob[g]), 'esink'], ['den'])
                B.recip(rec[:], den[:], ['den'], ['rec'])
                for g in range(2):
                    o3 = h4(ps[ob[g]][:, 0:260])
                    B.tt('dve', h4(o16[:, 256 * g:256 * (g + 1)]), o3[:, :, 0:64],
                         rec[:, 4 * g:4 * g + 4].unsqueeze(2).to_broadcast([P, 4, 64]), ALU.mult,
                         [PSK(ob[g]), 'rec'], ['o16'])
                yield
                b = 4
                B.tr([(ps16[b][:, c * P:(c + 1) * P], o16[:, c * P:(c + 1) * P]) for c in range(4)], ident16[:],
                     ['o16', 'ident16'], [PSK(b)])
                so = i % 2
                B.copy('act', oT16[so][:].rearrange("p a b -> p (a b)"), ps16[b][:, 0:512], [PSK(b)], [f'oT16_{so}'])
                B.dma('pool', mixT_s[i, :, 0:512], oT16[so][:].rearrange("p a b -> p (a b)"),
                      [f'oT16_{so}'], [f'mixA{i}'], f'st_oT{so}')
                yield

            def load_x_a(t):
                s2 = t % 2
                B.dma('sp', xt[s2][:], x_src[t * P:(t + 1) * P, :], [f'xs{t}'] if l > 0 else [], [f'xt{s2}'], f'ld_xt{s2}')

            def load_rope(t):
                s2 = t % 2
                B.dma('sp', ropet[s2][:], rope_d[t], [], [f'rope{s2}'], f'ld_rope{s2}')

            def proj_q(t):
                s2 = t % 2
                hT_, hk = hTs[s2], f'hT{s2}'
                bq = 0
                if t + 2 < NT:
                    load_x_a(t + 2)
                if t + 1 < NT:
                    load_rope(t + 1)
                for j in range(4):
                    B.mm([(ps[bq][:, j * P:(j + 1) * P], win16[:, kc, QC + j * P:QC + (j + 1) * P], hT_[:, kc, :], kc == 0, kc == 7)
                          for kc in range(8)], ['win16', hk], [PSK(bq)])
                    if j == 1:
                        yield
                yield
                B.copy('act', qk32[:, 0:4, :].rearrange("p a b -> p (a b)"), ps[bq][:, :], [PSK(bq)], ['q32'])
                yield
                B.mm([(ps[bq][:, j * P:(j + 1) * P], perm[:], qk32[:, j, :], True, True) for j in range(4)],
                     ['consts', 'q32'], [PSK(bq)])
                yield
                B.tt('dve', rt1[:, 0:4, :], qk32[:, 0:4, :], ropet[s2][:, 0, :].unsqueeze(1).to_broadcast([P, 4, P]), ALU.mult,
                     ['q32', f'rope{s2}'], ['rt1q'])
                yield
                B.tt('dve', rt2[:, 0:4, :], h4(ps[bq][:, :]),
                     ropet[s2][:, 1, :].unsqueeze(1).to_broadcast([P, 4, P]), ALU.mult, [PSK(bq), f'rope{s2}'], ['rt2q'])
                yield
                B.tt('dve', QT[:, t % 3, :, :], rt1[:, 0:4, :], rt2[:, 0:4, :], ALU.add, ['rt1q', 'rt2q'], [f'QT{t % 3}'])
                yield

            def proj_x(t):
                s2 = t % 2
                hT_, hk = hTs[s2], f'hT{s2}'
                bk = 1
                B.mm([(ps[bk][:, 0:P], win16[:, kc, KF:KF + P], hT_[:, kc, :], kc == 0, kc == 7) for kc in range(8)],
                     ['win16', hk], [PSK(bk)])
                yield
                B.copy('act', qk32[:, 4, :], ps[bk][:, 0:P], [PSK(bk)], ['k32'])
                yield
                B.mm([(ps[bk][:, P:2 * P], perm[:], qk32[:, 4, :], True, True)]
                     + [(ps[bk][:, 2 * P:4 * P], hT_[:, kc, :], win16[:, kc, KVO:KVO + 256], kc == 0, kc == 7) for kc in range(8)],
                     ['win16', hk, 'consts', 'k32'], [PSK(bk)])
                yield
                B.tt('dve', rt1[:, 4, :], qk32[:, 4, :], ropet[s2][:, 0, :], ALU.mult, ['k32', f'rope{s2}'], ['rt1k'])
                yield
                B.tt('dve', rt2[:, 4, :], ps[bk][:, P:2 * P], ropet[s2][:, 1, :], ALU.mult, [PSK(bk), f'rope{s2}'], ['rt2k'])
                yield
                B.copy('act', kv32[s2][:], ps[bk][:, 2 * P:4 * P], [PSK(bk)], [f'kv32_{s2}'])
                B.copy('dve', VA[:, t, :, 0:64], ps[bk][:, 3 * P:4 * P].rearrange("p (g d) -> p g d", g=2),
                       [PSK(bk), 'VAall'], [f'VA{t}'])
                yield
                B.tt('dve', KT[:, t * P:(t + 1) * P], rt1[:, 4, :], rt2[:, 4, :], ALU.add, ['rt1k', 'rt2k'], [f'KT{t}'])
                B.dma('pool', kout_d[l, t * P:(t + 1) * P, :], kv32[s2][:, 0:128], [f'kv32_{s2}'], [], f'st_kv{s2}')
                B.dma('pool', vout_d[l, t * P:(t + 1) * P, :], kv32[s2][:, 128:256], [f'kv32_{s2}'], [], f'st_kv{s2}')
                yield
                if t + 1 < NT:
                    s3 = (t + 1) % 2
                    yield from norm_mod_g(xt[s3], f'xt{s3}', M[0], M[1], 'M0', 'M1', 1, hTs[s3], f'hT{s3}')

            def proj_y(t):
                s2 = t % 2
                hT_, hk = hTs[s2], f'hT{s2}'
                bqk, bz = 2, 3
                B.mm([(ps[bz][0:32, 0:P], win16[:, kc, GZO:GZO + 32], hT_[:, kc, :], kc == 0, kc == 7) for kc in range(8)],
                     ['win16', hk], [PSK(bz)])
                yield
                B.copy('act', zT33[0:32, :], ps[bz][0:32, 0:P], [PSK(bz)], ['zT33'])
                yield
                B.mm([(ps[bz][:, :], zT33[0:33, :], gaug, True, True)], ['zT33', 'gaug'], [PSK(bz)])
                assert S.last_w.get(f'lxg{s2}') is not None
                B.mm([(ps[bqk][:, :], hT_[:, kc, :], win16[:, kc, GQKO:GQKO + 512], kc == 0, kc == 7) for kc in range(8)],
                     ['win16', hk], [PSK(bqk)])
                yield
                B.act(sp32[:], ps[bz][:, :], AF.Exp, [PSK(bz)], ['sp32'], scale=-1.0)
                B.copy('dve', gqk32[:], ps[bqk][:, :], [PSK(bqk)], ['gqk32'])
                yield
                B.act(sp32[:], sp32[:], AF.Ln, ['sp32'], ['sp32'], bias=1.0)
                yield
                yield from proj_dir(t, 0, bz)

            def proj_dir(t, dr, b):
                s2 = t % 2
                hT_, hk = hTs[s2], f'hT{s2}'
                B.mm([(ps[b][:, 0:256], tri[:, 2 * dr, :], sp32[:, 256 * dr:256 * (dr + 1)], True, True),
                      (ps[b][:, 256:512], tri[:, 2 * dr + 1, :], sp32[:, 256 * dr:256 * (dr + 1)], True, True)],
                     ['consts', 'sp32'], [PSK(b)])
                yield
                B.act(EqEe[dr][:], ps[b][:, :], AF.Exp, [PSK(b)], [f'EqEe{dr}'], scale=-1.0 / 16)
                yield
                B.act(Ek[dr][:], ps[b][:, 0:256], AF.Exp, [PSK(b)], [f'Ek{dr}'], scale=1.0 / 16)
                yield
                qd, kd = qin16[:, 256 * dr:256 * (dr + 1)], kin16[:, 256 * dr:256 * (dr + 1)]
                B.stt('dve', qd, gqk32[:, 0:256], 0.125, EqEe[dr][:, 0:256], ALU.mult, ALU.mult, ['gqk32', f'EqEe{dr}'], [f'qin16_{dr}'])
                yield
                B.tt('dve', kd, gqk32[:, 256:512], Ek[dr][:], ALU.mult, ['gqk32', f'Ek{dr}'], [f'kin16_{dr}'])
                yield
                B.tt('dve', ke16[s2][:, 256 * dr:256 * (dr + 1)], gqk32[:, 256:512], EqEe[dr][:, 256:512], ALU.mult,
                     ['gqk32', f'EqEe{dr}'], [f'ke16_{s2}_{dr}'])
                B.dma('pool', ke_s[t, :, 256 * dr:256 * (dr + 1)], ke16[s2][:, 256 * dr:256 * (dr + 1)],
                      [f'ke16_{s2}_{dr}'], [f'ke_s{t}_{dr}'], f'st_ke{s2}{dr}')
                specs = []
                for hp in range(2):
                    specs.append((ps16[b][:, hp * P:(hp + 1) * P], qin16[:, 256 * dr + hp * P:256 * dr + (hp + 1) * P]))
                    specs.append((ps16[b][:, (2 + hp) * P:(3 + hp) * P], kin16[:, 256 * dr + hp * P:256 * dr + (hp + 1) * P]))
                B.tr(specs, ident16[:], [f'qin16_{dr}', f'kin16_{dr}', 'ident16'], [PSK(b)])
                yield
                B.copy('act', qkT16[s2][:, 4 * dr:4 * dr + 4, :].rearrange("p a b -> p (a b)"), ps16[b][:, 0:512], [PSK(b)], [f'qkT16_{s2}_{dr}'])
                B.dma('pool', qkT_s[t, :, 512 * dr:512 * (dr + 1)], qkT16[s2][:, 4 * dr:4 * dr + 4, :].rearrange("p a b -> p (a b)"),
                      [f'qkT16_{s2}_{dr}'], [f'qkT_s{t}_{dr}'], f'st_qkT{s2}{dr}')
                yield
                if dr == 0:
                    B.mm([(ps[b][:, d2 * 2 + hp:d2 * 2 + hp + 1], sp32[:, 256 * d2 + hp * P:256 * d2 + (hp + 1) * P], ones_col[:, :], True, True)
                          for d2 in range(2) for hp in range(2)], ['sp32', 'ones_col'], [PSK(b)])
                    yield
                    B.act(dec32[s2][:], ps[b][:, 0:4], AF.Exp, [PSK(b)], [f'dec{s2}'], scale=-1.0 / 16)
                    B.dma('pool', dec_s[t], dec32[s2][:], [f'dec{s2}'], [f'dec_s{t}'], f'st_dec{s2}')
                    yield
                else:
                    B.mm([(ps[b][:, :], hT_[:, kc, :], win16[:, kc, GVRO:GVRO + 512], kc == 0, kc == 7) for kc in range(8)],
                         ['win16', hk], [PSK(b)])
                    yield
                    B.copy('act', v16[s2][:], ps[b][:, 0:256], [PSK(b)], [f'v16_{s2}'])
                    B.dma('pool', v16_s[t], v16[s2][:], [f'v16_{s2}'], [f'v16_s{t}'], f'st_v16{s2}')
                    yield
                    B.act(sgr32[s2][:], ps[b][:, 256:512], AF.Exp, [PSK(b)], [f'sgr{s2}'], scale=-1.0)
                    yield
                    B.ts('dve', sgr32[s2][:], sgr32[s2][:], 1.0, None, ALU.add, None, [f'sgr{s2}'], [f'sgr{s2}'])
                    yield
                    B.recip(sgr32[s2][:], sgr32[s2][:], [f'sgr{s2}'], [f'sgr{s2}'])
                    yield
                    B.tt('dve', sgr32[s2][:], ps[b][:, 256:512], sgr32[s2][:], ALU.mult, [PSK(b), f'sgr{s2}'], [f'sgr{s2}'])
                    yield
                    B.tt('pool', sgr32[s2][:], sgr32[s2][:], gnb[:], ALU.mult, [f'sgr{s2}', 'gnb'], [f'sgr{s2}'])
                    B.dma('pool', sgr_s[t], sgr32[s2][:], [f'sgr{s2}'], [f'sgr_s{t}'], f'st_sgr{s2}')
                    yield

            def proj_y1(t):
                s2 = t % 2
                hT_, hk = hTs[s2], f'hT{s2}'
                for j, co in enumerate((LXO, LXO + P, LGO, LGO + P)):
                    B.mm([(ps[2][:, j * P:(j + 1) * P], win16[:, kc, co:co + P], hT_[:, kc, :], kc == 0, kc == 7)
                          for kc in range(8)], ['win16', hk], [PSK(2)])
                yield
                B.copy('dve', lxg[s2][:].rearrange("p a b -> p (a b)"), ps[2][:, :], [PSK(2)], [f'lxg{s2}'])
                B.dma('pool', lxg_s[:, :, t * P:(t + 1) * P].rearrange("c p n -> p c n"), lxg[s2][:],
                      [f'lxg{s2}'], ['lxg_s'], f'st_lxg{s2}')
                yield
                for _ in range(3):
                    yield
                assert S.last_w.get('gqk32') is not None
                yield from proj_dir(t, 1, 2)

            def interleave(gens, weights=None):
                gens = list(gens)
                weights = weights or [1] * len(gens)
                alive = [True] * len(gens)
                while any(alive):
                    for gi, g_ in enumerate(gens):
                        for _ in range(weights[gi]):
                            if alive[gi]:
                                try:
                                    next(g_)
                                except StopIteration:
                                    alive[gi] = False

            load_x_a(0)
            load_x_a(1)
            load_rope(0)
            norm_mod(xt[0], 'xt0', M[0], M[1], 'M0', 'M1', 0, hTs[0], 'hT0')
            for t in range(NT + 2):
                gs = []
                if t < NT:
                    gs.append(proj_y(t))
                    gs.append(proj_y1(t))
                    gs.append(proj_x(t))
                    gs.append(proj_q(t))
                if t >= 2:
                    gs.append(attention(t - 2))
                interleave(gs, ([2, 2, 1, 1] + [1] * (len(gs) - 4)) if t < NT else None)
                if t == 0:
                    S.ckpt(3)
            S.ckpt(5)

            S.barrier()
            for j in range(2):
                mid_ = W2PRE + (32 - W2PRE) // 2
                k0, k1 = (W2PRE, mid_) if j == 0 else (mid_, 32)
                B.dma('pool', w2_16[:, k0:k1, :], w2_d[l, k0 * P:k1 * P, :].rearrange("(k p) n -> p k n", p=P), [], ['w2'], f'ld_w2b{j}')
            for dr in range(2):
                B.dma('sp', S32[dr][:], sgla_d[l, dr], [], [f'S32_{dr}'], f'ld_S{dr}')
                B.copy('dve', S16[dr][:].rearrange("p a b -> p (a b)"), S32[dr][:].rearrange("p a b -> p (a b)"),
                       [f'S32_{dr}'], [f'S16_{dr}'])

            def gla_load(dr, t, slot):
                k_ = f'ld_g{dr}{slot}'
                B.dma('sp', gq_ld[dr][slot][:].rearrange("p a b -> p (a b)"), qkT_s[t, :, dr * 512:(dr + 1) * 512],
                      [f'qkT_s{t}_{dr}'], [f'gq{dr}{slot}'], k_)
                B.dma('sp', gke_ld[dr][slot][:], ke_s[t, :, dr * 256:(dr + 1) * 256], [f'ke_s{t}_{dr}'], [f'gke{dr}{slot}'], k_)
                B.dma('sp', gv_ld[dr][slot][:], v16_s[t], [f'v16_s{t}'], [f'gv{dr}{slot}'], k_)
                B.dma('sp', gdec_ld[dr][slot][:], dec_s[t, :, dr * 2:(dr + 1) * 2], [f'dec_s{t}'], [f'gdec{dr}{slot}'], k_)

            def gla_final(t, b, si):
                assert S.last_w.get(f'o_s0_{t}', -1) >= b_start and S.last_w.get(f'o_s1_{t}', -1) >= b_start, t
                F_ = FS[si]
                fo, fsq, fy, fy16, fyT, st_ = F_['fo'], F_['fsq'], F_['fy'], F_['fy16'], F_['fyT'], F_['st']
                n_ = lambda k: f'{k}_f{si}'
                k_ = f'ld_fo{si}'
                B.dma('sp', fo[0][:], o_s[0, t], [f'o_s0_{t}'], [n_('fo0')], k_)
                B.dma('sp', fo[1][:], o_s[1, t], [f'o_s1_{t}'], [n_('fo1')], k_)
                B.dma('sp', fo[2][:], sgr_s[t], [f'sgr_s{t}'], [n_('fo2')], k_)
                fr = [n_('fo0'), n_('fo1'), n_('fo2')]
                yield
                B.tt('dve', fy[:], fo[0][:], fo[1][:], ALU.add, fr, [n_('fy')])
                yield
                B.act(fsq[:], fy[:], AF.Square, [n_('fy')], [n_('fsq')])
                yield
                B.reduce_sum(st_[:, 0:4], h4(fsq[:]), [n_('fsq')], [n_('st')])
                yield
                B.act(st_[:, 0:4], st_[:, 0:4], AF.Ln, [n_('st')], [n_('st')], bias=EPS, scale=1.0 / 64)
                yield
                B.act(st_[:, 0:4], st_[:, 0:4], AF.Exp, [n_('st')], [n_('st')], scale=-0.5)
                yield
                B.tt('dve', h4(fy[:]), h4(fy[:]), st_[:, 0:4].unsqueeze(2).to_broadcast([P, 4, 64]), ALU.mult, [n_('fy'), n_('st')], [n_('fy')])
                yield
                B.tt('dve', fy16[:], fy[:], fo[2][:], ALU.mult, [n_('fy')] + fr, [n_('fy16')])
                yield
                B.tr([(ps16[b][:, c * P:(c + 1) * P], fy16[:, c * P:(c + 1) * P]) for c in range(2)], ident16[:],
                     [n_('fy16'), 'ident16'], [PSK(b)])
                yield
                B.copy('act', fyT[:].rearrange("p a b -> p (a b)"), ps16[b][:, 0:256], [PSK(b)], [n_('fyT')])
                B.dma('pool', mixT_s[t, :, 512:768], fyT[:].rearrange("p a b -> p (a b)"), [n_('fyT')], [f'mixG{t}'], f'st_fyT{si}')
                yield

            fin_done = set()
            b_start = len(S.ins)

            def fin_pop():
                for t_ in range(NT):
                    if (t_ not in fin_done and S.last_w.get(f'o_s0_{t_}', -1) >= b_start
                            and S.last_w.get(f'o_s1_{t_}', -1) >= b_start):
                        fin_done.add(t_)
                        return t_
                return None

            def gla(dr):
                bA, bB = 2 * dr, 2 * dr + 1
                order_ = order[dr]
                gla_load(dr, order_[0], 0)
                for step in range(NT):
                    t = order_[step]
                    slot = step % 2
                    if step + 1 < NT:
                        gla_load(dr, order_[step + 1], (step + 1) % 2)
                    qk = gq_ld[dr][slot]
                    ke_ = gke_ld[dr][slot]
                    v_ = gv_ld[dr][slot]
                    dc_ = gdec_ld[dr][slot]
                    rk = [f'gq{dr}{slot}', f'gke{dr}{slot}', f'gv{dr}{slot}', f'gdec{dr}{slot}']
                    seq_start = (t % 2 == 0) if dr == 0 else (t % 2 == 1)
                    s32f = S32[dr][:].rearrange("p a b -> p (a b)")
                    s16f = S16[dr][:].rearrange("p a b -> p (a b)")
                    if seq_start and step > 0:
                        B.ts('dve', s32f, s32f, sf, None, ALU.mult, None, [f'S32_{dr}', FLG], [f'S32_{dr}'])
                        B.copy('dve', s16f, s32f, [f'S32_{dr}'], [f'S16_{dr}'])
                        yield
                    ba = [bA, bB]
                    B.mm([(ps[ba[h % 2]][:, (h // 2) * P:(h // 2 + 1) * P], qk[64 * (h % 2):64 * (h % 2) + 64, 2 + h // 2, :],
                           qk[64 * (h % 2):64 * (h % 2) + 64, h // 2, :], True, True) for h in range(4)],
                         rk, [PSK(bA), PSK(bB)])
                    yield
                    at4 = h4(AT16[dr][:])
                    for hh in range(2):
                        B.tt('dve', at4[:, hh:4:2, :], h4(ps[ba[hh]][:, 0:256], 2),
                             glam16[:, dr, :].unsqueeze(1).to_broadcast([P, 2, P]), ALU.mult,
                             [PSK(ba[hh]), 'glam16'], [f'AT16_{dr}'])
                        yield
                    specs = []
                    for h in range(4):
                        specs.append((ps[bA][:, h * 64:(h + 1) * 64], AT16[dr][:, h * P:(h + 1) * P], v_[:, h * 64:(h + 1) * 64], True, False))
                        specs.append((ps[bA][:, h * 64:(h + 1) * 64], qk[64 * (h % 2):64 * (h % 2) + 64, h // 2, :],
                                      S16[dr][64 * (h % 2):64 * (h % 2) + 64, h // 2, :], False, True))
                    B.mm(specs, [f'AT16_{dr}', f'S16_{dr}'] + rk, [PSK(bA)])
                    B.mm([(ps[bB][:, hp * P:(hp + 1) * P], ke_[:, hp * P:(hp + 1) * P], v_[:, hp * P:(hp + 1) * P], True, True)
                          for hp in range(2)], rk, [PSK(bB)])
                    yield
                    os_ = o32[dr][slot]
                    B.copy('act', os_[:], ps[bA][:, 0:256], [PSK(bA)], [f'o32_{dr}{slot}'])
                    B.dma('pool', o_s[dr, t], os_[:], [f'o32_{dr}{slot}'], [f'o_s{dr}_{t}'], f'st_o{dr}{slot}')
                    yield
                    for hp in range(2):
                        for hh in range(2):
                            B.stt('dve', S32[dr][64 * hh:64 * hh + 64, hp, :], S32[dr][64 * hh:64 * hh + 64, hp, :],
                                  dc_[64 * hh:64 * hh + 64, hp:hp + 1], ps[bB][64 * hh:64 * hh + 64, hp * P + hh * 64: hp * P + hh * 64 + 64],
                                  ALU.mult, ALU.add, [f'S32_{dr}', PSK(bB)] + rk, [f'S32_{dr}'])
                        yield
                    B.copy('dve', s16f, s32f, [f'S32_{dr}'], [f'S16_{dr}'])
                    seq_end = (t % 2 == 1) if dr == 0 else (t % 2 == 0)
                    if seq_end:
                        sq = t // 2
                        B.dma('pool', gst_d[sq, l, dr].rearrange("(hp hh) d v -> (hh d) hp v", hp=2), S32[dr][:],
                              [f'S32_{dr}'], [], f'st_gst{dr}')
                    yield
                    tt_ = fin_pop()
                    if tt_ is not None:
                        yield from gla_final(tt_, bA, dr)
                while len(fin_done) < NT:
                    tt_ = fin_pop()
                    if tt_ is None:
                        yield
                    else:
                        yield from gla_final(tt_, bA, dr)

            def lru(ct):
                L_ = LT[ct]
                hf, lxc, xc, xc16, rg, ig, av, bt = (L_[k] for k in ('hf', 'lxc', 'xc', 'xc16', 'rg', 'ig', 'av', 'bt'))
                hb, lgc, y16, stc, sto = (L_[k] for k in ('hb', 'lgc', 'y16', 'stc', 'sto'))
                br_, bi_ = 4 + 2 * ct, 5 + 2 * ct
                n_ = lambda k: f'{k}_{ct}'
                wv = lambda k: lvec[:, ct * 11 + k: ct * 11 + k + 1]
                nw = lambda k: lder[:, 4 + ct * 4 + k: 4 + ct * 4 + k + 1]

                def conv(c, slot):
                    c0 = c * CH
                    lo = max(0, c0 - 2)
                    hi = min(T, c0 + CH + 1)
                    lt = lxc[slot]
                    lk = n_(f'lxc{slot}')
                    if c == 0:
                        B.memset('pool', lt[:, 0:2], 0.0, [lk])
                    if c == NCH - 1:
                        B.memset('pool', lt[:, CH + 2:CH + 4], 0.0, [lk])
                    B.dma('sp', lt[:, 2 - (c0 - lo):2 - (c0 - lo) + (hi - lo)], lxg_s[ct, :, lo:hi], ['lxg_s'], [lk], f'ld_lxc{ct}{slot}')
                    yield
                    B.ts('pool', xc[:], lt[:, 2:2 + CH], wv(2), wv(4), ALU.mult, ALU.add, [lk, 'lvec'], [n_('xc')])
                    yield
                    for k in (0, 1, 3):
                        B.stt('dve', xc[:], lt[:, k:k + CH], wv(k), xc[:], ALU.mult, ALU.add, [lk, 'lvec', n_('xc')], [n_('xc')])
                        yield
                    for (tt_, k) in ((0, 0), (0, 1), (1, 0), (255, 3)):
                        src = lt[:, tt_ + k: tt_ + k + 257: 256]
                        dst = xc[:, tt_: tt_ + 257: 256]
                        B.stt('dve', dst, src, nw(k), dst, ALU.mult, ALU.add, [lk, 'lder', n_('xc')], [n_('xc')])
                    yield
                    B.copy('act', xc16[:], xc[:], [n_('xc')], [n_('xc16')])
                    yield

                def gates(dr, resets):
                    B.mm([(ps[br_][:, :], wbd16[:, 0 * 4 + dr * 2 + ct, :], xc16[:], True, True)], ['wbd16', n_('xc16')], [PSK(br_)])
                    B.mm([(ps[bi_][:, :], wbd16[:, 1 * 4 + dr * 2 + ct, :], xc16[:], True, True)], ['wbd16', n_('xc16')], [PSK(bi_)])
                    yield
                    vb = ct * 11 + 5 + dr * 3
                    B.act(rg[:], ps[br_][:, :], AF.Sigmoid, [PSK(br_), 'lvec'], [n_('rg')], bias=lvec[:, vb:vb + 1])
                    yield
                    B.act(ig[:], ps[bi_][:, :], AF.Sigmoid, [PSK(bi_), 'lvec'], [n_('ig')], bias=lvec[:, vb + 1:vb + 2])
                    yield
                    B.act(av[:], rg[:], AF.Exp, [n_('rg'), 'lder'], [n_('av')], scale=lder[:, dr * 2 + ct: dr * 2 + ct + 1])
                    yield
                    B.act(rg[:], av[:], AF.Square, [n_('av')], [n_('rg')])
                    yield
                    B.act(rg[:], rg[:], AF.Sqrt, [n_('rg')], [n_('rg')], bias=1.0, scale=-1.0)
                    yield
                    B.tt('dve', ig[:], ig[:], xc[:], ALU.mult, [n_('ig'), n_('xc')], [n_('ig')])
                    yield
                    B.tt('dve', bt[:], rg[:], ig[:], ALU.mult, [n_('rg'), n_('ig')], [n_('bt')])
                    for r0 in resets:
                        d_ = av[:, r0:r0 + 257:256]
                        B.ts('dve', d_, d_, sf, None, ALU.mult, None, [n_('av'), FLG], [n_('av')])
                    yield

                for c in range(NCH):
                    yield from conv(c, c % 2)
                    yield from gates(0, [0])
                    init = lruh0[:, (l * 2 + 0) * 2 + ct:(l * 2 + 0) * 2 + ct + 1] if c == 0 else hf[:, c * CH - 1:c * CH]
                    B.scan(hf[:, c * CH:(c + 1) * CH], av[:], bt[:], init, [n_('av'), n_('bt'), 'consts', n_('hf')], [n_('hf')])
                    yield
                B.copy('dve', stc[:, 0, :], hf[:, 255:T:256], [n_('hf')], [n_('stc')])
                for ci, c in enumerate(range(NCH - 1, -1, -1)):
                    s2 = ci % 2
                    yield from conv(c, c % 2)
                    yield from gates(1, [255])
                    B.dma('sp', lgc[s2][:], lxg_s[2 + ct, :, c * CH:(c + 1) * CH], ['lxg_s'], [n_(f'lgc{s2}')], f'ld_lgc{ct}{s2}')
                    init = lruh0[:, (l * 2 + 1) * 2 + ct:(l * 2 + 1) * 2 + ct + 1] if ci == 0 else hb[1 - s2][:, 0:1]
                    B.scan(hb[s2][:, ::-1], av[:, ::-1], bt[:, ::-1], init,
                           [n_('av'), n_('bt'), 'consts', n_(f'hb{1 - s2}')], [n_(f'hb{s2}')])
                    yield
                    B.copy('dve', stc[:, 1, 2 * c:2 * c + 2], hb[s2][:, 0:257:256], [n_(f'hb{s2}')], [n_('stc')])
                    g_ = lgc[s2]
                    gk = n_(f'lgc{s2}')
                    B.tt('dve', av[:], hf[:, c * CH:(c + 1) * CH], hb[s2][:], ALU.add, [n_('hf'), n_(f'hb{s2}'), n_('av')], [n_('av')])
                    yield
                    B.tt('pool', bt[:], g_[:], g_[:], ALU.mult, [gk, n_('bt')], [n_('bt')])
                    yield
                    B.ts('pool', bt[:], bt[:], 0.044715, 1.0, ALU.mult, ALU.add, [n_('bt')], [n_('bt')])
                    yield
                    B.tt('pool', bt[:], bt[:], g_[:], ALU.mult, [n_('bt'), gk], [n_('bt')])
                    yield
                    B.act(bt[:], bt[:], AF.Sigmoid, [n_('bt')], [n_('bt')], scale=1.5957691216057308)
                    B.tt('dve', av[:], av[:], g_[:], ALU.mult, [n_('av'), gk], [n_('av')])
                    yield
                    B.tt('dve', y16[s2][:], av[:], bt[:], ALU.mult, [n_('av'), n_('bt')], [n_(f'y16_{s2}')])
                    B.dma('pool', mixT_s[4 * c:4 * c + 4, :, (6 + ct) * P:(7 + ct) * P].rearrange("t p n -> p t n"),
                          y16[s2][:].rearrange("p (t n) -> p t n", t=4), [n_(f'y16_{s2}')], [f'mixL{ct}'], f'st_y16{ct}{s2}')
                    yield
                for dr in range(2):
                    b = br_ if dr == 0 else bi_
                    B.mm([(ps[b][0:16, 0:P], stc[:, dr, :], identf[:], True, True)], [n_('stc'), 'consts'], [PSK(b)])
                    B.copy('act', sto[dr][0:16, :], ps[b][0:16, 0:P], [PSK(b)], [n_(f'sto{dr}')])
                    B.dma('pool', lst_d[:, l, dr, ct * P:(ct + 1) * P], sto[dr][0:16, :], [n_(f'sto{dr}')], [], f'st_sto{ct}{dr}')
                    yield

            order = [list(range(NT)), list(range(NT - 1, -1, -1))]
            interleave([gla(0), gla(1), lru(0), lru(1)])

            S.ckpt(8)
            S.barrier()
            B.nmod = 6
            for j in range(4):
                B.dma('pool', w1_16[:, 2 * j:2 * j + 2, :],
                      w1_d[l, 2 * j * P:(2 * j + 2) * P, :].rearrange("(k p) n -> p k n", p=P), [], ['w1'], f'ld_w1{j}')
            B.dma('pool', w2_16[:, 0:W2PRE, :], w2_d[l, 0:W2PRE * P, :].rearrange("(k p) n -> p k n", p=P), [], ['w2'], 'ld_w2a')
            last = (l == L - 1)
            load_mod(M[0], 2, 'M0', 'ld_m0')
            load_mod(M[1], 4, 'M1', 'ld_m1')
            B.dma('sp', tmp32[:], n2_d[l].partition_broadcast(P), [], ['tmp32'], 'ld_tmp')
            B.stt('dve', M[1][:], M[1][:], 1.0, tmp32[:], ALU.add, ALU.mult, ['M1', 'tmp32'], ['M1'])
            load_mod(M[2], 3, 'M2', 'ld_m2')
            load_mod(M[3], 5, 'M3', 'ld_m3')
            if last:
                fnb = stage[0][:, 0:D]
                B.dma('sp', fnb, fn_d.partition_broadcast(P), [], ['fnb', 'stage0'], 'ld_st0')
            S.ckpt(9)

            def front(t):
                s2 = t % 2
                xs_ = xt[s2]
                xk = f'xt{s2}'
                yield
                yield
                for hh in range(2):
                    b = hh
                    B.mm([(ps[b][:, :], mixTt[s2][:, kc, :], wout16[:, kc, hh * 512:(hh + 1) * 512], kc == 0, kc == 7)
                          for kc in range(8)], [f'mixTt{s2}', 'wout16'], [PSK(b)])
                    yield
                for hh in range(2):
                    b = hh
                    B.tt('dve', tmp32[:, hh * 512:(hh + 1) * 512], ps[b][:, :], M[0][:, hh * 512:(hh + 1) * 512], ALU.mult,
                         [PSK(b), 'M0'], ['tmp32'])
                    yield
                B.tt('dve', xs_[:], xs_[:], tmp32[:], ALU.add, [xk, 'tmp32'], [xk])
                yield
                yield from norm_mod_g(xs_, xk, M[1], M[2], 'M1', 'M2', 2, hTs[s2], f'hT{s2}')

            def mlp(t):
                s2 = t % 2
                xs_ = xt[s2]
                xk = f'xt{s2}'
                hT_ = hTs[s2]
                hk = f'hT{s2}'

                def mlp1(c8):
                    b = 3 + c8 % 2
                    B.mm([(ps[b][:, :], hT_[:, kc, :], w1_16[:, kc, c8 * 512:(c8 + 1) * 512], kc == 0, kc == 7)
                          for kc in range(8)], ['w1', hk], [PSK(b)])
                    B.act(rl32[:], ps[b][:, :], AF.Relu, [PSK(b)], ['rl32'])
                    B.tt('dve', u16[c8 % 2][:], rl32[:], rl32[:], ALU.mult, ['rl32'], [f'u16_{c8 % 2}'])

                def utr(c8):
                    b = 5
                    B.tr([(ps16[b][:, fi * P:(fi + 1) * P], u16[c8 % 2][:, fi * P:(fi + 1) * P]) for fi in range(4)], ident16[:],
                         [f'u16_{c8 % 2}', 'ident16'], [PSK(b)])
                    B.copy('act', uT[:, c8 % 2, :, :].rearrange("p a b -> p (a b)"), ps16[b][:, 0:512], [PSK(b)], [f'uT{c8 % 2}'])

                def mlp2(f4):
                    for hh in range(2):
                        B.mm([(ps[6 + hh][:, :], uT[:, f4 % 2, fi, :], w2_16[:, f4 * 4 + fi, hh * 512:(hh + 1) * 512],
                               f4 == 0 and fi == 0, f4 == 7 and fi == 3) for fi in range(4)],
                             [f'uT{f4 % 2}', 'w2'], [PSK(6 + hh)])

                mlp1(0)
                for c8 in range(8):
                    if c8 + 1 < 8:
                        mlp1(c8 + 1)
                        yield
                    utr(c8)
                    if c8 >= 1:
                        mlp2(c8 - 1)
                    yield
                mlp2(7)
                for hh in range(2):
                    B.tt('dve', rl32[:], ps[6 + hh][:, :], M[3][:, hh * 512:(hh + 1) * 512], ALU.mult,
                         [PSK(6 + hh), 'M3'], ['rl32'])
                    B.tt('dve', xs_[:, hh * 512:(hh + 1) * 512], xs_[:, hh * 512:(hh + 1) * 512], rl32[:], ALU.add, [xk, 'rl32'], [xk])
                yield
                if not last:
                    B.dma('pool', xs_s[t * P:(t + 1) * P, :], xs_[:], [xk], [f'xs{t}'], f'st_x{s2}')
                else:
                    B.act(h16[:], xs_[:], AF.Square, [xk], ['h16', 'st0'], accum=st4[:, 0:1])
                    B.act(st4[:, 1:2], st4[:, 0:1], AF.Ln, ['st0'], ['st1'], bias=EPS, scale=1.0 / D)
                    B.act(st4[:, 2:3], st4[:, 1:2], AF.Exp, ['st1'], ['st2'], scale=-0.5)
                    B.stt('dve', xs_[:], xs_[:], st4[:, 2:3], fnb, ALU.mult, ALU.mult, [xk, 'st2', 'fnb'], [xk])
                    B.dma('pool', y_d[t * P:(t + 1) * P, :], xs_[:], [xk], [], f'st_x{s2}')
                yield

            def loads_c(t):
                s2 = t % 2
                B.dma('sp', xt[s2][:], x_src[t * P:(t + 1) * P, :], [f'xs{t}'] if l > 0 else [], [f'xt{s2}'], f'ld_xt{s2}')
                B.dma('sp', mixTt[s2][:].rearrange("p a b -> p (a b)"), mixT_s[t],
                      [f'mixA{t}', f'mixG{t}', 'mixL0', 'mixL1'], [f'mixTt{s2}'], f'ld_mix{s2}')

            loads_c(0)
            for t in range(NT + 1):
                gs = []
                if t >= 1:
                    gs.append(mlp(t - 1))
                if t < NT:
                    gs.append(front(t))
                interleave(gs)
                if t + 1 < NT:
                    loads_c(t + 1)
                if t == 1:
                    S.ckpt(10)
            S.ckpt(11)
        S.finalize()
        print("program: %d sched-instrs, eng counts %s, dma keys %d, phaseA carve end %d" %
              (len(S.ins), S.eng_count, len(S.dma_keys), offA), flush=True)
        eng_sems = {e: [es.enter_context(nc.semaphore(f"s_{e}{k}")) for k in range(S.n_eng_sems[e])] for e in COMPUTE}
        dma_sems = {k: es.enter_context(nc.semaphore(f"d_{k}")) for k in S.dma_keys}
        with nc.Block() as block:
            @block.sync
            def _(h):
                S.emit_engine('sp', h, eng_sems, dma_sems, final_wait_all_dma=True)

            @block.tensor
            def _(h):
                S.emit_engine('pe', h, eng_sems, dma_sems)

            @block.scalar
            def _(h):
                S.emit_engine('act', h, eng_sems, dma_sems)

            @block.vector
            def _(h):
                S.emit_engine('dve', h, eng_sems, dma_sems)

            @block.gpsimd
            def _(h):
                S.emit_engine('pool', h, eng_sems, dma_sems)
    return nc


def _consts():
    s = np.arange(P)[:, None]
    c = np.arange(P)[None, :]
    tri = np.stack([(s <= c), (s > c), (s >= c), (s < c)]).astype(np.float32)
    glam = np.stack([(s <= c), (s >= c)]).astype(np.float32)
    perm = np.zeros((P, P), np.float32)
    for dp in range(P):
        hd = dp % 64
        src = dp + 16 if (hd % 32) < 16 else dp - 16
        perm[src, dp] = 1.0
    return tri, glam, perm


def _rope_tables(enabled):
    tab = np.zeros((NT, P, 2, P), np.float32)
    if not enabled:
        tab[:, :, 0, :] = 1.0
        return tab
    t = np.arange(T)
    row = (t // 64).astype(np.float32)
    col = (t % 64).astype(np.float32)
    nf = 16
    inv = (np.float32(10000.0) ** (-np.arange(nf, dtype=np.float32) / nf)).astype(np.float32)
    cosT = np.zeros((64, T), np.float32)
    sinT = np.zeros((64, T), np.float32)
    for d in range(64):
        half, j = d // 32, d % 32
        pos = row if half == 0 else col
        ang = (pos * inv[j % 16]).astype(np.float32)
        cosT[d] = np.cos(ang)
        sinT[d] = -np.sin(ang) if j < 16 else np.sin(ang)
    cos2 = np.concatenate([cosT, cosT], 0).reshape(P, NT, P).transpose(1, 0, 2)
    sin2 = np.concatenate([sinT, sinT], 0).reshape(P, NT, P).transpose(1, 0, 2)
    tab[:, :, 0, :] = cos2
    tab[:, :, 1, :] = sin2
    return tab


_NC_CACHE = {}


def prepare(x_prompt, x_sample, c, cache_k, cache_v, state_gla, state_lru, c_ctx,
           w_mod, b_mod, norm1, norm2, w_in, attn_sink, gla_gate_w, gla_gate_b, gla_norm,
           lru_conv_w, lru_conv_b, lru_wa, lru_ba, lru_wx, lru_bx, lru_lambda,
           w_out, w_mlp1, w_mlp2, final_norm):
    f = np.float32
    A = lambda a: np.ascontiguousarray(np.asarray(a, dtype=f))
    x_prompt, x_sample, c, cache_k, cache_v = A(x_prompt), A(x_sample), A(c), A(cache_k), A(cache_v)
    state_gla, state_lru, c_ctx = A(state_gla), A(state_lru), A(c_ctx)
    w_in = A(w_in)
    qcols = []
    for cc in range(4):
        qcols += list(range(cc * 64, cc * 64 + 64)) + list(range((4 + cc) * 64, (4 + cc) * 64 + 64))
    cols = (qcols + list(range(512, 640)) + list(range(1824, 2080)) + list(range(2080, 2336)) + list(range(1536, 1568))
            + list(range(512, 640)) + list(range(640, 768)) + list(range(768, 1024)) + list(range(1024, 1280))
            + list(range(1280, 1536)) + list(range(1568, 1824)))
    assert len(cols) == WIN
    w_in_r = np.ascontiguousarray(w_in[:, :, cols])
    gaug = np.zeros((L, 33, 512), f)
    gw, gb = A(gla_gate_w), A(gla_gate_b)
    gaug[:, 0:16, 0:256] = gw[:, 0]
    gaug[:, 16:32, 256:512] = gw[:, 1]
    gaug[:, 32, 0:256] = gb[:, 0]
    gaug[:, 32, 256:512] = gb[:, 1]
    lvec = np.zeros((L, P, NVEC), f)
    cw, cb, ba, bx, lam = A(lru_conv_w), A(lru_conv_b), A(lru_ba), A(lru_bx), A(lru_lambda)
    for ct in range(2):
        sl = slice(ct * P, (ct + 1) * P)
        for k in range(4):
            lvec[:, :, ct * 11 + k] = cw[:, k, sl]
        lvec[:, :, ct * 11 + 4] = cb[:, sl]
        for dr in range(2):
            lvec[:, :, ct * 11 + 5 + dr * 3 + 0] = ba[:, dr, sl]
            lvec[:, :, ct * 11 + 5 + dr * 3 + 1] = bx[:, dr, sl]
            lvec[:, :, ct * 11 + 5 + dr * 3 + 2] = lam[:, dr, sl]
    tri, glam, perm = _consts()
    s_ = np.arange(P)[:, None]
    c_ = np.arange(P)[None, :]
    one = np.ones((P, P), f)
    zero = np.zeros((P, P), f)
    attm_sample = np.stack([s_ >= c_, s_ >= c_, s_ <= c_, s_ <= c_]).astype(f)
    attm_prompt = np.stack([zero, one, one, zero]).astype(f)
    shared = dict(
        ident=np.eye(P, dtype=f), tri=tri, glam=glam, perm=perm,
        w_mod=A(w_mod), b_mod=A(b_mod).reshape(L, 1, 6 * D), norm1=A(norm1).reshape(L, 1, D),
        norm2=A(norm2).reshape(L, 1, D), final_norm=A(final_norm).reshape(1, D), w_in_r=w_in_r,
        w_out=A(w_out), w_mlp1=A(w_mlp1), w_mlp2=A(w_mlp2), gaug=gaug,
        gla_norm=A(gla_norm).reshape(L, 1, 256), attn_sink=A(attn_sink).reshape(L, 1, 8), lruvec=lvec,
        lru_wa=A(lru_wa), lru_wx=A(lru_wx))
    rope_on = _rope_tables(True)
    rope_off = _rope_tables(False)
    in_maps = []
    for core in range(8):
        m = dict(shared)
        if core < 4:
            b = core
            m['x'] = x_sample[b]
            m['cond8'] = np.ascontiguousarray(c[b].reshape(8, P).T)
            m['ck'] = np.ascontiguousarray(cache_k[b].reshape(L, 512, 128))
            m['cv'] = np.ascontiguousarray(cache_v[b].reshape(L, 512, 128))
            sg = state_gla[b].reshape(L, 2, 2, 2, 64, 64)
            m['sgla'] = np.ascontiguousarray(sg.transpose(0, 1, 3, 4, 2, 5).reshape(L, 2, P, 2, 64))
            sl_ = state_lru[b].reshape(L, 2, 2, P)
            m['lruh0'] = np.ascontiguousarray(sl_.transpose(3, 0, 1, 2).reshape(P, L * 4))
            fl = np.zeros((P, 4), f)
            fl[:, 2] = 1.0
            m['flags'] = fl
            m['attm'] = attm_sample
            m['rope'] = rope_on
        else:
            pc = (core - 4) % 2
            m['x'] = np.ascontiguousarray(x_prompt[pc * NSEQ:(pc + 1) * NSEQ].reshape(T, D))
            m['cond8'] = np.ascontiguousarray(c_ctx.reshape(8, P).T)
            m['ck'] = np.zeros((L, 512, 128), f)
            m['cv'] = np.zeros((L, 512, 128), f)
            m['sgla'] = np.zeros((L, 2, P, 2, 64), f)
            m['lruh0'] = np.zeros((P, L * 4), f)
            fl = np.zeros((P, 4), f)
            fl[:, 0] = -30000.0
            fl[:, 1] = 1.0
            fl[:, 3] = -1.0
            m['flags'] = fl
            m['attm'] = attm_prompt
            m['rope'] = rope_off
        in_maps.append(m)
    return in_maps


def kernel(**inputs):
    f = np.float32
    in_maps = prepare(**inputs)
    if 'nc' not in _NC_CACHE:
        _NC_CACHE['nc'] = build_program()
    nc = _NC_CACHE['nc']
    res = run_bass_kernel_spmd(nc, in_maps, core_ids=list(range(8)))
    R = res.results
    y_sample = np.stack([R[b]['y'] for b in range(4)]).astype(f)
    y_prompt = np.concatenate([R[4]['y'], R[5]['y']], 0).reshape(32, 256, D).astype(f)

    def kvfix(name):
        parts = []
        for core in (4, 5):
            a = R[core][name].reshape(L, NSEQ, 256, 2, 64).transpose(1, 0, 2, 3, 4)
            parts.append(a)
        return np.ascontiguousarray(np.concatenate(parts, 0)).astype(f)

    new_k = kvfix('kout')
    new_v = kvfix('vout')
    gst = np.concatenate([R[4]['gst'], R[5]['gst']], 0).astype(f)
    lst = np.concatenate([R[4]['lst'], R[5]['lst']], 0).astype(f)
    return (y_prompt, y_sample, new_k, new_v, gst, lst)
```

```python
import bisect
import contextlib
import numpy as np
import concourse.bass as bass
import concourse.mybir as mybir
from concourse.bass_utils import run_bass_kernel_spmd

F32 = mybir.dt.float32
BF16 = mybir.dt.bfloat16
AF = mybir.ActivationFunctionType
ALU = mybir.AluOpType
AX = mybir.AxisListType

L = 2
D = 1024
T = 4096
NT = 32
P = 128
NSEQ = 16
WIN = 2464
QC, KF, LXO, LGO, GZO, KVO, GQKO, GVRO = 0, 512, 640, 896, 1152, 1184, 1440, 1952
EPS = 1e-6
NVEC = 22

COMPUTE = ('pe', 'act', 'dve', 'pool')
EPOCH = 8000


class Sched:
    def __init__(self):
        self.ins = []
        self.last_w = {}
        self.readers = {}
        self.pending = {}
        self.since_barrier_dma = []
        self.last_on = {}

    def barrier(self):
        ids = set(self.last_on.values()) | set(self.since_barrier_dma)
        self.since_barrier_dma = []
        for e in COMPUTE + ('sp',):
            self.pending[e] = set(ids) | self.pending.get(e, set())

    dead = False

    stopped = False

    def ckpt(self, n):
        kd = os.environ.get('KDEAD', '')
        if kd:
            a, b = [int(v) for v in kd.split(':')]
            if n == a and not self.stopped:
                self.dead = True
            if n == b and not self.stopped:
                self.dead = False
        if KSTOP[0] == n:
            self.dead = True
            self.stopped = True

    def add(self, eng, fn, reads=(), writes=(), dma_key=None):
        if self.dead:
            return -1
        pr = [r for r in reads if isinstance(r, str) and r.startswith('ps') and r[2:].isdigit()]
        if pr:
            writes = list(writes) + [r for r in pr if r not in writes]
        i = len(self.ins)
        deps = set()
        for r in reads:
            w = self.last_w.get(r)
            if w is not None:
                deps.add(w)
        for w_ in writes:
            w = self.last_w.get(w_)
            if w is not None:
                deps.add(w)
            for rd in self.readers.get(w_, {}).values():
                deps.add(rd)
        for r in reads:
            d = self.readers.setdefault(r, {})
            d[('dma', i) if dma_key is not None else eng] = i
        for w_ in writes:
            self.last_w[w_] = i
            self.readers[w_] = {}
        pend = self.pending.pop(eng, None)
        if pend:
            deps |= pend
        deps.discard(i)
        self.ins.append(dict(eng=eng, fn=fn, deps=deps, dma_key=dma_key, id=i))
        if dma_key is not None:
            self.since_barrier_dma.append(i)
        else:
            self.last_on[eng] = i
        return i

    def finalize(self):
        ins = self.ins
        for it in ins:
            best = {}
            keep = set()
            for d in it['deps']:
                p = ins[d]
                if p['dma_key'] is not None:
                    keep.add(d)
                else:
                    e = p['eng']
                    if e == 'pe' and it['eng'] == 'pe' and it['dma_key'] is None:
                        continue
                    if e not in best or best[e] < d:
                        best[e] = d
            keep.update(best.values())
            it['deps'] = sorted(keep)
        sig = set()
        for it in ins:
            sig.update(it['deps'])
        self.eng_count = {e: 0 for e in COMPUTE}
        self.dma_count = {}
        for i, it in enumerate(ins):
            if it['dma_key'] is not None:
                k = it['dma_key']
                self.dma_count[k] = self.dma_count.get(k, 0) + 1
                it['sig'] = ('dma', k, self.dma_count[k] * 16)
            elif i in sig:
                e = it['eng']
                c = self.eng_count[e]
                self.eng_count[e] = c + 1
                it['sig'] = ('eng', e, c // EPOCH, c % EPOCH + 1)
            else:
                it['sig'] = None
        self.n_eng_sems = {e: self.eng_count[e] // EPOCH + 1 for e in COMPUTE}
        self.dma_keys = sorted(self.dma_count.keys(), key=str)
        self.dma_ids = {}
        for i, it in enumerate(ins):
            if it['dma_key'] is not None:
                self.dma_ids.setdefault(it['dma_key'], []).append(i)

    def emit_engine(self, eng, handle, eng_sems, dma_sems, final_wait_all_dma=False):
        seen = {}
        ins = self.ins
        for it in ins:
            if it['eng'] != eng:
                continue
            need = {}
            for d in it['deps']:
                s = ins[d]['sig']
                if s[0] == 'dma':
                    nbefore = bisect.bisect_left(self.dma_ids[s[1]], it['id'])
                    sem, val, key = dma_sems[s[1]], max(s[2], nbefore * 16), ('dma', s[1])
                else:
                    sem, val, key = eng_sems[s[1]][s[2]], s[3], ('eng', s[1], s[2])
                if key not in need or need[key][1] < val:
                    need[key] = (sem, val)
            for key, (sem, val) in need.items():
                if seen.get(key, 0) < val:
                    handle.wait_ge(sem, val)
                    seen[key] = val
            r = it['fn'](handle)
            s = it['sig']
            if s is not None:
                if s[0] == 'dma':
                    r.then_inc(dma_sems[s[1]], 16)
                else:
                    r.then_inc(eng_sems[s[1]][s[2]], 1)
        if final_wait_all_dma:
            for k, n in self.dma_count.items():
                handle.wait_ge(dma_sems[k], n * 16)


import os
KSTOP = [int(os.environ.get('KSTOP', '0'))]


class Builder:
    def __init__(self, nc, es):
        self.nc = nc
        self.es = es
        self.S = Sched()
        self.nb = 0
        self.nmod = 8

    def sb(self, name, shape, dt):
        return self.es.enter_context(self.nc.sbuf_tensor('sb_' + name, list(shape), dt))

    def din(self, name, shape, dt=F32):
        return self.nc.dram_tensor(name, list(shape), dt, kind="ExternalInput").ap()

    def dout(self, name, shape, dt=F32):
        return self.nc.dram_tensor(name, list(shape), dt, kind="ExternalOutput").ap()

    def dscr(self, name, shape, dt=F32):
        return self.nc.dram_tensor(name, list(shape), dt, kind="Internal").ap()

    def bank(self):
        b = self.nb % self.nmod
        self.nb += 1
        return b

    def dma(self, q, out, in_, reads, writes, key):
        self.S.add(q, lambda e, o=out, i=in_: e.dma_start(out=o, in_=i), reads, writes, dma_key=key)

    def act(self, out, in_, func, reads, writes, bias=None, scale=None, accum=None):
        kw = {}
        if bias is not None:
            kw['bias'] = bias
        if scale is not None:
            kw['scale'] = scale
        if accum is not None:
            kw['accum_out'] = accum
        self.S.add('act', lambda e, o=out, i=in_, f=func, kw=kw: e.activation(out=o, in_=i, func=f, **kw),
                   reads, writes)

    def tt(self, eng, out, a, b, op, reads, writes):
        self.S.add(eng, lambda e, o=out, a=a, b=b, op=op: e.tensor_tensor(out=o, in0=a, in1=b, op=op),
                   reads, writes)

    def ts(self, eng, out, a, s1, s2, op0, op1, reads, writes):
        if s2 is None:
            self.S.add(eng, lambda e, o=out, a=a, s1=s1, op0=op0: e.tensor_single_scalar(out=o, in_=a, scalar=s1, op=op0),
                       reads, writes)
        else:
            self.S.add(eng, lambda e, o=out, a=a, s1=s1, s2=s2, op0=op0, op1=op1:
                       e.tensor_scalar(out=o, in0=a, scalar1=s1, scalar2=s2, op0=op0, op1=op1), reads, writes)

    def stt(self, eng, out, a, s, b, op0, op1, reads, writes):
        self.S.add(eng, lambda e, o=out, a=a, s=s, b=b, op0=op0, op1=op1:
                   e.scalar_tensor_tensor(out=o, in0=a, scalar=s, in1=b, op0=op0, op1=op1), reads, writes)

    def copy(self, eng, out, in_, reads, writes):
        if eng == 'act':
            self.S.add('act', lambda e, o=out, i=in_: e.copy(out=o, in_=i), reads, writes)
        else:
            self.S.add(eng, lambda e, o=out, i=in_: e.tensor_copy(out=o, in_=i), reads, writes)

    def memset(self, eng, ap, val, writes):
        self.S.add(eng, lambda e, a=ap, v=val: e.memset(a, v), (), writes)

    def recip(self, out, in_, reads, writes):
        self.S.add('dve', lambda e, o=out, i=in_: e.reciprocal(out=o, in_=i), reads, writes)

    def mm(self, specs, reads, writes):
        def fn(e, specs=specs):
            r = None
            for sp_ in specs:
                o, l, rh, st, sp = sp_[:5]
                if len(sp_) > 5 and sp_[5]:
                    r = e.matmul(o, lhsT=l, rhs=rh, start=st, stop=sp, skip_group_check=True)
                else:
                    r = e.matmul(o, lhsT=l, rhs=rh, start=st, stop=sp)
            return r
        self.S.add('pe', fn, reads, writes)

    def tr(self, specs, ident, reads, writes):
        def fn(e, specs=specs, ident=ident):
            r = None
            for (o, i) in specs:
                r = e.transpose(out=o, in_=i, identity=ident)
            return r
        self.S.add('pe', fn, reads, writes)

    def scan(self, out, d0, d1, init, reads, writes):
        self.S.add('dve', lambda e, o=out, a=d0, b=d1, i=init:
                   e.tensor_tensor_scan(out=o, data0=a, data1=b, initial=i, op0=ALU.mult, op1=ALU.add),
                   reads, writes)

    def reduce_sum(self, out, in_, reads, writes):
        self.S.add('dve', lambda e, o=out, i=in_: e.reduce_sum(out=o, in_=i, axis=AX.X), reads, writes)


def build_program():
    nc = bass.Bass("TRN2", target_bir_lowering=False)
    with contextlib.ExitStack() as es:
        B = Builder(nc, es)
        S = B.S
        x_d = B.din("x", [T, D])
        cond_d = B.din("cond8", [P, 8])
        ck_d = B.din("ck", [L, 512, 128])
        cv_d = B.din("cv", [L, 512, 128])
        sgla_d = B.din("sgla", [L, 2, P, 2, 64])
        lruh0_d = B.din("lruh0", [P, L * 4])
        flags_d = B.din("flags", [P, 4])
        ident_d = B.din("ident", [P, P])
        tri_d = B.din("tri", [4, P, P])
        glam_d = B.din("glam", [2, P, P])
        attm_d = B.din("attm", [4, P, P])
        perm_d = B.din("perm", [P, P])
        rope_d = B.din("rope", [NT, P, 2, P])
        wmod_d = B.din("w_mod", [L, D, 6 * D])
        bmod_d = B.din("b_mod", [L, 1, 6 * D])
        n1_d = B.din("norm1", [L, 1, D])
        n2_d = B.din("norm2", [L, 1, D])
        fn_d = B.din("final_norm", [1, D])
        win_d = B.din("w_in_r", [L, D, WIN])
        wout_d = B.din("w_out", [L, D, D])
        w1_d = B.din("w_mlp1", [L, D, 4 * D])
        w2_d = B.din("w_mlp2", [L, 4 * D, D])
        gaug_d = B.din("gaug", [L, 33, 512])
        gnorm_d = B.din("gla_norm", [L, 1, 256])
        sink_d = B.din("attn_sink", [L, 1, 8])
        lvec_d = B.din("lruvec", [L, P, NVEC])
        wa_d = B.din("lru_wa", [L, 2, 4, 64, 64])
        wx_d = B.din("lru_wx", [L, 2, 4, 64, 64])

        y_d = B.dout("y", [T, D])
        kout_d = B.dout("kout", [L, T, 128])
        vout_d = B.dout("vout", [L, T, 128])
        gst_d = B.dout("gst", [NSEQ, L, 2, 4, 64, 64])
        lst_d = B.dout("lst", [NSEQ, L, 2, 256])

        xs_s = B.dscr("xs_s", [T, D])
        mod_s = B.dscr("mod_s", [1, 6 * D])
        lxg_s = B.dscr("lxg_s", [4, P, T])
        qkT_s = B.dscr("qkT_s", [NT, P, 1024], BF16)
        ke_s = B.dscr("ke_s", [NT, P, 512], BF16)
        v16_s = B.dscr("v16_s", [NT, P, 256], BF16)
        dec_s = B.dscr("dec_s", [NT, P, 4])
        sgr_s = B.dscr("sgr_s", [NT, P, 256])
        o_s = B.dscr("o_s", [2, NT, P, 256])
        mixT_s = B.dscr("mixT_s", [NT, P, 1024], BF16)

        SW = 2048
        ident16 = B.sb("ident16", [P, P], BF16)
        identf = B.sb("identf", [P, P], F32)
        tri = B.sb("tri", [P, 4, P], F32)
        glam16 = B.sb("glam16", [P, 2, P], BF16)
        attm16 = B.sb("attm16", [P, 4, P], BF16)
        perm = B.sb("perm", [P, P], F32)
        flags = B.sb("flags", [P, 4], F32)
        ones_col = B.sb("ones_col", [P, 1], F32)
        scond = B.sb("scond", [P, 8], F32)
        scond16 = B.sb("scond16", [P, 8], BF16)
        lruh0 = B.sb("lruh0", [P, L * 4], F32)
        M = [B.sb(f"mod{i}", [P, D], F32) for i in range(4)]
        wout16 = B.sb("wout16", [P, 8, D], BF16)
        wbig = B.sb("wbig", [P, 65536], BF16)
        stage = [B.sb("stage0", [P, SW], F32), B.sb("stage1", [P, 1024], F32)]
        esink = B.sb("esink", [P, 8], F32)
        lvec = B.sb("lvec", [P, NVEC], F32)
        lder = B.sb("lder", [P, 16], F32)
        xt = [B.sb(f"xt{i}", [P, D], F32) for i in range(2)]
        tmp32 = B.sb("tmp32", [P, D], F32)
        h16 = B.sb("h16", [P, D], BF16)
        hTs = [B.sb(f"hT{i}", [P, 8, P], BF16) for i in range(2)]
        hT = hTs[0]
        st4 = B.sb("st4", [P, 8], F32)
        u16 = [B.sb(f"u16_{i}", [P, 512], BF16) for i in range(2)]
        wbd32 = stage[1][:, 0:1024].rearrange("p (a b) -> p a b", a=8)

        off = [0]
        un = B.sb("un", [P, 4096], BF16)
        base = [wbig, 65536]

        def carve(nelem_bf16, dt, shape):
            a = base[0][:, off[0]:off[0] + nelem_bf16]
            off[0] += nelem_bf16
            assert off[0] <= base[1], off[0]
            if dt == F32:
                a = a.bitcast(F32)
            if len(shape) == 3:
                a = a.rearrange("p (a b) -> p a b", a=shape[1])
            elif len(shape) == 4:
                a = a.rearrange("p (a b c) -> p a b c", a=shape[1], b=shape[2])
            assert list(a.shape) == list(shape), (a.shape, shape)
            return a

        base[:] = [un, 4096]
        gaug_full = carve(1024, F32, [P, 512])
        gaug = gaug_full[0:33, :]
        gnb = carve(512, F32, [P, 256])
        wbd16 = carve(1024, BF16, [P, 8, P])
        ckT = carve(512, BF16, [P, 512])
        cva = carve(520, BF16, [P, 4, 2, 65])
        off[0] = 0
        uT = carve(1024, BF16, [P, 2, 4, P])
        rl32 = carve(1024, F32, [P, 512])
        mixTt = [carve(1024, BF16, [P, 8, P]) for _ in range(2)]
        off[0] = 0
        base[:] = [wbig, 65536]
        win16 = wbig[:, 0:8 * WIN].rearrange("p (k n) -> p k n", k=8)
        w1_16 = wbig[:, 0:32768].rearrange("p (k n) -> p k n", k=8)
        w2_16 = wbig[:, 32768:65536].rearrange("p (k n) -> p k n", k=32)
        wmod16 = wbig[:, 0:8 * 6144].rearrange("p (k n) -> p k n", k=8)
        off[0] = 8 * WIN
        KT = carve(T, BF16, [P, T])
        VAflat = wbig[:, off[0]:off[0] + NT * 130]
        off[0] += NT * 130
        VA = VAflat.rearrange("p (t g c) -> p t g c", t=NT, g=2)
        QT = carve(3 * 4 * P, BF16, [P, 3, 4, P])
        qk32 = carve(2 * 5 * P, F32, [P, 5, P])
        rt1 = carve(2 * 5 * P, F32, [P, 5, P])
        rt2 = carve(2 * 5 * P, F32, [P, 5, P])
        ropet = [carve(2 * 2 * P, F32, [P, 2, P]) for _ in range(2)]
        kv32 = [carve(2 * 256, F32, [P, 256]) for _ in range(2)]
        lxg = [carve(2 * 4 * P, F32, [P, 4, P]) for _ in range(2)]
        zT33 = carve(2 * P, F32, [P, P])
        sp32 = carve(2 * 512, F32, [P, 512])
        EqEe = [carve(2 * 512, F32, [P, 512]) for _ in range(2)]
        Ek = [carve(2 * 256, F32, [P, 256]) for _ in range(2)]
        qin16 = carve(512, BF16, [P, 512])
        kin16 = carve(512, BF16, [P, 512])
        ke16 = [carve(512, BF16, [P, 512]) for _ in range(2)]
        v16 = [carve(256, BF16, [P, 256]) for _ in range(2)]
        sgr32 = [carve(2 * 256, F32, [P, 256]) for _ in range(2)]
        qkT16 = [carve(1024, BF16, [P, 8, P]) for _ in range(2)]
        dec32 = [carve(2 * 4, F32, [P, 4]) for _ in range(2)]
        PT = [carve(512, BF16, [P, 512]) for _ in range(3)]
        den = carve(2 * 8, F32, [P, 8])
        rec = carve(2 * 8, F32, [P, 8])
        o16 = carve(512, BF16, [P, 512])
        oT16 = [carve(512, BF16, [P, 4, P]) for _ in range(2)]
        gqk32 = carve(2 * 512, F32, [P, 512])
        offA = off[0]
        off[0] = 0
        gq_ld = [[carve(512, BF16, [P, 4, P]) for _ in range(2)] for _ in range(2)]
        gke_ld = [[carve(256, BF16, [P, 256]) for _ in range(2)] for _ in range(2)]
        gv_ld = [[carve(256, BF16, [P, 256]) for _ in range(2)] for _ in range(2)]
        gdec_ld = [[carve(2 * 2, F32, [P, 2]) for _ in range(2)] for _ in range(2)]
        S32 = [carve(2 * 128, F32, [P, 2, 64]) for _ in range(2)]
        S16 = [carve(128, BF16, [P, 2, 64]) for _ in range(2)]
        AT16 = [carve(512, BF16, [P, 512]) for _ in range(2)]
        o32 = [[carve(2 * 256, F32, [P, 256]) for _ in range(2)] for _ in range(2)]
        FS = [dict(fo=[carve(2 * 256, F32, [P, 256]) for _ in range(3)], fsq=carve(2 * 256, F32, [P, 256]),
                   fy=carve(2 * 256, F32, [P, 256]), fy16=carve(256, BF16, [P, 256]), fyT=carve(256, BF16, [P, 2, P]),
                   st=carve(2 * 4, F32, [P, 4])) for _ in range(2)]
        CH = 512
        NCH = T // CH
        W2PRE = 24
        LT = []
        for _ct in range(2):
            LT.append(dict(
                hf=carve(2 * T, F32, [P, T]),
                lxc=[carve(2 * (CH + 4), F32, [P, CH + 4]) for _ in range(2)],
                xc=carve(2 * CH, F32, [P, CH]), xc16=carve(CH, BF16, [P, CH]),
                rg=carve(2 * CH, F32, [P, CH]), ig=carve(2 * CH, F32, [P, CH]),
                av=carve(2 * CH, F32, [P, CH]), bt=carve(2 * CH, F32, [P, CH]),
                hb=[carve(2 * CH, F32, [P, CH]) for _ in range(2)],
                lgc=[carve(2 * CH, F32, [P, CH]) for _ in range(2)],
                y16=[carve(CH, BF16, [P, CH]) for _ in range(2)],
                stc=carve(2 * 32, F32, [P, 2, 16]),
                sto=[carve(2 * P, F32, [P, P]) for _ in range(2)]))
        assert off[0] <= 32768 + W2PRE * 1024, off[0]

        ps = [es.enter_context(nc.psum_tensor(f"ps{i}", [P, 512], F32)) for i in range(8)]
        ps16 = [p_[:].bitcast(BF16) for p_ in ps]

        def PSK(b):
            return f"ps{b}"

        def h4(ap, n=4):
            return ap.rearrange("p (h c) -> p h c", h=n)

        CK = 'ld_c'
        B.dma('sp', identf[:], ident_d[:, :], [], ['consts'], CK)
        B.dma('sp', tri[:], tri_d.rearrange("k p n -> p k n"), [], ['consts'], CK)
        B.dma('sp', perm[:], perm_d[:, :], [], ['consts'], CK)
        B.dma('sp', flags[:], flags_d[:, :], [], ['consts'], CK)
        B.dma('sp', lruh0[:], lruh0_d[:, :], [], ['consts'], CK)
        B.dma('sp', scond[:], cond_d[:, :], [], ['consts'], CK)
        B.dma('sp', stage[0][:, 0:256].rearrange("p (k n) -> p k n", k=2), glam_d.rearrange("k p n -> p k n"), [], ['consts'], CK)
        B.dma('sp', stage[0][:, 256:768].rearrange("p (k n) -> p k n", k=4), attm_d.rearrange("k p n -> p k n"), [], ['consts'], CK)
        B.copy('dve', ident16[:], identf[:], ['consts'], ['ident16'])
        B.copy('dve', glam16[:].rearrange("p a b -> p (a b)"), stage[0][:, 0:256], ['consts'], ['glam16', 'stage0'])
        B.copy('dve', attm16[:].rearrange("p a b -> p (a b)"), stage[0][:, 256:768], ['consts'], ['attm16', 'stage0'])
        B.act(scond[:], scond[:], AF.Silu, ['consts'], ['scond'])
        B.copy('dve', scond16[:], scond[:], ['scond'], ['scond16'])
        B.memset('dve', ones_col[:], 1.0, ['ones_col'])
        ctxb = flags[:, 0:1]
        sf = flags[:, 2:3]
        npf = flags[:, 3:4]
        FLG = 'consts'

        def load_cast_weight(dst_fn, src_fn, nk, width, rkey):
            n = [0]
            for kc in range(nk):
                for c0 in range(0, width, SW):
                    c1 = min(width, c0 + SW)
                    s = n[0] % 2
                    n[0] += 1
                    B.dma('sp', stage[s][:, 0:c1 - c0], src_fn(kc)[:, c0:c1], [], [f'stage{s}'], f'ld_st{s}')
                    B.copy('pool' if s == 0 else 'dve', dst_fn(kc)[:, c0:c1], stage[s][:, 0:c1 - c0], [f'stage{s}'], [rkey])

        def load_mod(dst, chunk, key, dkey):
            B.dma('sp', dst[:], mod_s[0:1, chunk * D:(chunk + 1) * D].partition_broadcast(P), ['mod_s'], [key], dkey)

        for l in range(L):
            x_src = x_d if l == 0 else xs_s
            B.nmod = 8
            S.barrier()
            for j in range(4):
                B.dma('pool', wmod16[:, 2 * j:2 * j + 2, :],
                      wmod_d[l, 2 * j * P:(2 * j + 2) * P, :].rearrange("(k p) n -> p k n", p=P), [], [f'wmod{j}'], f'ld_W{j}')
            B.dma('pool', wout16[:], wout_d[l].rearrange("(k p) n -> p k n", p=P), [], ['wout16'], 'ld_wout')
            for part in range(3):
                row = stage[0][0:1, 0:2048]
                rk_ = 'stage0'
                B.dma('sp', row, bmod_d[l][0:1, part * 2048:(part + 1) * 2048], [], [rk_], 'ld_st0')
                banks = [B.bank() for _ in range(4)]
                for kc in range(8):
                    B.mm([(ps[banks[j]][0:1, :], scond16[:, kc:kc + 1], wmod16[:, kc, (part * 4 + j) * 512:(part * 4 + j + 1) * 512],
                           kc == 0, kc == 7) for j in range(4)], ['scond16', f'wmod{kc // 2}'], [PSK(b) for b in banks])
                for j in range(4):
                    B.tt('dve', row[:, j * 512:(j + 1) * 512], ps[banks[j]][0:1, :], row[:, j * 512:(j + 1) * 512], ALU.add,
                         [PSK(banks[j]), rk_], [rk_])
                B.dma('pool', mod_s[0:1, part * 2048:(part + 1) * 2048], row, [rk_], ['mod_s'], 'st_mod')
            S.ckpt(1)
            S.barrier()
            for j in range(2):
                B.dma('pool', win16[:, 4 * j:4 * j + 4, :],
                      win_d[l, 4 * j * P:(4 * j + 4) * P, :].rearrange("(k p) n -> p k n", p=P), [], ['win16'], f'ld_W{4 + j}')
            load_mod(M[0], 1, 'M0', 'ld_m0')
            B.dma('sp', tmp32[:], n1_d[l].partition_broadcast(P), [], ['tmp32'], 'ld_tmp')
            B.stt('dve', M[0][:], M[0][:], 1.0, tmp32[:], ALU.add, ALU.mult, ['M0', 'tmp32'], ['M0'])
            load_mod(M[1], 0, 'M1', 'ld_m1')
            B.dma('sp', gaug, gaug_d[l], [], ['gaug'], 'ld_s0')
            B.dma('sp', gnb[:], gnorm_d[l].partition_broadcast(P), [], ['gnb'], 'ld_s1')
            B.dma('sp', esink[:], sink_d[l].partition_broadcast(P), [], ['esink'], 'ld_s2')
            B.act(esink[:], esink[:], AF.Exp, ['esink'], ['esink'])
            B.dma('sp', lvec[:], lvec_d[l], [], ['lvec'], 'ld_s3')
            for ct in range(2):
                for dr in range(2):
                    lamc = lvec[:, ct * 11 + 5 + dr * 3 + 2: ct * 11 + 5 + dr * 3 + 3]
                    dc = lder[:, dr * 2 + ct: dr * 2 + ct + 1]
                    B.act(dc, lamc, AF.Exp, ['lvec'], ['lder'], scale=-1.0)
                    B.act(dc, dc, AF.Ln, ['lder'], ['lder'], bias=1.0)
                    B.ts('dve', dc, dc, -8.0, None, ALU.mult, None, ['lder'], ['lder'])
                B.ts('dve', lder[:, 4 + ct * 4: 8 + ct * 4], lvec[:, ct * 11: ct * 11 + 4], npf, None, ALU.mult, None,
                     ['lvec', FLG], ['lder'])
            B.memset('dve', stage[1][:, 0:1024], 0.0, ['stage1'])
            for g_, wsrc in enumerate((wa_d, wx_d)):
                for dr in range(2):
                    for n in range(4):
                        ct, hb_ = n // 2, n % 2
                        idx = g_ * 4 + dr * 2 + ct
                        B.dma('sp', wbd32[hb_ * 64:(hb_ + 1) * 64, idx, hb_ * 64:(hb_ + 1) * 64], wsrc[l, dr, n],
                              ['stage1'], ['wbd32'], 'ld_wbd')
            B.copy('dve', wbd16[:].rearrange("p a b -> p (a b)"), stage[1][:, 0:1024], ['wbd32', 'stage1'], ['wbd16', 'stage1'])
            B.memset('dve', cva[:].rearrange("p a b c -> p (a b c)"), 1.0, ['cva'])
            for c in range(4):
                B.dma('sp', stage[0][:, 0:128], ck_d[l, c * P:(c + 1) * P, :], [], ['stage0'], 'ld_st0')
                B.copy('dve', h16[:, 0:128], stage[0][:, 0:128], ['stage0'], ['h16'])
                b = B.bank()
                B.tr([(ps16[b][:, 0:128], h16[:, 0:128])], ident16[:], ['h16', 'ident16'], [PSK(b)])
                B.copy('act', ckT[:, c * P:(c + 1) * P], ps16[b][:, 0:128], [PSK(b)], ['ckT'])
                B.dma('sp', stage[1][:, 0:128], cv_d[l, c * P:(c + 1) * P, :], [], ['stage1'], 'ld_st1')
                B.copy('dve', cva[:, c, :, 0:64], stage[1][:, 0:128].rearrange("p (g d) -> p g d", g=2), ['stage1'], ['cva'])
            B.memset('pool', VAflat, 1.0, ['VAall'])
            B.memset('pool', zT33[32:33, :], 1.0, ['zT33'])

            S.ckpt(2)
            def norm_mod(xs_, xk, m_a, m_b, ka, kb, b, hT_=None, hk='hT'):
                hT_ = hT if hT_ is None else hT_
                B.act(h16[:], xs_[:], AF.Square, [xk], ['h16', 'st0'], accum=st4[:, 0:1])
                B.act(st4[:, 1:2], st4[:, 0:1], AF.Ln, ['st0'], ['st1'], bias=EPS, scale=1.0 / D)
                B.act(st4[:, 2:3], st4[:, 1:2], AF.Exp, ['st1'], ['st2'], scale=-0.5)
                B.stt('dve', tmp32[:], xs_[:], st4[:, 2:3], m_a[:], ALU.mult, ALU.mult, [xk, 'st2', ka], ['tmp32'])
                B.tt('dve', h16[:], tmp32[:], m_b[:], ALU.add, ['tmp32', kb], ['h16'])
                B.tr([(ps16[b][:, kc * P:(kc + 1) * P], h16[:, kc * P:(kc + 1) * P]) for kc in range(8)], ident16[:],
                     ['h16', 'ident16'], [PSK(b)])
                B.copy('act', hT_[:].rearrange("p a b -> p (a b)"), ps16[b][:, :], [PSK(b)], [hk])

            def norm_mod_g(xs_, xk, m_a, m_b, ka, kb, b, hT_, hk):
                B.act(h16[:], xs_[:], AF.Square, [xk], ['h16', 'st0'], accum=st4[:, 0:1])
                yield
                B.act(st4[:, 1:2], st4[:, 0:1], AF.Ln, ['st0'], ['st1'], bias=EPS, scale=1.0 / D)
                yield
                B.act(st4[:, 2:3], st4[:, 1:2], AF.Exp, ['st1'], ['st2'], scale=-0.5)
                yield
                B.stt('dve', tmp32[:], xs_[:], st4[:, 2:3], m_a[:], ALU.mult, ALU.mult, [xk, 'st2', ka], ['tmp32'])
                yield
                B.tt('dve', h16[:], tmp32[:], m_b[:], ALU.add, ['tmp32', kb], ['h16'])
                yield
                B.tr([(ps16[b][:, kc * P:(kc + 1) * P], h16[:, kc * P:(kc + 1) * P]) for kc in range(8)], ident16[:],
                     ['h16', 'ident16'], [PSK(b)])
                yield
                B.copy('act', hT_[:].rearrange("p a b -> p (a b)"), ps16[b][:, :], [PSK(b)], [hk])
                yield

            def attention(i):
                slot = i % 3
                kbs = [('c', c) for c in range(4)]
                if i > 0:
                    kbs.append(('p', i - 1))
                kbs.append(('o', i))
                if i < NT - 1:
                    kbs.append(('n', i + 1))
                ob = [6, 7]
                seq = [(g, n_, kind, j) for g in range(2) for n_, (kind, j) in enumerate(kbs)]

                def s_mm(idx):
                    g, n_, kind, j = seq[idx]
                    b = 4 + idx % 2
                    if kind == 'c':
                        ksrc, rk = ckT[64 * g:64 * g + 64, j * P:(j + 1) * P], 'ckT'
                    else:
                        ksrc, rk = KT[64 * g:64 * g + 64, j * P:(j + 1) * P], f'KT{j}'
                    B.mm([(ps[b][:, :], ksrc, QT[64 * g:64 * g + 64, slot, :, :].rearrange("p a b -> p (a b)"), True, True)],
                         [rk, f'QT{slot}'], [PSK(b)])

                s_mm(0)
                for idx, (g, n_, kind, j) in enumerate(seq):
                    if idx + 1 < len(seq):
                        s_mm(idx + 1)
                    b = 4 + idx % 2
                    pt = PT[idx % 3]
                    ptk = f'PT{idx % 3}'
                    if kind == 'c':
                        vsrc, rv = cva[:, j, g, :], 'cva'
                        B.act(pt[:], ps[b][:, :], AF.Exp, [PSK(b), FLG], [ptk], bias=ctxb, scale=0.125)
                    else:
                        vsrc, rv = VA[:, j, g, :], f'VA{j}'
                        B.act(pt[:], ps[b][:, :], AF.Exp, [PSK(b)], [ptk], scale=0.125)
                    if kind in ('p', 'n'):
                        mi = (0 if kind == 'p' else 2) + (i % 2)
                        B.tt('dve', h4(pt[:]), h4(pt[:]), attm16[:, mi, :].unsqueeze(1).to_broadcast([P, 4, P]), ALU.mult,
                             [ptk, 'attm16'], [ptk])
                    B.mm([(ps[ob[g]][:, a * 65:(a + 1) * 65], pt[:, a * P:(a + 1) * P], vsrc,
                           n_ == 0 and a == 0, n_ == len(kbs) - 1 and a == 3, True) for a in range(4)],
                         [ptk, rv], [PSK(ob[g])])
                    yield
                for g in range(2):
                    o3 = h4(ps[ob[g]][:, 0:260])
                    B.tt('dve', den[:, 4 * g:4 * g + 4], o3[:, :, 64], esink[:, 4 * g:4 * g + 4], ALU.add,
                         [PSK(ob[g]), 'esink'], ['den'])
                B.recip(rec[:], den[:], ['den'], ['rec'])
                for g in range(2):
                    o3 = h4(ps[ob[g]][:, 0:260])
                    B.tt('dve', h4(o16[:, 256 * g:256 * (g + 1)]), o3[:, :, 0:64],
                         rec[:, 4 * g:4 * g + 4].unsqueeze(2).to_broadcast([P, 4, 64]), ALU.mult,
                         [PSK(ob[g]), 'rec'], ['o16'])
                yield
                b = 4
                B.tr([(ps16[b][:, c * P:(c + 1) * P], o16[:, c * P:(c + 1) * P]) for c in range(4)], ident16[:],
                     ['o16', 'ident16'], [PSK(b)])
                so = i % 2
                B.copy('act', oT16[so][:].rearrange("p a b -> p (a b)"), ps16[b][:, 0:512], [PSK(b)], [f'oT16_{so}'])
                B.dma('pool', mixT_s[i, :, 0:512], oT16[so][:].rearrange("p a b -> p (a b)"),
                      [f'oT16_{so}'], [f'mixA{i}'], f'st_oT{so}')
                yield

            def load_x_a(t):
                s2 = t % 2
                B.dma('sp', xt[s2][:], x_src[t * P:(t + 1) * P, :], [f'xs{t}'] if l > 0 else [], [f'xt{s2}'], f'ld_xt{s2}')

            def load_rope(t):
                s2 = t % 2
                B.dma('sp', ropet[s2][:], rope_d[t], [], [f'rope{s2}'], f'ld_rope{s2}')

            def proj_q(t):
                s2 = t % 2
                hT_, hk = hTs[s2], f'hT{s2}'
                bq = 0
                if t + 2 < NT:
                    load_x_a(t + 2)
                if t + 1 < NT:
                    load_rope(t + 1)
                for j in range(4):
                    B.mm([(ps[bq][:, j * P:(j + 1) * P], win16[:, kc, QC + j * P:QC + (j + 1) * P], hT_[:, kc, :], kc == 0, kc == 7)
                          for kc in range(8)], ['win16', hk], [PSK(bq)])
                    if j == 1:
                        yield
                yield
                B.copy('act', qk32[:, 0:4, :].rearrange("p a b -> p (a b)"), ps[bq][:, :], [PSK(bq)], ['q32'])
                yield
                B.mm([(ps[bq][:, j * P:(j + 1) * P], perm[:], qk32[:, j, :], True, True) for j in range(4)],
                     ['consts', 'q32'], [PSK(bq)])
                yield
                B.tt('dve', rt1[:, 0:4, :], qk32[:, 0:4, :], ropet[s2][:, 0, :].unsqueeze(1).to_broadcast([P, 4, P]), ALU.mult,
                     ['q32', f'rope{s2}'], ['rt1q'])
                yield
                B.tt('dve', rt2[:, 0:4, :], h4(ps[bq][:, :]),
                     ropet[s2][:, 1, :].unsqueeze(1).to_broadcast([P, 4, P]), ALU.mult, [PSK(bq), f'rope{s2}'], ['rt2q'])
                yield
                B.tt('dve', QT[:, t % 3, :, :], rt1[:, 0:4, :], rt2[:, 0:4, :], ALU.add, ['rt1q', 'rt2q'], [f'QT{t % 3}'])
                yield

            def proj_x(t):
                s2 = t % 2
                hT_, hk = hTs[s2], f'hT{s2}'
                bk = 1
                B.mm([(ps[bk][:, 0:P], win16[:, kc, KF:KF + P], hT_[:, kc, :], kc == 0, kc == 7) for kc in range(8)],
                     ['win16', hk], [PSK(bk)])
                yield
                B.copy('act', qk32[:, 4, :], ps[bk][:, 0:P], [PSK(bk)], ['k32'])
                yield
                B.mm([(ps[bk][:, P:2 * P], perm[:], qk32[:, 4, :], True, True)]
                     + [(ps[bk][:, 2 * P:4 * P], hT_[:, kc, :], win16[:, kc, KVO:KVO + 256], kc == 0, kc == 7) for kc in range(8)],
                     ['win16', hk, 'consts', 'k32'], [PSK(bk)])
                yield
                B.tt('dve', rt1[:, 4, :], qk32[:, 4, :], ropet[s2][:, 0, :], ALU.mult, ['k32', f'rope{s2}'], ['rt1k'])
                yield
                B.tt('dve', rt2[:, 4, :], ps[bk][:, P:2 * P], ropet[s2][:, 1, :], ALU.mult, [PSK(bk), f'rope{s2}'], ['rt2k'])
                yield
                B.copy('act', kv32[s2][:], ps[bk][:, 2 * P:4 * P], [PSK(bk)], [f'kv32_{s2}'])
                B.copy('dve', VA[:, t, :, 0:64], ps[bk][:, 3 * P:4 * P].rearrange("p (g d) -> p g d", g=2),
                       [PSK(bk), 'VAall'], [f'VA{t}'])
                yield
                B.tt('dve', KT[:, t * P:(t + 1) * P], rt1[:, 4, :], rt2[:, 4, :], ALU.add, ['rt1k', 'rt2k'], [f'KT{t}'])
                B.dma('pool', kout_d[l, t * P:(t + 1) * P, :], kv32[s2][:, 0:128], [f'kv32_{s2}'], [], f'st_kv{s2}')
                B.dma('pool', vout_d[l, t * P:(t + 1) * P, :], kv32[s2][:, 128:256], [f'kv32_{s2}'], [], f'st_kv{s2}')
                yield
                if t + 1 < NT:
                    s3 = (t + 1) % 2
                    yield from norm_mod_g(xt[s3], f'xt{s3}', M[0], M[1], 'M0', 'M1', 1, hTs[s3], f'hT{s3}')

            def proj_y(t):
                s2 = t % 2
                hT_, hk = hTs[s2], f'hT{s2}'
                bqk, bz = 2, 3
                B.mm([(ps[bz][0:32, 0:P], win16[:, kc, GZO:GZO + 32], hT_[:, kc, :], kc == 0, kc == 7) for kc in range(8)],
                     ['win16', hk], [PSK(bz)])
                yield
                B.copy('act', zT33[0:32, :], ps[bz][0:32, 0:P], [PSK(bz)], ['zT33'])
                yield
                B.mm([(ps[bz][:, :], zT33[0:33, :], gaug, True, True)], ['zT33', 'gaug'], [PSK(bz)])
                assert S.last_w.get(f'lxg{s2}') is not None
                B.mm([(ps[bqk][:, :], hT_[:, kc, :], win16[:, kc, GQKO:GQKO + 512], kc == 0, kc == 7) for kc in range(8)],
                     ['win16', hk], [PSK(bqk)])
                yield
                B.act(sp32[:], ps[bz][:, :], AF.Exp, [PSK(bz)], ['sp32'], scale=-1.0)
                B.copy('dve', gqk32[:], ps[bqk][:, :], [PSK(bqk)], ['gqk32'])
                yield
                B.act(sp32[:], sp32[:], AF.Ln, ['sp32'], ['sp32'], bias=1.0)
                yield
                yield from proj_dir(t, 0, bz)

            def proj_dir(t, dr, b):
                s2 = t % 2
                hT_, hk = hTs[s2], f'hT{s2}'
                B.mm([(ps[b][:, 0:256], tri[:, 2 * dr, :], sp32[:, 256 * dr:256 * (dr + 1)], True, True),
                      (ps[b][:, 256:512], tri[:, 2 * dr + 1, :], sp32[:, 256 * dr:256 * (dr + 1)], True, True)],
                     ['consts', 'sp32'], [PSK(b)])
                yield
                B.act(EqEe[dr][:], ps[b][:, :], AF.Exp, [PSK(b)], [f'EqEe{dr}'], scale=-1.0 / 16)
                yield
                B.act(Ek[dr][:], ps[b][:, 0:256], AF.Exp, [PSK(b)], [f'Ek{dr}'], scale=1.0 / 16)
                yield
                qd, kd = qin16[:, 256 * dr:256 * (dr + 1)], kin16[:, 256 * dr:256 * (dr + 1)]
                B.stt('dve', qd, gqk32[:, 0:256], 0.125, EqEe[dr][:, 0:256], ALU.mult, ALU.mult, ['gqk32', f'EqEe{dr}'], [f'qin16_{dr}'])
                yield
                B.tt('dve', kd, gqk32[:, 256:512], Ek[dr][:], ALU.mult, ['gqk32', f'Ek{dr}'], [f'kin16_{dr}'])
                yield
                B.tt('dve', ke16[s2][:, 256 * dr:256 * (dr + 1)], gqk32[:, 256:512], EqEe[dr][:, 256:512], ALU.mult,
                     ['gqk32', f'EqEe{dr}'], [f'ke16_{s2}_{dr}'])
                B.dma('pool', ke_s[t, :, 256 * dr:256 * (dr + 1)], ke16[s2][:, 256 * dr:256 * (dr + 1)],
                      [f'ke16_{s2}_{dr}'], [f'ke_s{t}_{dr}'], f'st_ke{s2}{dr}')
                specs = []
                for hp in range(2):
                    specs.append((ps16[b][:, hp * P:(hp + 1) * P], qin16[:, 256 * dr + hp * P:256 * dr + (hp + 1) * P]))
                    specs.append((ps16[b][:, (2 + hp) * P:(3 + hp) * P], kin16[:, 256 * dr + hp * P:256 * dr + (hp + 1) * P]))
                B.tr(specs, ident16[:], [f'qin16_{dr}', f'kin16_{dr}', 'ident16'], [PSK(b)])
                yield
                B.copy('act', qkT16[s2][:, 4 * dr:4 * dr + 4, :].rearrange("p a b -> p (a b)"), ps16[b][:, 0:512], [PSK(b)], [f'qkT16_{s2}_{dr}'])
                B.dma('pool', qkT_s[t, :, 512 * dr:512 * (dr + 1)], qkT16[s2][:, 4 * dr:4 * dr + 4, :].rearrange("p a b -> p (a b)"),
                      [f'qkT16_{s2}_{dr}'], [f'qkT_s{t}_{dr}'], f'st_qkT{s2}{dr}')
                yield
                if dr == 0:
                    B.mm([(ps[b][:, d2 * 2 + hp:d2 * 2 + hp + 1], sp32[:, 256 * d2 + hp * P:256 * d2 + (hp + 1) * P], ones_col[:, :], True, True)
                          for d2 in range(2) for hp in range(2)], ['sp32', 'ones_col'], [PSK(b)])
                    yield
                    B.act(dec32[s2][:], ps[b][:, 0:4], AF.Exp, [PSK(b)], [f'dec{s2}'], scale=-1.0 / 16)
                    B.dma('pool', dec_s[t], dec32[s2][:], [f'dec{s2}'], [f'dec_s{t}'], f'st_dec{s2}')
                    yield
                else:
                    B.mm([(ps[b][:, :], hT_[:, kc, :], win16[:, kc, GVRO:GVRO + 512], kc == 0, kc == 7) for kc in range(8)],
                         ['win16', hk], [PSK(b)])
                    yield
                    B.copy('act', v16[s2][:], ps[b][:, 0:256], [PSK(b)], [f'v16_{s2}'])
                    B.dma('pool', v16_s[t], v16[s2][:], [f'v16_{s2}'], [f'v16_s{t}'], f'st_v16{s2}')
                    yield
                    B.act(sgr32[s2][:], ps[b][:, 256:512], AF.Exp, [PSK(b)], [f'sgr{s2}'], scale=-1.0)
                    yield
                    B.ts('dve', sgr32[s2][:], sgr32[s2][:], 1.0, None, ALU.add, None, [f'sgr{s2}'], [f'sgr{s2}'])
                    yield
                    B.recip(sgr32[s2][:], sgr32[s2][:], [f'sgr{s2}'], [f'sgr{s2}'])
                    yield
                    B.tt('dve', sgr32[s2][:], ps[b][:, 256:512], sgr32[s2][:], ALU.mult, [PSK(b), f'sgr{s2}'], [f'sgr{s2}'])
                    yield
                    B.tt('pool', sgr32[s2][:], sgr32[s2][:], gnb[:], ALU.mult, [f'sgr{s2}', 'gnb'], [f'sgr{s2}'])
                    B.dma('pool', sgr_s[t], sgr32[s2][:], [f'sgr{s2}'], [f'sgr_s{t}'], f'st_sgr{s2}')
                    yield

            def proj_y1(t):
                s2 = t % 2
                hT_, hk = hTs[s2], f'hT{s2}'
                for j, co in enumerate((LXO, LXO + P, LGO, LGO + P)):
                    B.mm([(ps[2][:, j * P:(j + 1) * P], win16[:, kc, co:co + P], hT_[:, kc, :], kc == 0, kc == 7)
                          for kc in range(8)], ['win16', hk], [PSK(2)])
                yield
                B.copy('dve', lxg[s2][:].rearrange("p a b -> p (a b)"), ps[2][:, :], [PSK(2)], [f'lxg{s2}'])
                B.dma('pool', lxg_s[:, :, t * P:(t + 1) * P].rearrange("c p n -> p c n"), lxg[s2][:],
                      [f'lxg{s2}'], ['lxg_s'], f'st_lxg{s2}')
                yield
                for _ in range(3):
                    yield
                assert S.last_w.get('gqk32') is not None
                yield from proj_dir(t, 1, 2)

            def interleave(gens, weights=None):
                gens = list(gens)
                weights = weights or [1] * len(gens)
                alive = [True] * len(gens)
                while any(alive):
                    for gi, g_ in enumerate(gens):
                        for _ in range(weights[gi]):
                            if alive[gi]:
                                try:
                                    next(g_)
                                except StopIteration:
                                    alive[gi] = False

            load_x_a(0)
            load_x_a(1)
            load_rope(0)
            norm_mod(xt[0], 'xt0', M[0], M[1], 'M0', 'M1', 0, hTs[0], 'hT0')
            for t in range(NT + 2):
                gs = []
                if t < NT:
                    gs.append(proj_y(t))
                    gs.append(proj_y1(t))
                    gs.append(proj_x(t))
                    gs.append(proj_q(t))
                if t >= 2:
                    gs.append(attention(t - 2))
                interleave(gs)
                if t == 0:
                    S.ckpt(3)
            S.ckpt(5)

            S.barrier()
            for f4 in range(W2PRE // 4, 8):
                B.dma('pool', w2_16[:, 4 * f4:4 * f4 + 4, :],
                      w2_d[l, 4 * f4 * P:(4 * f4 + 4) * P, :].rearrange("(k p) n -> p k n", p=P), [], [f'w2_{f4}'], f'ld_V{f4}')
            for dr in range(2):
                B.dma('sp', S32[dr][:], sgla_d[l, dr], [], [f'S32_{dr}'], f'ld_S{dr}')
                B.copy('dve', S16[dr][:].rearrange("p a b -> p (a b)"), S32[dr][:].rearrange("p a b -> p (a b)"),
                       [f'S32_{dr}'], [f'S16_{dr}'])

            def gla_load(dr, t, slot):
                k_ = f'ld_g{dr}{slot}'
                B.dma('sp', gq_ld[dr][slot][:].rearrange("p a b -> p (a b)"), qkT_s[t, :, dr * 512:(dr + 1) * 512],
                      [f'qkT_s{t}_{dr}'], [f'gq{dr}{slot}'], k_)
                B.dma('sp', gke_ld[dr][slot][:], ke_s[t, :, dr * 256:(dr + 1) * 256], [f'ke_s{t}_{dr}'], [f'gke{dr}{slot}'], k_)
                B.dma('sp', gv_ld[dr][slot][:], v16_s[t], [f'v16_s{t}'], [f'gv{dr}{slot}'], k_)
                B.dma('sp', gdec_ld[dr][slot][:], dec_s[t, :, dr * 2:(dr + 1) * 2], [f'dec_s{t}'], [f'gdec{dr}{slot}'], k_)

            def gla_final(t, b, si):
                assert S.last_w.get(f'o_s0_{t}', -1) >= b_start and S.last_w.get(f'o_s1_{t}', -1) >= b_start, t
                F_ = FS[si]
                fo, fsq, fy, fy16, fyT, st_ = F_['fo'], F_['fsq'], F_['fy'], F_['fy16'], F_['fyT'], F_['st']
                n_ = lambda k: f'{k}_f{si}'
                k_ = f'ld_fo{si}'
                B.dma('sp', fo[0][:], o_s[0, t], [f'o_s0_{t}'], [n_('fo0')], k_)
                B.dma('sp', fo[1][:], o_s[1, t], [f'o_s1_{t}'], [n_('fo1')], k_)
                B.dma('sp', fo[2][:], sgr_s[t], [f'sgr_s{t}'], [n_('fo2')], k_)
                fr = [n_('fo0'), n_('fo1'), n_('fo2')]
                yield
                B.tt('dve', fy[:], fo[0][:], fo[1][:], ALU.add, fr, [n_('fy')])
                yield
                B.act(fsq[:], fy[:], AF.Square, [n_('fy')], [n_('fsq')])
                yield
                B.reduce_sum(st_[:, 0:4], h4(fsq[:]), [n_('fsq')], [n_('st')])
                yield
                B.act(st_[:, 0:4], st_[:, 0:4], AF.Ln, [n_('st')], [n_('st')], bias=EPS, scale=1.0 / 64)
                yield
                B.act(st_[:, 0:4], st_[:, 0:4], AF.Exp, [n_('st')], [n_('st')], scale=-0.5)
                yield
                B.tt('dve', h4(fy[:]), h4(fy[:]), st_[:, 0:4].unsqueeze(2).to_broadcast([P, 4, 64]), ALU.mult, [n_('fy'), n_('st')], [n_('fy')])
                yield
                B.tt('dve', fy16[:], fy[:], fo[2][:], ALU.mult, [n_('fy')] + fr, [n_('fy16')])
                yield
                B.tr([(ps16[b][:, c * P:(c + 1) * P], fy16[:, c * P:(c + 1) * P]) for c in range(2)], ident16[:],
                     [n_('fy16'), 'ident16'], [PSK(b)])
                yield
                B.copy('act', fyT[:].rearrange("p a b -> p (a b)"), ps16[b][:, 0:256], [PSK(b)], [n_('fyT')])
                B.dma('pool', mixT_s[t, :, 512:768], fyT[:].rearrange("p a b -> p (a b)"), [n_('fyT')], [f'mixG{t}'], f'st_fyT{si}')
                yield

            fin_done = set()
            b_start = len(S.ins)

            def fin_pop():
                for t_ in range(NT):
                    if (t_ not in fin_done and S.last_w.get(f'o_s0_{t_}', -1) >= b_start
                            and S.last_w.get(f'o_s1_{t_}', -1) >= b_start):
                        fin_done.add(t_)
                        return t_
                return None

            def gla(dr):
                bA, bB = 2 * dr, 2 * dr + 1
                order_ = order[dr]
                gla_load(dr, order_[0], 0)
                for step in range(NT):
                    t = order_[step]
                    slot = step % 2
                    if step + 1 < NT:
                        gla_load(dr, order_[step + 1], (step + 1) % 2)
                    qk = gq_ld[dr][slot]
                    ke_ = gke_ld[dr][slot]
                    v_ = gv_ld[dr][slot]
                    dc_ = gdec_ld[dr][slot]
                    rk = [f'gq{dr}{slot}', f'gke{dr}{slot}', f'gv{dr}{slot}', f'gdec{dr}{slot}']
                    seq_start = (t % 2 == 0) if dr == 0 else (t % 2 == 1)
                    s32f = S32[dr][:].rearrange("p a b -> p (a b)")
                    s16f = S16[dr][:].rearrange("p a b -> p (a b)")
                    if seq_start and step > 0:
                        B.ts('dve', s32f, s32f, sf, None, ALU.mult, None, [f'S32_{dr}', FLG], [f'S32_{dr}'])
                        B.copy('dve', s16f, s32f, [f'S32_{dr}'], [f'S16_{dr}'])
                        yield
                    ba = [bA, bB]
                    B.mm([(ps[ba[h % 2]][:, (h // 2) * P:(h // 2 + 1) * P], qk[64 * (h % 2):64 * (h % 2) + 64, 2 + h // 2, :],
                           qk[64 * (h % 2):64 * (h % 2) + 64, h // 2, :], True, True) for h in range(4)],
                         rk, [PSK(bA), PSK(bB)])
                    yield
                    at4 = h4(AT16[dr][:])
                    for hh in range(2):
                        B.tt('dve', at4[:, hh:4:2, :], h4(ps[ba[hh]][:, 0:256], 2),
                             glam16[:, dr, :].unsqueeze(1).to_broadcast([P, 2, P]), ALU.mult,
                             [PSK(ba[hh]), 'glam16'], [f'AT16_{dr}'])
                        yield
                    specs = []
                    for h in range(4):
                        specs.append((ps[bA][:, h * 64:(h + 1) * 64], AT16[dr][:, h * P:(h + 1) * P], v_[:, h * 64:(h + 1) * 64], True, False))
                        specs.append((ps[bA][:, h * 64:(h + 1) * 64], qk[64 * (h % 2):64 * (h % 2) + 64, h // 2, :],
                                      S16[dr][64 * (h % 2):64 * (h % 2) + 64, h // 2, :], False, True))
                    B.mm(specs, [f'AT16_{dr}', f'S16_{dr}'] + rk, [PSK(bA)])
                    B.mm([(ps[bB][:, hp * P:(hp + 1) * P], ke_[:, hp * P:(hp + 1) * P], v_[:, hp * P:(hp + 1) * P], True, True)
                          for hp in range(2)], rk, [PSK(bB)])
                    yield
                    os_ = o32[dr][slot]
                    B.copy('act', os_[:], ps[bA][:, 0:256], [PSK(bA)], [f'o32_{dr}{slot}'])
                    B.dma('pool', o_s[dr, t], os_[:], [f'o32_{dr}{slot}'], [f'o_s{dr}_{t}'], f'st_o{dr}{slot}')
                    yield
                    for hp in range(2):
                        for hh in range(2):
                            B.stt('dve', S32[dr][64 * hh:64 * hh + 64, hp, :], S32[dr][64 * hh:64 * hh + 64, hp, :],
                                  dc_[64 * hh:64 * hh + 64, hp:hp + 1], ps[bB][64 * hh:64 * hh + 64, hp * P + hh * 64: hp * P + hh * 64 + 64],
                                  ALU.mult, ALU.add, [f'S32_{dr}', PSK(bB)] + rk, [f'S32_{dr}'])
                        yield
                    B.copy('dve', s16f, s32f, [f'S32_{dr}'], [f'S16_{dr}'])
                    seq_end = (t % 2 == 1) if dr == 0 else (t % 2 == 0)
                    if seq_end:
                        sq = t // 2
                        B.dma('pool', gst_d[sq, l, dr].rearrange("(hp hh) d v -> (hh d) hp v", hp=2), S32[dr][:],
                              [f'S32_{dr}'], [], f'st_gst{dr}')
                    yield
                    tt_ = fin_pop()
                    if tt_ is not None:
                        yield from gla_final(tt_, bA, dr)
                while len(fin_done) < NT:
                    tt_ = fin_pop()
                    if tt_ is None:
                        yield
                    else:
                        yield from gla_final(tt_, bA, dr)

            def lru(ct):
                L_ = LT[ct]
                hf, lxc, xc, xc16, rg, ig, av, bt = (L_[k] for k in ('hf', 'lxc', 'xc', 'xc16', 'rg', 'ig', 'av', 'bt'))
                hb, lgc, y16, stc, sto = (L_[k] for k in ('hb', 'lgc', 'y16', 'stc', 'sto'))
                br_, bi_ = 4 + 2 * ct, 5 + 2 * ct
                n_ = lambda k: f'{k}_{ct}'
                wv = lambda k: lvec[:, ct * 11 + k: ct * 11 + k + 1]
                nw = lambda k: lder[:, 4 + ct * 4 + k: 4 + ct * 4 + k + 1]

                def conv(c, slot):
                    c0 = c * CH
                    lo = max(0, c0 - 2)
                    hi = min(T, c0 + CH + 1)
                    lt = lxc[slot]
                    lk = n_(f'lxc{slot}')
                    if c == 0:
                        B.memset('pool', lt[:, 0:2], 0.0, [lk])
                    if c == NCH - 1:
                        B.memset('pool', lt[:, CH + 2:CH + 4], 0.0, [lk])
                    B.dma('sp', lt[:, 2 - (c0 - lo):2 - (c0 - lo) + (hi - lo)], lxg_s[ct, :, lo:hi], ['lxg_s'], [lk], f'ld_lxc{ct}{slot}')
                    yield
                    B.ts('pool', xc[:], lt[:, 2:2 + CH], wv(2), wv(4), ALU.mult, ALU.add, [lk, 'lvec'], [n_('xc')])
                    yield
                    for k in (0, 1, 3):
                        B.stt('dve', xc[:], lt[:, k:k + CH], wv(k), xc[:], ALU.mult, ALU.add, [lk, 'lvec', n_('xc')], [n_('xc')])
                        yield
                    for (tt_, k) in ((0, 0), (0, 1), (1, 0), (255, 3)):
                        src = lt[:, tt_ + k: tt_ + k + 257: 256]
                        dst = xc[:, tt_: tt_ + 257: 256]
                        B.stt('dve', dst, src, nw(k), dst, ALU.mult, ALU.add, [lk, 'lder', n_('xc')], [n_('xc')])
                    yield
                    B.copy('act', xc16[:], xc[:], [n_('xc')], [n_('xc16')])
                    yield

                def gates(dr, resets):
                    B.mm([(ps[br_][:, :], wbd16[:, 0 * 4 + dr * 2 + ct, :], xc16[:], True, True)], ['wbd16', n_('xc16')], [PSK(br_)])
                    B.mm([(ps[bi_][:, :], wbd16[:, 1 * 4 + dr * 2 + ct, :], xc16[:], True, True)], ['wbd16', n_('xc16')], [PSK(bi_)])
                    yield
                    vb = ct * 11 + 5 + dr * 3
                    B.act(rg[:], ps[br_][:, :], AF.Sigmoid, [PSK(br_), 'lvec'], [n_('rg')], bias=lvec[:, vb:vb + 1])
                    yield
                    B.act(ig[:], ps[bi_][:, :], AF.Sigmoid, [PSK(bi_), 'lvec'], [n_('ig')], bias=lvec[:, vb + 1:vb + 2])
                    yield
                    B.act(av[:], rg[:], AF.Exp, [n_('rg'), 'lder'], [n_('av')], scale=lder[:, dr * 2 + ct: dr * 2 + ct + 1])
                    yield
                    B.act(rg[:], av[:], AF.Square, [n_('av')], [n_('rg')])
                    yield
                    B.act(rg[:], rg[:], AF.Sqrt, [n_('rg')], [n_('rg')], bias=1.0, scale=-1.0)
                    yield
                    B.tt('dve', ig[:], ig[:], xc[:], ALU.mult, [n_('ig'), n_('xc')], [n_('ig')])
                    yield
                    B.tt('dve', bt[:], rg[:], ig[:], ALU.mult, [n_('rg'), n_('ig')], [n_('bt')])
                    for r0 in resets:
                        d_ = av[:, r0:r0 + 257:256]
                        B.ts('dve', d_, d_, sf, None, ALU.mult, None, [n_('av'), FLG], [n_('av')])
                    yield

                for c in range(NCH):
                    yield from conv(c, c % 2)
                    yield from gates(0, [0])
                    init = lruh0[:, (l * 2 + 0) * 2 + ct:(l * 2 + 0) * 2 + ct + 1] if c == 0 else hf[:, c * CH - 1:c * CH]
                    B.scan(hf[:, c * CH:(c + 1) * CH], av[:], bt[:], init, [n_('av'), n_('bt'), 'consts', n_('hf')], [n_('hf')])
                    yield
                B.copy('dve', stc[:, 0, :], hf[:, 255:T:256], [n_('hf')], [n_('stc')])
                for ci, c in enumerate(range(NCH - 1, -1, -1)):
                    s2 = ci % 2
                    yield from conv(c, c % 2)
                    yield from gates(1, [255])
                    B.dma('sp', lgc[s2][:], lxg_s[2 + ct, :, c * CH:(c + 1) * CH], ['lxg_s'], [n_(f'lgc{s2}')], f'ld_lgc{ct}{s2}')
                    init = lruh0[:, (l * 2 + 1) * 2 + ct:(l * 2 + 1) * 2 + ct + 1] if ci == 0 else hb[1 - s2][:, 0:1]
                    B.scan(hb[s2][:, ::-1], av[:, ::-1], bt[:, ::-1], init,
                           [n_('av'), n_('bt'), 'consts', n_(f'hb{1 - s2}')], [n_(f'hb{s2}')])
                    yield
                    B.copy('dve', stc[:, 1, 2 * c:2 * c + 2], hb[s2][:, 0:257:256], [n_(f'hb{s2}')], [n_('stc')])
                    g_ = lgc[s2]
                    gk = n_(f'lgc{s2}')
                    B.tt('dve', av[:], hf[:, c * CH:(c + 1) * CH], hb[s2][:], ALU.add, [n_('hf'), n_(f'hb{s2}'), n_('av')], [n_('av')])
                    yield
                    B.tt('pool', bt[:], g_[:], g_[:], ALU.mult, [gk, n_('bt')], [n_('bt')])
                    yield
                    B.ts('pool', bt[:], bt[:], 0.044715, 1.0, ALU.mult, ALU.add, [n_('bt')], [n_('bt')])
                    yield
                    B.tt('pool', bt[:], bt[:], g_[:], ALU.mult, [n_('bt'), gk], [n_('bt')])
                    yield
                    B.act(bt[:], bt[:], AF.Sigmoid, [n_('bt')], [n_('bt')], scale=1.5957691216057308)
                    B.tt('dve', av[:], av[:], g_[:], ALU.mult, [n_('av'), gk], [n_('av')])
                    yield
                    B.tt('dve', y16[s2][:], av[:], bt[:], ALU.mult, [n_('av'), n_('bt')], [n_(f'y16_{s2}')])
                    B.dma('pool', mixT_s[4 * c:4 * c + 4, :, (6 + ct) * P:(7 + ct) * P].rearrange("t p n -> p t n"),
                          y16[s2][:].rearrange("p (t n) -> p t n", t=4), [n_(f'y16_{s2}')], [f'mixL{ct}'], f'st_y16{ct}{s2}')
                    yield
                for dr in range(2):
                    b = br_ if dr == 0 else bi_
                    B.mm([(ps[b][0:16, 0:P], stc[:, dr, :], identf[:], True, True)], [n_('stc'), 'consts'], [PSK(b)])
                    B.copy('act', sto[dr][0:16, :], ps[b][0:16, 0:P], [PSK(b)], [n_(f'sto{dr}')])
                    B.dma('pool', lst_d[:, l, dr, ct * P:(ct + 1) * P], sto[dr][0:16, :], [n_(f'sto{dr}')], [], f'st_sto{ct}{dr}')
                    yield

            order = [list(range(NT)), list(range(NT - 1, -1, -1))]
            interleave([gla(0), gla(1), lru(0), lru(1)])

            S.ckpt(8)
            S.barrier()
            B.nmod = 6
            NG2 = W2PRE // 4
            for c8 in range(8):
                B.dma('pool', w1_16[:, :, c8 * 512:(c8 + 1) * 512],
                      w1_d[l, :, c8 * 512:(c8 + 1) * 512].rearrange("(k p) n -> p k n", p=P), [], [f'w1_{c8}'], f'ld_W{c8}')
                if c8 < NG2:
                    B.dma('pool', w2_16[:, 4 * c8:4 * c8 + 4, :],
                          w2_d[l, 4 * c8 * P:(4 * c8 + 4) * P, :].rearrange("(k p) n -> p k n", p=P), [], [f'w2_{c8}'], f'ld_V{c8}')
            last = (l == L - 1)
            load_mod(M[0], 2, 'M0', 'ld_m0')
            load_mod(M[1], 4, 'M1', 'ld_m1')
            B.dma('sp', tmp32[:], n2_d[l].partition_broadcast(P), [], ['tmp32'], 'ld_tmp')
            B.stt('dve', M[1][:], M[1][:], 1.0, tmp32[:], ALU.add, ALU.mult, ['M1', 'tmp32'], ['M1'])
            load_mod(M[2], 3, 'M2', 'ld_m2')
            load_mod(M[3], 5, 'M3', 'ld_m3')
            if last:
                fnb = stage[0][:, 0:D]
                B.dma('sp', fnb, fn_d.partition_broadcast(P), [], ['fnb', 'stage0'], 'ld_st0')
            S.ckpt(9)

            def front(t):
                s2 = t % 2
                xs_ = xt[s2]
                xk = f'xt{s2}'
                yield
                yield
                for hh in range(2):
                    b = hh
                    B.mm([(ps[b][:, :], mixTt[s2][:, kc, :], wout16[:, kc, hh * 512:(hh + 1) * 512], kc == 0, kc == 7)
                          for kc in range(8)], [f'mixTt{s2}', 'wout16'], [PSK(b)])
                    yield
                for hh in range(2):
                    b = hh
                    B.tt('dve', tmp32[:, hh * 512:(hh + 1) * 512], ps[b][:, :], M[0][:, hh * 512:(hh + 1) * 512], ALU.mult,
                         [PSK(b), 'M0'], ['tmp32'])
                    yield
                B.tt('dve', xs_[:], xs_[:], tmp32[:], ALU.add, [xk, 'tmp32'], [xk])
                yield
                yield from norm_mod_g(xs_, xk, M[1], M[2], 'M1', 'M2', 2, hTs[s2], f'hT{s2}')

            def mlp(t):
                s2 = t % 2
                xs_ = xt[s2]
                xk = f'xt{s2}'
                hT_ = hTs[s2]
                hk = f'hT{s2}'

                def mlp1(c8):
                    b = 3 + c8 % 2
                    B.mm([(ps[b][:, :], hT_[:, kc, :], w1_16[:, kc, c8 * 512:(c8 + 1) * 512], kc == 0, kc == 7)
                          for kc in range(8)], [f'w1_{c8}', hk], [PSK(b)])
                    B.act(rl32[:], ps[b][:, :], AF.Relu, [PSK(b)], ['rl32'])
                    B.tt('dve', u16[c8 % 2][:], rl32[:], rl32[:], ALU.mult, ['rl32'], [f'u16_{c8 % 2}'])

                def utr(c8):
                    b = 5
                    B.tr([(ps16[b][:, fi * P:(fi + 1) * P], u16[c8 % 2][:, fi * P:(fi + 1) * P]) for fi in range(4)], ident16[:],
                         [f'u16_{c8 % 2}', 'ident16'], [PSK(b)])
                    B.copy('act', uT[:, c8 % 2, :, :].rearrange("p a b -> p (a b)"), ps16[b][:, 0:512], [PSK(b)], [f'uT{c8 % 2}'])

                def mlp2(f4):
                    for hh in range(2):
                        B.mm([(ps[6 + hh][:, :], uT[:, f4 % 2, fi, :], w2_16[:, f4 * 4 + fi, hh * 512:(hh + 1) * 512],
                               f4 == 0 and fi == 0, f4 == 7 and fi == 3) for fi in range(4)],
                             [f'uT{f4 % 2}', f'w2_{f4}'], [PSK(6 + hh)])

                mlp1(0)
                for c8 in range(8):
                    if c8 + 1 < 8:
                        mlp1(c8 + 1)
                        yield
                    utr(c8)
                    if c8 >= 1:
                        mlp2(c8 - 1)
                    yield
                mlp2(7)
                for hh in range(2):
                    B.tt('dve', rl32[:], ps[6 + hh][:, :], M[3][:, hh * 512:(hh + 1) * 512], ALU.mult,
                         [PSK(6 + hh), 'M3'], ['rl32'])
                    B.tt('dve', xs_[:, hh * 512:(hh + 1) * 512], xs_[:, hh * 512:(hh + 1) * 512], rl32[:], ALU.add, [xk, 'rl32'], [xk])
                yield
                if not last:
                    B.dma('pool', xs_s[t * P:(t + 1) * P, :], xs_[:], [xk], [f'xs{t}'], f'st_x{s2}')
                else:
                    B.act(h16[:], xs_[:], AF.Square, [xk], ['h16', 'st0'], accum=st4[:, 0:1])
                    B.act(st4[:, 1:2], st4[:, 0:1], AF.Ln, ['st0'], ['st1'], bias=EPS, scale=1.0 / D)
                    B.act(st4[:, 2:3], st4[:, 1:2], AF.Exp, ['st1'], ['st2'], scale=-0.5)
                    B.stt('dve', xs_[:], xs_[:], st4[:, 2:3], fnb, ALU.mult, ALU.mult, [xk, 'st2', 'fnb'], [xk])
                    B.dma('pool', y_d[t * P:(t + 1) * P, :], xs_[:], [xk], [], f'st_x{s2}')
                yield

            def loads_c(t):
                s2 = t % 2
                B.dma('sp', xt[s2][:], x_src[t * P:(t + 1) * P, :], [f'xs{t}'] if l > 0 else [], [f'xt{s2}'], f'ld_xt{s2}')
                B.dma('sp', mixTt[s2][:].rearrange("p a b -> p (a b)"), mixT_s[t],
                      [f'mixA{t}', f'mixG{t}', 'mixL0', 'mixL1'], [f'mixTt{s2}'], f'ld_mix{s2}')

            loads_c(0)
            for t in range(NT + 1):
                gs = []
                if t >= 1:
                    gs.append(mlp(t - 1))
                if t < NT:
                    gs.append(front(t))
                interleave(gs)
                if t + 1 < NT:
                    loads_c(t + 1)
                if t == 1:
                    S.ckpt(10)
            S.ckpt(11)
        S.finalize()
        print("program: %d sched-instrs, eng counts %s, dma keys %d, phaseA carve end %d" %
              (len(S.ins), S.eng_count, len(S.dma_keys), offA), flush=True)
        eng_sems = {e: [es.enter_context(nc.semaphore(f"s_{e}{k}")) for k in range(S.n_eng_sems[e])] for e in COMPUTE}
        dma_sems = {k: es.enter_context(nc.semaphore(f"d_{k}")) for k in S.dma_keys}
        with nc.Block() as block:
            @block.sync
            def _(h):
                S.emit_engine('sp', h, eng_sems, dma_sems, final_wait_all_dma=True)

            @block.tensor
            def _(h):
                S.emit_engine('pe', h, eng_sems, dma_sems)

            @block.scalar
            def _(h):
                S.emit_engine('act', h, eng_sems, dma_sems)

            @block.vector
            def _(h):
                S.emit_engine('dve', h, eng_sems, dma_sems)

            @block.gpsimd
            def _(h):
                S.emit_engine('pool', h, eng_sems, dma_sems)
    return nc


def _consts():
    s = np.arange(P)[:, None]
    c = np.arange(P)[None, :]
    tri = np.stack([(s <= c), (s > c), (s >= c), (s < c)]).astype(np.float32)
    glam = np.stack([(s <= c), (s >= c)]).astype(np.float32)
    perm = np.zeros((P, P), np.float32)
    for dp in range(P):
        hd = dp % 64
        src = dp + 16 if (hd % 32) < 16 else dp - 16
        perm[src, dp] = 1.0
    return tri, glam, perm


def _rope_tables(enabled):
    tab = np.zeros((NT, P, 2, P), np.float32)
    if not enabled:
        tab[:, :, 0, :] = 1.0
        return tab
    t = np.arange(T)
    row = (t // 64).astype(np.float32)
    col = (t % 64).astype(np.float32)
    nf = 16
    inv = (np.float32(10000.0) ** (-np.arange(nf, dtype=np.float32) / nf)).astype(np.float32)
    cosT = np.zeros((64, T), np.float32)
    sinT = np.zeros((64, T), np.float32)
    for d in range(64):
        half, j = d // 32, d % 32
        pos = row if half == 0 else col
        ang = (pos * inv[j % 16]).astype(np.float32)
        cosT[d] = np.cos(ang)
        sinT[d] = -np.sin(ang) if j < 16 else np.sin(ang)
    cos2 = np.concatenate([cosT, cosT], 0).reshape(P, NT, P).transpose(1, 0, 2)
    sin2 = np.concatenate([sinT, sinT], 0).reshape(P, NT, P).transpose(1, 0, 2)
    tab[:, :, 0, :] = cos2
    tab[:, :, 1, :] = sin2
    return tab


_NC_CACHE = {}


def prepare(x_prompt, x_sample, c, cache_k, cache_v, state_gla, state_lru, c_ctx,
           w_mod, b_mod, norm1, norm2, w_in, attn_sink, gla_gate_w, gla_gate_b, gla_norm,
           lru_conv_w, lru_conv_b, lru_wa, lru_ba, lru_wx, lru_bx, lru_lambda,
           w_out, w_mlp1, w_mlp2, final_norm):
    f = np.float32
    A = lambda a: np.ascontiguousarray(np.asarray(a, dtype=f))
    x_prompt, x_sample, c, cache_k, cache_v = A(x_prompt), A(x_sample), A(c), A(cache_k), A(cache_v)
    state_gla, state_lru, c_ctx = A(state_gla), A(state_lru), A(c_ctx)
    w_in = A(w_in)
    qcols = []
    for cc in range(4):
        qcols += list(range(cc * 64, cc * 64 + 64)) + list(range((4 + cc) * 64, (4 + cc) * 64 + 64))
    cols = (qcols + list(range(512, 640)) + list(range(1824, 2080)) + list(range(2080, 2336)) + list(range(1536, 1568))
            + list(range(512, 640)) + list(range(640, 768)) + list(range(768, 1024)) + list(range(1024, 1280))
            + list(range(1280, 1536)) + list(range(1568, 1824)))
    assert len(cols) == WIN
    w_in_r = np.ascontiguousarray(w_in[:, :, cols])
    gaug = np.zeros((L, 33, 512), f)
    gw, gb = A(gla_gate_w), A(gla_gate_b)
    gaug[:, 0:16, 0:256] = gw[:, 0]
    gaug[:, 16:32, 256:512] = gw[:, 1]
    gaug[:, 32, 0:256] = gb[:, 0]
    gaug[:, 32, 256:512] = gb[:, 1]
    lvec = np.zeros((L, P, NVEC), f)
    cw, cb, ba, bx, lam = A(lru_conv_w), A(lru_conv_b), A(lru_ba), A(lru_bx), A(lru_lambda)
    for ct in range(2):
        sl = slice(ct * P, (ct + 1) * P)
        for k in range(4):
            lvec[:, :, ct * 11 + k] = cw[:, k, sl]
        lvec[:, :, ct * 11 + 4] = cb[:, sl]
        for dr in range(2):
            lvec[:, :, ct * 11 + 5 + dr * 3 + 0] = ba[:, dr, sl]
            lvec[:, :, ct * 11 + 5 + dr * 3 + 1] = bx[:, dr, sl]
            lvec[:, :, ct * 11 + 5 + dr * 3 + 2] = lam[:, dr, sl]
    tri, glam, perm = _consts()
    s_ = np.arange(P)[:, None]
    c_ = np.arange(P)[None, :]
    one = np.ones((P, P), f)
    zero = np.zeros((P, P), f)
    attm_sample = np.stack([s_ >= c_, s_ >= c_, s_ <= c_, s_ <= c_]).astype(f)
    attm_prompt = np.stack([zero, one, one, zero]).astype(f)
    shared = dict(
        ident=np.eye(P, dtype=f), tri=tri, glam=glam, perm=perm,
        w_mod=A(w_mod), b_mod=A(b_mod).reshape(L, 1, 6 * D), norm1=A(norm1).reshape(L, 1, D),
        norm2=A(norm2).reshape(L, 1, D), final_norm=A(final_norm).reshape(1, D), w_in_r=w_in_r,
        w_out=A(w_out), w_mlp1=A(w_mlp1), w_mlp2=A(w_mlp2), gaug=gaug,
        gla_norm=A(gla_norm).reshape(L, 1, 256), attn_sink=A(attn_sink).reshape(L, 1, 8), lruvec=lvec,
        lru_wa=A(lru_wa), lru_wx=A(lru_wx))
    rope_on = _rope_tables(True)
    rope_off = _rope_tables(False)
    in_maps = []
    for core in range(8):
        m = dict(shared)
        if core < 4:
            b = core
            m['x'] = x_sample[b]
            m['cond8'] = np.ascontiguousarray(c[b].reshape(8, P).T)
            m['ck'] = np.ascontiguousarray(cache_k[b].reshape(L, 512, 128))
            m['cv'] = np.ascontiguousarray(cache_v[b].reshape(L, 512, 128))
            sg = state_gla[b].reshape(L, 2, 2, 2, 64, 64)
            m['sgla'] = np.ascontiguousarray(sg.transpose(0, 1, 3, 4, 2, 5).reshape(L, 2, P, 2, 64))
            sl_ = state_lru[b].reshape(L, 2, 2, P)
            m['lruh0'] = np.ascontiguousarray(sl_.transpose(3, 0, 1, 2).reshape(P, L * 4))
            fl = np.zeros((P, 4), f)
            fl[:, 2] = 1.0
            m['flags'] = fl
            m['attm'] = attm_sample
            m['rope'] = rope_on
        else:
            pc = (core - 4) % 2
            m['x'] = np.ascontiguousarray(x_prompt[pc * NSEQ:(pc + 1) * NSEQ].reshape(T, D))
            m['cond8'] = np.ascontiguousarray(c_ctx.reshape(8, P).T)
            m['ck'] = np.zeros((L, 512, 128), f)
            m['cv'] = np.zeros((L, 512, 128), f)
            m['sgla'] = np.zeros((L, 2, P, 2, 64), f)
            m['lruh0'] = np.zeros((P, L * 4), f)
            fl = np.zeros((P, 4), f)
            fl[:, 0] = -30000.0
            fl[:, 1] = 1.0
            fl[:, 3] = -1.0
            m['flags'] = fl
            m['attm'] = attm_prompt
            m['rope'] = rope_off
        in_maps.append(m)
    return in_maps


def kernel(**inputs):
    f = np.float32
    in_maps = prepare(**inputs)
    if 'nc' not in _NC_CACHE:
        _NC_CACHE['nc'] = build_program()
    nc = _NC_CACHE['nc']
    res = run_bass_kernel_spmd(nc, in_maps, core_ids=list(range(8)))
    R = res.results
    y_sample = np.stack([R[b]['y'] for b in range(4)]).astype(f)
    y_prompt = np.concatenate([R[4]['y'], R[5]['y']], 0).reshape(32, 256, D).astype(f)

    def kvfix(name):
        parts = []
        for core in (4, 5):
            a = R[core][name].reshape(L, NSEQ, 256, 2, 64).transpose(1, 0, 2, 3, 4)
            parts.append(a)
        return np.ascontiguousarray(np.concatenate(parts, 0)).astype(f)

    new_k = kvfix('kout')
    new_v = kvfix('vout')
    gst = np.concatenate([R[4]['gst'], R[5]['gst']], 0).astype(f)
    lst = np.concatenate([R[4]['lst'], R[5]['lst']], 0).astype(f)
    return (y_prompt, y_sample, new_k, new_v, gst, lst)
```
